# Optimizing a Trainium2 kernel written in Bass

```python
import jax, jax.numpy as jnp
from jax import lax
import numpy as np

D_MODEL = 2048
BATCH = 4
SEQ = 8192
DEPTH = 1

GRID_W = 64
ROPE_THETA = 10000.0
Q_BLOCK = 128
HEAD_DIM = 128
GQA_Q_HEADS = 8
GQA_KV_HEADS = 2
MLA_HEADS = 8
MLA_Q_RANK = 512
MLA_KV_RANK = 512
MLA_NOPE_DIM = 128
MLA_ROPE_DIM = 64
MLA_V_DIM = 128
N_BRANCHES = 2
D_FF = ((8 * D_MODEL + 3 * 256 - 1) // (3 * 256)) * 256
DEEPNORM_ALPHA = (2.0 * DEPTH) ** 0.25
DEEPNORM_BETA = (8.0 * DEPTH) ** -0.25
LN_EPS = 1e-5
RMS_EPS = 1e-6

GQA_Q_COLS = GQA_Q_HEADS * HEAD_DIM
GQA_KV_COLS = GQA_KV_HEADS * HEAD_DIM
GATE_COLS = N_BRANCHES * D_MODEL
IN_SIZES = [GQA_Q_COLS, GQA_KV_COLS, GQA_KV_COLS, MLA_Q_RANK, MLA_KV_RANK, MLA_ROPE_DIM, GATE_COLS]
IN_SPLITS = list(np.cumsum(IN_SIZES)[:-1].tolist())
IN_COLS = sum(IN_SIZES)

kernel_name = "hybrid_gqa_mla_gated_deepnorm_encoder"


def layer_norm(x):
    xf = x.astype(jnp.float32)
    mu = jnp.mean(xf, axis=-1, keepdims=True)
    var = jnp.mean(jnp.square(xf - mu), axis=-1, keepdims=True)
    return ((xf - mu) * lax.rsqrt(var + LN_EPS)).astype(x.dtype)


def layer_norm_affine(x, g, b):
    xf = x.astype(jnp.float32)
    mu = jnp.mean(xf, axis=-1, keepdims=True)
    var = jnp.mean(jnp.square(xf - mu), axis=-1, keepdims=True)
    y = (xf - mu) * lax.rsqrt(var + LN_EPS) * g.astype(jnp.float32) + b.astype(jnp.float32)
    return y.astype(x.dtype)


def rms_norm(x, g):
    xf = x.astype(jnp.float32)
    y = xf * lax.rsqrt(jnp.mean(jnp.square(xf), axis=-1, keepdims=True) + RMS_EPS) * g.astype(jnp.float32)
    return y.astype(x.dtype)


def modulate(h, shift, scale):
    return h * (1.0 + scale) + shift


def axial_rope_tables(seq, dim):
    rows = seq // GRID_W
    quarter = dim // 4
    inv_freq = ROPE_THETA ** (-jnp.arange(quarter, dtype=jnp.float32) / quarter)
    row_ang = jnp.arange(rows, dtype=jnp.float32)[:, None] * inv_freq
    col_ang = jnp.arange(GRID_W, dtype=jnp.float32)[:, None] * inv_freq
    ang = jnp.concatenate([
        jnp.broadcast_to(row_ang[:, None, :], (rows, GRID_W, quarter)),
        jnp.broadcast_to(col_ang[None, :, :], (rows, GRID_W, quarter)),
    ], axis=-1).reshape(seq, 2 * quarter)
    return jnp.cos(ang), jnp.sin(ang)


def apply_rope(x, cos, sin):
    xf = x.astype(jnp.float32).reshape(*x.shape[:-1], x.shape[-1] // 2, 2)
    x0, x1 = xf[..., 0], xf[..., 1]
    c = cos[None, :, None, :]
    s = sin[None, :, None, :]
    out = jnp.stack([x0 * c - x1 * s, x0 * s + x1 * c], axis=-1).reshape(x.shape)
    return out.astype(x.dtype)


def gqa_attention(q, k, v):
    b, s, hq, d = q.shape
    hkv = k.shape[2]
    g = hq // hkv
    nb = s // Q_BLOCK
    qb = q.reshape(b, nb, Q_BLOCK, hkv, g, d).transpose(1, 0, 2, 3, 4, 5)
    scale = d ** -0.5

    def block(q_blk):
        sc = jnp.einsum('bqkgd,bskd->bkgqs', q_blk, k, preferred_element_type=jnp.float32) * scale
        p = jax.nn.softmax(sc, axis=-1).astype(v.dtype)
        return jnp.einsum('bkgqs,bskd->bqkgd', p, v)

    out = lax.map(block, qb)
    return out.transpose(1, 0, 2, 3, 4, 5).reshape(b, s, hq * d)


def mla_attention(q_nope, q_rope, k_nope, k_rope, v):
    b, s, h, dn = q_nope.shape
    dr = q_rope.shape[-1]
    dv = v.shape[-1]
    nb = s // Q_BLOCK
    qn = q_nope.reshape(b, nb, Q_BLOCK, h, dn).transpose(1, 0, 2, 3, 4)
    qr = q_rope.reshape(b, nb, Q_BLOCK, h, dr).transpose(1, 0, 2, 3, 4)
    scale = (dn + dr) ** -0.5

    def block(args):
        qn_blk, qr_blk = args
        sc = (jnp.einsum('bqhd,bshd->bhqs', qn_blk, k_nope, preferred_element_type=jnp.float32)
              + jnp.einsum('bqhr,bsr->bhqs', qr_blk, k_rope, preferred_element_type=jnp.float32)) * scale
        p = jax.nn.softmax(sc, axis=-1).astype(v.dtype)
        return jnp.einsum('bhqs,bshd->bqhd', p, v)

    out = lax.map(block, (qn, qr))
    return out.transpose(1, 0, 2, 3, 4).reshape(b, s, h * dv)


def _normal(k, shape, fan_in, gain=1.0):
    return jax.random.normal(k, shape, jnp.float32) * (gain * fan_in ** -0.5)


def _gain(k, shape):
    return 1.0 + 0.02 * jax.random.normal(k, shape, jnp.float32)


def setup_inputs(seed: int = 0) -> dict:
    key = jax.random.key(seed)
    ks = jax.random.split(key, 24)
    L, D = DEPTH, D_MODEL
    x = jax.random.normal(ks[0], (BATCH, SEQ, D), jnp.float32)
    c = jax.random.normal(ks[1], (BATCH, D), jnp.float32)
    w_ada = _normal(ks[2], (L, D, 6 * D), D, 0.5)
    b_ada = 0.02 * jax.random.normal(ks[3], (L, 6 * D), jnp.float32)
    col_scale = jnp.concatenate([
        jnp.ones((GQA_Q_COLS + GQA_KV_COLS,), jnp.float32),
        jnp.full((GQA_KV_COLS,), DEEPNORM_BETA, jnp.float32),
        jnp.ones((MLA_Q_RANK + MLA_KV_RANK + MLA_ROPE_DIM + GATE_COLS,), jnp.float32),
    ])
    w_in = _normal(ks[4], (L, D, IN_COLS), D) * col_scale
    b_gates = 0.01 * jax.random.normal(ks[5], (L, GATE_COLS), jnp.float32)
    gqa_q_gain = _gain(ks[6], (L, HEAD_DIM))
    gqa_k_gain = _gain(ks[7], (L, HEAD_DIM))
    mla_q_gain = _gain(ks[8], (L, MLA_Q_RANK))
    mla_kv_gain = _gain(ks[9], (L, MLA_KV_RANK))
    w_mla_uq = _normal(ks[10], (L, MLA_Q_RANK, MLA_HEADS * (MLA_NOPE_DIM + MLA_ROPE_DIM)), MLA_Q_RANK)
    w_uk = _normal(ks[11], (L, MLA_KV_RANK, MLA_HEADS, MLA_NOPE_DIM), MLA_KV_RANK)
    w_uv = _normal(ks[12], (L, MLA_KV_RANK, MLA_HEADS, MLA_V_DIM), MLA_KV_RANK, DEEPNORM_BETA)
    w_mla_ukv = jnp.concatenate([w_uk, w_uv], axis=-1).reshape(L, MLA_KV_RANK, MLA_HEADS * (MLA_NOPE_DIM + MLA_V_DIM))
    w_branch_gqa = _normal(ks[13], (L, GQA_Q_COLS, D), GQA_Q_COLS, DEEPNORM_BETA)
    w_branch_mla = _normal(ks[14], (L, MLA_HEADS * MLA_V_DIM, D), MLA_HEADS * MLA_V_DIM, DEEPNORM_BETA)
    w_out = _normal(ks[15], (L, D, D), D, DEEPNORM_BETA)
    ln1_g = _gain(ks[16], (L, D))
    ln1_b = 0.02 * jax.random.normal(ks[17], (L, D), jnp.float32)
    w_ffn_gate = _normal(ks[18], (L, D, D_FF), D, DEEPNORM_BETA)
    w_ffn_up = _normal(ks[19], (L, D, D_FF), D, DEEPNORM_BETA)
    w_ffn_down = _normal(ks[20], (L, D_FF, D), D_FF, DEEPNORM_BETA)
    ln2_g = _gain(ks[21], (L, D))
    ln2_b = 0.02 * jax.random.normal(ks[22], (L, D), jnp.float32)
    return {
        "x": x, "c": c, "w_ada": w_ada, "b_ada": b_ada, "w_in": w_in, "b_gates": b_gates,
        "gqa_q_gain": gqa_q_gain, "gqa_k_gain": gqa_k_gain, "mla_q_gain": mla_q_gain,
        "mla_kv_gain": mla_kv_gain, "w_mla_uq": w_mla_uq, "w_mla_ukv": w_mla_ukv,
        "w_branch_gqa": w_branch_gqa, "w_branch_mla": w_branch_mla, "w_out": w_out,
        "ln1_g": ln1_g, "ln1_b": ln1_b, "w_ffn_gate": w_ffn_gate, "w_ffn_up": w_ffn_up,
        "w_ffn_down": w_ffn_down, "ln2_g": ln2_g, "ln2_b": ln2_b,
    }


def reference(x, c, w_ada, b_ada, w_in, b_gates, gqa_q_gain, gqa_k_gain, mla_q_gain, mla_kv_gain,
              w_mla_uq, w_mla_ukv, w_branch_gqa, w_branch_mla, w_out, ln1_g, ln1_b,
              w_ffn_gate, w_ffn_up, w_ffn_down, ln2_g, ln2_b):
    b, s, _ = x.shape
    cos_g, sin_g = axial_rope_tables(s, HEAD_DIM)
    cos_m, sin_m = axial_rope_tables(s, MLA_ROPE_DIM)
    c_act = jax.nn.silu(c)
    for l in range(DEPTH):
        mod = (c_act @ w_ada[l] + b_ada[l])[:, None, :]
        shift1, scale1, gate1, shift2, scale2, gate2 = jnp.split(mod, 6, axis=-1)

        h = modulate(layer_norm(x), shift1, scale1)
        proj = h @ w_in[l]
        q_g, k_g, v_g, q_lat, kv_lat, k_r, gate_logits = jnp.split(proj, IN_SPLITS, axis=-1)

        q_g = apply_rope(rms_norm(q_g.reshape(b, s, GQA_Q_HEADS, HEAD_DIM), gqa_q_gain[l]), cos_g, sin_g)
        k_g = apply_rope(rms_norm(k_g.reshape(b, s, GQA_KV_HEADS, HEAD_DIM), gqa_k_gain[l]), cos_g, sin_g)
        v_g = v_g.reshape(b, s, GQA_KV_HEADS, HEAD_DIM)
        y_gqa = gqa_attention(q_g, k_g, v_g)

        q_m = (rms_norm(q_lat, mla_q_gain[l]) @ w_mla_uq[l]).reshape(b, s, MLA_HEADS, MLA_NOPE_DIM + MLA_ROPE_DIM)
        q_nope = q_m[..., :MLA_NOPE_DIM]
        q_rope = apply_rope(q_m[..., MLA_NOPE_DIM:], cos_m, sin_m)
        kv = (rms_norm(kv_lat, mla_kv_gain[l]) @ w_mla_ukv[l]).reshape(b, s, MLA_HEADS, MLA_NOPE_DIM + MLA_V_DIM)
        k_nope = kv[..., :MLA_NOPE_DIM]
        v_m = kv[..., MLA_NOPE_DIM:]
        k_rope = apply_rope(k_r[:, :, None, :], cos_m, sin_m)[:, :, 0, :]
        y_mla = mla_attention(q_nope, q_rope, k_nope, k_rope, v_m)

        g_gqa, g_mla = jnp.split(jax.nn.sigmoid(gate_logits + b_gates[l]), N_BRANCHES, axis=-1)
        merged = g_gqa * (y_gqa @ w_branch_gqa[l]) + g_mla * (y_mla @ w_branch_mla[l])
        x = layer_norm_affine(DEEPNORM_ALPHA * x + gate1 * (merged @ w_out[l]), ln1_g[l], ln1_b[l])

        h = modulate(layer_norm(x), shift2, scale2)
        f = (jax.nn.silu(h @ w_ffn_gate[l]) * (h @ w_ffn_up[l])) @ w_ffn_down[l]
        x = layer_norm_affine(DEEPNORM_ALPHA * x + gate2 * f, ln2_g[l], ln2_b[l])
    return x
```

```python
from contextlib import ExitStack
import numpy as np
import concourse.bass as bass
import concourse.mybir as mybir
from concourse.bass_utils import run_bass_kernel_spmd

F32 = mybir.dt.float32
BF16 = mybir.dt.bfloat16
AF = mybir.ActivationFunctionType
ALU = mybir.AluOpType

D = 2048
S = 8192
SQ = 4096
DFF = 5632
LN_EPS = 1e-5
RMS_EPS = 1e-6
ALPHA = 2.0 ** 0.25
TQ = 512
NQT = SQ // TQ
NKT = S // TQ
KP = 1024
NPIECE = S // KP
WSLOT = 4096


class Buf:
    __slots__ = ("name", "w", "r", "guard")

    def __init__(self, name):
        self.name = name
        self.w = None
        self.r = {}
        self.guard = ()


class Eng:
    def __init__(self, name, e, sem, is_pe=False):
        self.name, self.e, self.sem, self.is_pe = name, e, sem, is_pe
        self.n = 0
        self.seen = {}


class Slot:
    def __init__(self, sem):
        self.sem = sem
        self.count = 0


class K:
    def __init__(self, nc, es):
        self.nc = nc
        self.es = es
        self.nsem = 0
        self.pe = Eng("pe", nc.tensor, self.newsem("pe"), is_pe=True)
        self.act = Eng("act", nc.scalar, self.newsem("act"))
        self.dve = Eng("dve", nc.vector, self.newsem("dve"))
        self.pool = Eng("pool", nc.gpsimd, self.newsem("pool"))
        self.sp = Eng("sp", nc.sync, self.newsem("sp"))
        self.slots = []
        self.ps_i = 0

    def newsem(self, name):
        self.nsem += 1
        return self.es.enter_context(self.nc.semaphore(f"s{self.nsem}_{name}"))

    def newslot(self, name):
        s = Slot(self.newsem(name))
        self.slots.append(s)
        return s

    def _gather(self, eng, reads, writes, deps):
        need = {}

        def add(tok):
            if tok is None:
                return
            s, v = tok
            if s is eng.sem:
                if eng.is_pe or v <= eng.n - 3:
                    return
            if eng.seen.get(s, 0) >= v:
                return
            if need.get(s, 0) < v:
                need[s] = v

        for b in reads:
            add(b.w)
            for t in b.guard:
                add(t)
        for b in writes:
            add(b.w)
            for t in b.r.values():
                add(t)
            for t in b.guard:
                add(t)
        for t in deps:
            add(t)
        return list(need.items())

    def _apply_waits(self, eng, items, fn):
        for s, v in items[:-1]:
            eng.e.wait_ge(s, v)
            eng.seen[s] = v
        inst = fn()
        if items:
            s, v = items[-1]
            inst._wait_ge(s, v)
            eng.seen[s] = v
        return inst

    def emit(self, eng, fn, reads=(), writes=(), deps=()):
        items = self._gather(eng, reads, writes, deps)
        inst = self._apply_waits(eng, items, fn)
        eng.n += 1
        inst.then_inc(eng.sem, 1)
        tok = (eng.sem, eng.n)
        for b in reads:
            b.r[eng.sem] = tok
        for b in writes:
            b.w = tok
            b.r = {}
        return tok

    def dma(self, q, out, in_, slot, reads=(), writes=(), deps=()):
        items = self._gather(q, reads, writes, deps)
        inst = self._apply_waits(q, items, lambda: q.e.dma_start(out=out, in_=in_))
        slot.count += 16
        inst.then_inc(slot.sem, 16)
        tok = (slot.sem, slot.count)
        for b in reads:
            b.r[slot.sem] = tok
        for b in writes:
            b.w = tok
            b.r = {}
        return tok

    def fence(self):
        return tuple((e.sem, e.n) for e in (self.pe, self.act, self.dve, self.pool) if e.n > 0)


def build_program(nqt=NQT, nkt=NKT, stop_after=None, dumps=()):
    nc = bass.Bass("TRN2", target_bir_lowering=False)
    dumps = set(dumps)
    dump_specs = {}

    def din(name, shape, dt=F32):
        return nc.dram_tensor(name, list(shape), dt, kind="ExternalInput").ap()

    def dscr(name, shape, dt=BF16):
        kind = "ExternalOutput" if name in dumps else "Internal"
        if name in dumps:
            dump_specs[name] = (tuple(shape), dt)
        return nc.dram_tensor(name, list(shape), dt, kind=kind).ap()

    xa = din("xa", [S, D])
    xq = xa
    cT_d = din("cT", [128, 16])
    w_ada = din("w_ada", [D, 6 * D])
    badaT_d = din("badaT", [128, 64])
    bgbc_d = din("bgbc", [128, 2 * D])
    w_in = din("w_in", [D, 6720])
    bgT_d = din("bgatesT", [128, 32])
    gq_d = din("gq", [128, 1])
    gk_d = din("gk", [128, 1])
    mqg_d = din("mqg", [128, 4])
    mkg_d = din("mkg", [128, 4])
    w_uq = din("w_uq", [512, 1536])
    w_ukv = din("w_ukv", [512, 2048])
    w_bg = din("w_bg", [1024, D])
    w_bm = din("w_bm", [1024, D])
    w_out = din("w_out", [D, D])
    lnc_d = din("lnc", [4, 128, D])
    w_fg = din("w_fg", [D, DFF])
    w_fu = din("w_fu", [D, DFF])
    w_fd = din("w_fd", [DFF, D])
    cosg_d = din("cosg", [128, S])
    sing_d = din("sing", [128, S])
    cosm_d = din("cosm", [64, S])
    sinm_d = din("sinm", [64, S])
    cosgq_d, singq_d, cosmq_d, sinmq_d = cosg_d, sing_d, cosm_d, sinm_d
    cst_d = din("cst", [128, 3, 128])
    pm_d = din("pmat", [64, 64])

    y = nc.dram_tensor("y", [SQ, D], F32, kind="ExternalOutput").ap()

    kg_s = dscr("kg_s", [2, 128, S])
    vg_s = dscr("vg_s", [2, 128, S // 128, 128])
    kn_s = dscr("kn_s", [8, 128, S])
    kr_s = dscr("kr_s", [64, S])
    vm_s = dscr("vm_s", [8, 128, S // 128, 128])
    wout_s = dscr("wout_s", [D, D])
    wdn_s = dscr("wdn_s", [DFF, D])
    NJOB = 84
    wscr = dscr("wscr", [NJOB, 128, WSLOT])

    es = ExitStack()
    with es:
        k = K(nc, es)
        pe, act, dve, pool, sp = k.pe, k.act, k.dve, k.pool, k.sp
        T, V, A = nc.tensor, nc.vector, nc.scalar

        def sb(name, shape, dt, stack=es):
            return stack.enter_context(nc.sbuf_tensor("sb_" + name, list(shape), dt))

        psA = [es.enter_context(nc.psum_tensor(f"psA{i}", [128, 512], F32)) for i in range(6)]
        psT = [es.enter_context(nc.psum_tensor(f"psT{i}", [128, 1024], BF16)) for i in range(2)]
        psA_b = [Buf(f"psA{i}") for i in range(6)]
        psT_b = [Buf(f"psT{i}") for i in range(2)]

        def next_ps():
            i = k.ps_i % 5
            k.ps_i += 1
            return i

        cst = sb("cst", [128, 3, 128], BF16)
        pmat = sb("pmat", [64, 64], BF16)
        ident, ones, pg = cst[:, 0, :], cst[:, 1, :], cst[:, 2, :]
        cst_b, pm_b = Buf("cst"), Buf("pm")
        cT = sb("cT", [128, 16], F32)
        cact = sb("cact", [128, 16], BF16)
        cact_b = Buf("cact")
        modT = sb("modT", [128, 64], F32)
        modT_b = Buf("modT")
        badaT = sb("badaT", [128, 64], F32)
        bgT = sb("bgT", [128, 32], F32)
        gq = sb("gq", [128, 1], F32)
        gk = sb("gk", [128, 1], F32)
        mqg = sb("mqg", [128, 4], F32)
        mkg = sb("mkg", [128, 4], F32)
        small_b = Buf("small")
        epsc = sb("epsc", [128, 3], F32)
        epsc_b = Buf("epsc")
        xbuf = sb("xbuf", [128, 4, D], F32)
        xbuf_b = [Buf(f"xbuf{s}") for s in range(4)]
        xn = sb("xn", [128, 4, D], BF16)
        xn_b = [Buf(f"xn{s}") for s in range(4)]
        hT = sb("hT", [128, 16, TQ], BF16)
        hT_b = [Buf(f"hT{c}") for c in range(16)]
        NWS = 2
        wr = [sb(f"wr{i}", [128, WSLOT], BF16) for i in range(NWS)]
        wr_b = [Buf(f"wr{i}") for i in range(NWS)]
        wr_slot = [k.newslot(f"wr{i}") for i in range(NWS)]
        wr_i = [0]
        stats = sb("stats", [128, 4, 4, 6], F32)
        mv = sb("mv", [128, 4, 2], F32)
        rstd = sb("rstd", [128, 4], F32)
        nmr = sb("nmr", [128, 4], F32)
        st_b = [Buf(f"st{s}") for s in range(4)]
        tabg = sb("tabg", [128, 2, TQ], F32)
        tabm = sb("tabm", [64, 2, TQ], F32)
        tabg_b, tabm_b = Buf("tabg"), Buf("tabm")
        tab_slot = k.newslot("tab")
        class TSet:
            pass

        tsets = []
        for i in range(2):
            o = TSet()
            o.xg = sb(f"tmpxg{i}", [128, TQ], BF16)
            o.sq = sb(f"tmpsq{i}", [128, TQ], BF16)
            o.rs = sb(f"tmprs{i}", [128, TQ], F32)
            o.t1 = sb(f"tmpa{i}", [128, TQ], F32)
            o.t2 = sb(f"tmpb{i}", [128, TQ], F32)
            o.xg_b, o.sq_b, o.rs_b, o.t1_b, o.t2_b = (Buf(f"{n}{i}") for n in ("xg", "sq", "rs", "t1", "t2"))
            tsets.append(o)
        ts_i = [0]

        def next_tset():
            o = tsets[ts_i[0] % 2]
            ts_i[0] += 1
            return o

        rq_t = sb("rq_t", [128, TQ], F32)
        rq_b = Buf("rq")
        ps2 = sb("ps2", [128, 2, TQ], BF16)
        ps2_b = [Buf("ps2_0"), Buf("ps2_1")]
        cslot = k.newslot("const")
        xslot = [k.newslot(f"x{s}") for s in range(4)]
        yslot = [k.newslot(f"y{s}") for s in range(4)]

        for ci, cv in enumerate((LN_EPS, 128.0 * RMS_EPS, 512.0 * RMS_EPS)):
            k.emit(dve, lambda: V.memset(epsc[:, ci:ci + 1], float(cv)), writes=[epsc_b])

        def rsqrt_small(out_ap, in_ap, eps_col, rbufs, wbuf):
            k.emit(act, lambda: A.activation(out=out_ap, in_=in_ap, func=AF.Sqrt, bias=epsc[:, eps_col:eps_col + 1]),
                   reads=list(rbufs) + [epsc_b], writes=[wbuf])
            k.emit(dve, lambda: V.reciprocal(out=out_ap, in_=out_ap), reads=[wbuf], writes=[wbuf])

        def wload(view_src_pairs):
            i = wr_i[0] % NWS
            wr_i[0] += 1
            for view_fn, src in view_src_pairs:
                k.dma(pool, view_fn(wr[i]), src, wr_slot[i], writes=[wr_b[i]])
            return i

        def wload_bf(view_src_pairs, deps=()):
            i = wr_i[0] % NWS
            wr_i[0] += 1
            for view_fn, src in view_src_pairs:
                k.dma(sp, view_fn(wr[i]), src, wr_slot[i], writes=[wr_b[i]], deps=deps)
            return i

        def stream(jobs, lookahead=NWS):
            n = len(jobs)
            slots = [None] * n
            for i in range(min(lookahead, n)):
                slots[i] = jobs[i][0]()
            for i in range(n):
                jobs[i][1](slots[i])
                if i + lookahead < n:
                    slots[i + lookahead] = jobs[i + lookahead][0]()

        def kview(t, kc, n):
            return t[:, 0:kc * n].rearrange("p (k n) -> p k n", k=kc)

        def wsrc(w, c0, n):
            return w.rearrange("(k p) n -> p k n", p=128)[:, :, c0:c0 + n]

        dbg_slot = k.newslot("dbg")

        def dbg_dump(name, ap, shape, dt, bufs):
            dmp = nc.dram_tensor(name, list(shape), dt, kind="ExternalOutput").ap()
            dump_specs[name] = (tuple(shape), dt)
            k.dma(sp, dmp, ap, dbg_slot, reads=bufs)

        def mm(out, lhsT, rhs, start, stop, reads, writes):
            return k.emit(pe, lambda: T.matmul(out, lhsT, rhs, start=start, stop=stop),
                          reads=reads, writes=writes)

        k.dma(pool, cst[:], cst_d, cslot, writes=[cst_b])
        k.dma(pool, pmat[:], pm_d, cslot, writes=[pm_b])
        for t_sb, t_d in ((cT, cT_d), (badaT, badaT_d), (bgT, bgT_d), (gq, gq_d), (gk, gk_d),
                          (mqg, mqg_d), (mkg, mkg_d)):
            k.dma(sp, t_sb[:], t_d, cslot, writes=[small_b])
        k.emit(act, lambda: A.activation(out=cact[:], in_=cT[:], func=AF.Silu), reads=[small_b], writes=[cact_b])
        k.emit(dve, lambda: V.tensor_scalar(out=mqg[:], in0=mqg[:], scalar1=float(512.0 ** 0.5), scalar2=None,
                                            op0=ALU.mult), reads=[small_b], writes=[small_b])
        k.emit(dve, lambda: V.tensor_scalar(out=mkg[:], in0=mkg[:], scalar1=float(512.0 ** 0.5), scalar2=None,
                                            op0=ALU.mult), reads=[small_b], writes=[small_b])

        p0 = ExitStack()
        with p0:
            crep = sb("crep", [128, 16, 128], BF16, p0)
            crep_b = Buf("crep")
            gbc = sb("gbc", [128, 2 * D], F32, p0)
            gbc_b = Buf("gbc")
            bgbc = sb("bgbc", [128, 2 * D], F32, p0)
            bgbc_b = Buf("bgbc")
            fst = [sb(f"fst{i}", [128, D], F32, p0) for i in range(2)]
            fbf = [sb(f"fbf{i}", [128, D], BF16, p0) for i in range(2)]
            fst_b = [Buf(f"fst{i}") for i in range(2)]
            fbf_b = [Buf(f"fbf{i}") for i in range(2)]
            fslot = [k.newslot(f"fst{i}") for i in range(2)]
            fsslot = [k.newslot(f"fbf{i}") for i in range(2)]

            k.dma(sp, bgbc[:], bgbc_d, cslot, writes=[bgbc_b])
            for kc in range(16):
                k.emit(dve, lambda: V.tensor_copy(out=crep[:, kc, :], in_=cact[:, kc:kc + 1].to_broadcast([128, 128])),
                       reads=[cact_b], writes=[crep_b])

            col_starts = [0, D, 3 * D, 4 * D]
            mod_ps = 5
            jobs = []
            for blk in range(32):
                seg, off = divmod(blk * 256, D)
                c0 = col_starts[seg] + off

                def ld(c0=c0):
                    return wload([(lambda t: kview(t, 16, 256), wsrc(w_ada, c0, 256))])

                def cp(i, blk=blk):
                    wv = kview(wr[i], 16, 256)
                    for cc in range(2):
                        j = blk * 2 + cc
                        for kc in range(16):
                            mm(psA[mod_ps][:, j:j + 1], wv[:, kc, cc * 128:(cc + 1) * 128], cact[:, kc:kc + 1],
                               kc == 0, kc == 15, [wr_b[i], cact_b], [psA_b[mod_ps]])
                jobs.append((ld, cp))
            stream(jobs)
            k.emit(dve, lambda: V.tensor_tensor(out=modT[:], in0=psA[mod_ps][:, 0:64], in1=badaT[:], op=ALU.add),
                   reads=[psA_b[mod_ps], small_b], writes=[modT_b])
            for lo in (16, 48):
                k.emit(dve, lambda: V.tensor_scalar(out=modT[:, lo:lo + 16], in0=modT[:, lo:lo + 16], scalar1=1.0,
                                                    scalar2=None, op0=ALU.add), reads=[modT_b], writes=[modT_b])
            jobs = []
            for blk in range(16):
                seg, off = divmod(blk * 256, D)
                c0 = (2 * D if seg == 0 else 5 * D) + off

                def ld(c0=c0):
                    return wload([(lambda t: kview(t, 16, 256), wsrc(w_ada, c0, 256))])

                def cp(i, blk=blk):
                    wv = kview(wr[i], 16, 256)
                    b = next_ps()
                    for kc in range(16):
                        mm(psA[b][:, 0:256], crep[:, kc, :], wv[:, kc, :], kc == 0, kc == 15,
                           [wr_b[i], crep_b], [psA_b[b]])
                    k.emit(dve, lambda: V.tensor_tensor(out=gbc[:, blk * 256:(blk + 1) * 256], in0=psA[b][:, 0:256],
                                                        in1=bgbc[:, blk * 256:(blk + 1) * 256], op=ALU.add),
                           reads=[psA_b[b], bgbc_b], writes=[gbc_b])
                jobs.append((ld, cp))
            stream(jobs)
            fold_toks = []
            nfold = 0
            for (wsrc_d, wdst, nrb, goff) in ((w_out, wout_s, 16, 0), (w_fd, wdn_s, 44, D)):
                for rb in range(nrb):
                    i = nfold % 2
                    nfold += 1
                    k.dma(sp, fst[i][:], wsrc_d[rb * 128:(rb + 1) * 128, :], fslot[i], writes=[fst_b[i]])
                    k.emit(dve, lambda: V.tensor_tensor(out=fbf[i][:], in0=fst[i][:], in1=gbc[:, goff:goff + D],
                                                        op=ALU.mult),
                           reads=[fst_b[i], gbc_b], writes=[fbf_b[i]])
                    fold_toks.append(k.dma(sp, wdst[rb * 128:(rb + 1) * 128, :], fbf[i][:], fsslot[i],
                                           reads=[fbf_b[i]]))
            fold_deps = tuple({t[0]: t for t in fold_toks}.values())
            p0_fence = k.fence()
        p0_guard = p0_fence + fold_deps

        if stop_after == "p0":
            dmp = nc.dram_tensor("dbg_modT", [128, 64], F32, kind="ExternalOutput").ap()
            dump_specs["dbg_modT"] = ((128, 64), F32)
            tk = k.dma(sp, dmp, modT[:], cslot, reads=[modT_b], deps=p0_guard)
            sp.e.wait_ge(cslot.sem, cslot.count)
            return nc, dump_specs

        def layer_norm_to_xn(s):
            xs = xbuf[:, s, :]
            for c4 in range(4):
                k.emit(dve, lambda: V.bn_stats(out=stats[:, s, c4, :], in_=xs[:, c4 * 512:(c4 + 1) * 512]),
                       reads=[xbuf_b[s]], writes=[st_b[s]])
            k.emit(dve, lambda: V.bn_aggr(out=mv[:, s, :], in_=stats[:, s].rearrange("p c f -> p (c f)")),
                   reads=[st_b[s]], writes=[st_b[s]])
            rsqrt_small(rstd[:, s:s + 1], mv[:, s, 1:2], 0, [st_b[s]], st_b[s])
            k.emit(dve, lambda: V.scalar_tensor_tensor(out=nmr[:, s:s + 1], in0=mv[:, s, 0:1], scalar=-1.0,
                                                       in1=rstd[:, s:s + 1], op0=ALU.mult, op1=ALU.mult),
                   reads=[st_b[s]], writes=[st_b[s]])
            k.emit(dve, lambda: V.tensor_scalar(out=xn[:, s, :], in0=xs, scalar1=rstd[:, s:s + 1],
                                                scalar2=nmr[:, s:s + 1], op0=ALU.mult, op1=ALU.add),
                   reads=[xbuf_b[s], st_b[s]], writes=[xn_b[s]])

        tr_i = [0]

        def transpose_modulate(shift_col, scale_col):
            for c in range(16):
                n = tr_i[0]
                tr_i[0] += 1
                b = n % 2
                half = (n // 2) % 2
                dst = psT[b][:, half * 512:(half + 1) * 512]
                for s in range(4):
                    k.emit(pe, lambda: T.transpose(out=dst[:, s * 128:(s + 1) * 128],
                                                   in_=xn[:, s, c * 128:(c + 1) * 128], identity=ident),
                           reads=[xn_b[s], cst_b], writes=[psT_b[b]])
                sc = modT[:, scale_col + c:scale_col + c + 1]
                sh = modT[:, shift_col + c:shift_col + c + 1]
                if c % 2 == 0:
                    k.emit(act, lambda: A.activation(out=hT[:, c, :], in_=dst, func=AF.Identity, bias=sh, scale=sc),
                           reads=[psT_b[b], modT_b], writes=[hT_b[c]])
                else:
                    k.emit(dve, lambda: V.tensor_scalar(out=hT[:, c, :], in0=dst, scalar1=sc, scalar2=sh,
                                                        op0=ALU.mult, op1=ALU.add),
                           reads=[psT_b[b], modT_b], writes=[hT_b[c]])

        def proj_T(wview, wbuf, ncol_lo, m, kchunks, rhs_fn, rhs_bufs):
            b = next_ps()
            for kc in range(kchunks):
                mm(psA[b][0:m, :], wview[:, kc, ncol_lo:ncol_lo + m], rhs_fn(kc), kc == 0, kc == kchunks - 1,
                   [wbuf] + rhs_bufs(kc), [psA_b[b]])
            return b

        def rope_rms_finalize(b, gcol, tab, tab_b, col0, out_ap, out_b):
            ps = psA[b]
            o = next_tset()
            k.emit(act, lambda: A.activation(out=o.xg[:], in_=ps[:], func=AF.Identity, scale=gcol),
                   reads=[psA_b[b], small_b], writes=[o.xg_b])
            k.emit(act, lambda: A.activation(out=o.sq[:], in_=ps[:], func=AF.Square),
                   reads=[psA_b[b]], writes=[o.sq_b])
            b1 = next_ps()
            mm(psA[b1][:], ones, o.sq[:], True, True, [cst_b, o.sq_b], [psA_b[b1]])
            b2 = next_ps()
            mm(psA[b2][:], pg, o.xg[:], True, True, [cst_b, o.xg_b], [psA_b[b2]])
            rsqrt_small(o.rs[:], psA[b1][:], 1, [psA_b[b1]], o.rs_b)
            k.emit(dve, lambda: V.tensor_tensor(out=o.t1[:], in0=o.xg[:], in1=tab[:, 0, :], op=ALU.mult),
                   reads=[o.xg_b, tab_b], writes=[o.t1_b])
            k.emit(dve, lambda: V.tensor_tensor(out=o.t2[:], in0=psA[b2][:], in1=tab[:, 1, :], op=ALU.mult),
                   reads=[psA_b[b2], tab_b], writes=[o.t2_b])
            k.emit(dve, lambda: V.tensor_tensor(out=o.t1[:], in0=o.t1[:], in1=o.t2[:], op=ALU.add),
                   reads=[o.t1_b, o.t2_b], writes=[o.t1_b])
            k.emit(dve, lambda: V.scalar_tensor_tensor(out=out_ap, in0=o.t1[:], scalar=float(128.0 ** 0.5),
                                                       in1=o.rs[:], op0=ALU.mult, op1=ALU.mult),
                   reads=[o.t1_b, o.rs_b], writes=[out_b])

        def rope64_finalize(b, scale_bc, scale_b, tab, tab_b, out_ap, out_b):
            ps = psA[b]
            o = next_tset()
            k.emit(act, lambda: A.activation(out=o.xg[0:64, :], in_=ps[0:64, :], func=AF.Identity),
                   reads=[psA_b[b]], writes=[o.xg_b])
            b2 = next_ps()
            mm(psA[b2][0:64, :], pmat[:], o.xg[0:64, :], True, True, [pm_b, o.xg_b], [psA_b[b2]])
            k.emit(dve, lambda: V.tensor_tensor(out=o.t1[0:64, :], in0=o.xg[0:64, :], in1=tab[:, 0, :], op=ALU.mult),
                   reads=[o.xg_b, tab_b], writes=[o.t1_b])
            k.emit(dve, lambda: V.tensor_tensor(out=o.t2[0:64, :], in0=psA[b2][0:64, :], in1=tab[:, 1, :], op=ALU.mult),
                   reads=[psA_b[b2], tab_b], writes=[o.t2_b])
            if scale_bc is None:
                k.emit(dve, lambda: V.tensor_tensor(out=out_ap, in0=o.t1[0:64, :], in1=o.t2[0:64, :], op=ALU.add),
                       reads=[o.t1_b, o.t2_b], writes=[out_b])
            else:
                k.emit(dve, lambda: V.tensor_tensor(out=o.t1[0:64, :], in0=o.t1[0:64, :], in1=o.t2[0:64, :], op=ALU.add),
                       reads=[o.t1_b, o.t2_b], writes=[o.t1_b])
                k.emit(dve, lambda: V.tensor_tensor(out=out_ap, in0=o.t1[0:64, :], in1=scale_bc[0:64, :], op=ALU.mult),
                       reads=[o.t1_b, scale_b], writes=[out_b])

        def latent_chunk(b, j, gains, lat, lat_b):
            o = next_tset()
            k.emit(act, lambda: A.activation(out=lat[:, j, :], in_=psA[b][:], func=AF.Identity,
                                             scale=gains[:, j:j + 1]),
                   reads=[psA_b[b], small_b], writes=[lat_b[j]])
            k.emit(act, lambda: A.activation(out=o.sq[:], in_=psA[b][:], func=AF.Square),
                   reads=[psA_b[b]], writes=[o.sq_b])
            mm(psA[5][:], ones, o.sq[:], j == 0, j == 3, [cst_b, o.sq_b], [psA_b[5]])

        def latent_finish(lat, lat_b):
            rsqrt_small(rq_t[:], psA[5][:], 2, [psA_b[5]], rq_b)
            for j in range(4):
                k.emit(dve, lambda: V.tensor_tensor(out=lat[:, j, :], in0=lat[:, j, :], in1=rq_t[:], op=ALU.mult),
                       reads=[lat_b[j], rq_b], writes=[lat_b[j]])

        def v16(t):
            return kview(t, 16, 256)

        job_specs = []
        for blk in range(4):
            job_specs.append([(v16, wsrc(w_in, blk * 256, 256))])
        for blk in range(2):
            job_specs.append([(v16, wsrc(w_in, 1536 + blk * 256, 256))])
        for blk in range(2):
            job_specs.append([(lambda t: kview(t, 4, 768), wsrc(w_uq, blk * 768, 768))])
        for oc in range(16):
            job_specs.append([(lambda t: v16(t)[:, :, 0:128], wsrc(w_in, 2624 + oc * 128, 128)),
                              (lambda t: v16(t)[:, :, 128:256], wsrc(w_in, 2624 + D + oc * 128, 128))])
            job_specs.append([(lambda t: kview(t, 16, 128)[:, 0:8, :], wsrc(w_bg, oc * 128, 128)),
                              (lambda t: kview(t, 16, 128)[:, 8:16, :], wsrc(w_bm, oc * 128, 128))])
        for fc in range(44):
            job_specs.append([(lambda t: v16(t)[:, :, 0:128], wsrc(w_fg, fc * 128, 128)),
                              (lambda t: v16(t)[:, :, 128:256], wsrc(w_fu, fc * 128, 128))])
        assert len(job_specs) == NJOB
        JOB_B2, JOB_B5, JOB_B8 = 0, 8, 40
        pk_slot = [k.newslot(f"pk{i}") for i in range(NWS)]
        prep_state = {"next": 0, "pending": None, "toks": {}}

        def prep_jobs(n):
            for _ in range(n):
                j = prep_state["next"]
                if j < NJOB:
                    i = wload(job_specs[j])
                    prep_state["next"] = j + 1
                else:
                    i = None
                pend = prep_state["pending"]
                if pend is not None:
                    pj, pi = pend
                    prep_state["toks"][pi] = k.dma(pool, wscr[pj], wr[pi][:], pk_slot[pi], reads=[wr_b[pi]])
                prep_state["pending"] = (j, i) if i is not None else None

        def wload_packed(j, n=WSLOT):
            i = wr_i[0] % NWS
            wr_i[0] += 1
            k.dma(pool, wr[i][:, 0:n], wscr[j, :, 0:n], wr_slot[i], writes=[wr_b[i]],
                  deps=tuple(prep_state["toks"].values()))
            return i

        pA = ExitStack()
        with pA:
            wkv = sb("wkv", [128, 16, 1088], BF16, pA)
            wkv_b = Buf("wkv")
            wuk = sb("wuk", [128, 4, 2048], BF16, pA)
            wuk_b = Buf("wuk")
            kvl = sb("kvl", [128, 4, TQ], BF16, pA)
            kvl_b = [Buf(f"kvl{j}") for j in range(4)]
            kgst = sb("kgst", [128, 2, TQ], BF16, pA)
            vgst = sb("vgst", [128, 4, 256], BF16, pA)
            knst = sb("knst", [128, 8, TQ], BF16, pA)
            krst = sb("krst", [64, TQ], BF16, pA)
            vmst = sb("vmst", [128, 4, 1024], BF16, pA)
            kgst_b, vgst_b, knst_b, krst_b, vmst_b = (Buf(n) for n in ("kgst", "vgst", "knst", "krst", "vmst"))
            for bb in kvl_b + [wkv_b, wuk_b, kgst_b, vgst_b, knst_b, krst_b, vmst_b]:
                bb.guard = p0_guard
            stslot = {n: k.newslot(n) for n in ("kg", "vg", "kn", "kr", "vm")}
            scr_toks = {}

            wv_in = w_in.rearrange("(k p) n -> p k n", p=128)
            k.dma(pool, wkv[:, :, 0:512], wv_in[:, :, 1024:1536], cslot, writes=[wkv_b])
            k.dma(pool, wkv[:, :, 512:1088], wv_in[:, :, 2048:2624], cslot, writes=[wkv_b])
            k.dma(pool, wuk[:], w_ukv.rearrange("(k p) n -> p k n", p=128), cslot, writes=[wuk_b])
            wuk_v = wuk[:].rearrange("p k (h t d) -> p k h t d", h=8, t=2)
            hrhs = (lambda kc: hT[:, kc, :])
            hbufs = (lambda kc: [hT_b[kc]])

            for t in range(nkt):
                r0 = t * TQ
                prep_jobs(6 if nkt == NKT else (NJOB + 1) // nkt + 1)
                for s in range(4):
                    k.dma(sp, xbuf[:, s, :], xa[r0 + s * 128:r0 + (s + 1) * 128, :], xslot[s], writes=[xbuf_b[s]])
                k.dma(sp, tabg[:, 0, :], cosg_d[:, r0:r0 + TQ], tab_slot, writes=[tabg_b])
                k.dma(sp, tabg[:, 1, :], sing_d[:, r0:r0 + TQ], tab_slot, writes=[tabg_b])
                k.dma(sp, tabm[:, 0, :], cosm_d[:, r0:r0 + TQ], tab_slot, writes=[tabm_b])
                k.dma(sp, tabm[:, 1, :], sinm_d[:, r0:r0 + TQ], tab_slot, writes=[tabm_b])
                for s in range(4):
                    layer_norm_to_xn(s)
                transpose_modulate(0, 16)
                for h in range(2):
                    b = proj_T(wkv, wkv_b, h * 128, 128, 16, hrhs, hbufs)
                    rope_rms_finalize(b, gk[:, 0:1], tabg, tabg_b, 0, kgst[:, h, :], kgst_b)
                for h in range(2):
                    scr_toks["kg"] = k.dma(sp, kg_s[h, :, r0:r0 + TQ], kgst[:, h, :], stslot["kg"], reads=[kgst_b])
                for s in range(4):
                    b = next_ps()
                    for kc in range(16):
                        mm(psA[b][:, 0:256], hT[:, kc, s * 128:(s + 1) * 128], wkv[:, kc, 256:512], kc == 0, kc == 15,
                           [hT_b[kc], wkv_b], [psA_b[b]])
                    k.emit(act, lambda: A.activation(out=vgst[:, s, :], in_=psA[b][:, 0:256], func=AF.Identity),
                           reads=[psA_b[b]], writes=[vgst_b])
                for h in range(2):
                    scr_toks["vg"] = k.dma(sp, vg_s[h, :, t * 4:(t + 1) * 4, :], vgst[:, :, h * 128:(h + 1) * 128],
                                           stslot["vg"], reads=[vgst_b])
                for j in range(4):
                    b = proj_T(wkv, wkv_b, 512 + j * 128, 128, 16, hrhs, hbufs)
                    latent_chunk(b, j, mkg, kvl, kvl_b)
                latent_finish(kvl, kvl_b)
                b = proj_T(wkv, wkv_b, 1024, 64, 16, hrhs, hbufs)
                rope64_finalize(b, None, None, tabm, tabm_b, krst[:], krst_b)
                scr_toks["kr"] = k.dma(sp, kr_s[:, r0:r0 + TQ], krst[:], stslot["kr"], reads=[krst_b])
                for h in range(8):
                    b = next_ps()
                    for j in range(4):
                        mm(psA[b][:], wuk_v[:, j, h, 0, :], kvl[:, j, :], j == 0, j == 3, [wuk_b, kvl_b[j]], [psA_b[b]])
                    if h % 2 == 0:
                        k.emit(act, lambda: A.activation(out=knst[:, h, :], in_=psA[b][:], func=AF.Identity),
                               reads=[psA_b[b]], writes=[knst_b])
                    else:
                        k.emit(dve, lambda: V.tensor_copy(out=knst[:, h, :], in_=psA[b][:]),
                               reads=[psA_b[b]], writes=[knst_b])
                for h in range(8):
                    scr_toks["kn"] = k.dma(sp, kn_s[h, :, r0:r0 + TQ], knst[:, h, :], stslot["kn"], reads=[knst_b])
                for s in range(4):
                    for half in range(2):
                        b = next_ps()
                        for j in range(4):
                            mm(psA[b][:], kvl[:, j, s * 128:(s + 1) * 128], wuk_v[:, j, half * 4:(half + 1) * 4, 1, :],
                               j == 0, j == 3, [kvl_b[j], wuk_b], [psA_b[b]])
                        if half == 0:
                            k.emit(act, lambda: A.activation(out=vmst[:, s, 0:512], in_=psA[b][:], func=AF.Identity),
                                   reads=[psA_b[b]], writes=[vmst_b])
                        else:
                            k.emit(dve, lambda: V.tensor_copy(out=vmst[:, s, 512:1024], in_=psA[b][:]),
                                   reads=[psA_b[b]], writes=[vmst_b])
                for h in range(8):
                    scr_toks["vm"] = k.dma(sp, vm_s[h, :, t * 4:(t + 1) * 4, :], vmst[:, :, h * 128:(h + 1) * 128],
                                           stslot["vm"], reads=[vmst_b])
            while prep_state["next"] < NJOB or prep_state["pending"] is not None:
                prep_jobs(1)
            pA_fence = k.fence()
        scr_deps = tuple(scr_toks.values())
        pA_guard = pA_fence + scr_deps + p0_guard

        if stop_after == "pA":
            for s_ in k.slots:
                if s_.count:
                    sp.e.wait_ge(s_.sem, s_.count)
            return nc, dump_specs

        pB = ExitStack()
        with pB:
            r1 = sb("r1", [128, 22528], BF16, pB)
            r2 = sb("r2", [128, 14336], BF16, pB)
            lnc = sb("lnc", [128, 2, D], F32, pB)
            rl_t = sb("rl_t", [128, TQ], F32, pB)
            qg = r1[:, 0:4096].rearrange("p (h n) -> p h n", h=8)
            qn = r1[:, 4096:8192].rearrange("p (h n) -> p h n", h=8)
            qr = r1[0:64, 8192:12288].rearrange("p (h n) -> p h n", h=8)
            Pb = r1[:, 12288:14336].rearrange("p (h n) -> p h n", h=4)
            yT = r1[:, 14336:22528].rearrange("p (h n) -> p h n", h=16)
            aT = r1[:, 0:22528].rearrange("p (h n) -> p h n", h=44)
            Kr = r2[:, 0:4096].rearrange("p (s n) -> p s n", s=4)
            KRr = r2[0:64, 4096:8192].rearrange("p (s n) -> p s n", s=4)
            Vr = r2[:, 8192:12288].rearrange("p (s c d) -> p s c d", s=4, c=8)
            ql = r2[:, 12288:14336].rearrange("p (j n) -> p j n", j=4)
            mg = r2[:, 0:8192].rearrange("p (c n) -> p c n", c=16)
            qg_b = [Buf(f"qg{h}") for h in range(8)]
            qn_b = [Buf(f"qn{h}") for h in range(8)]
            qr_b = [Buf(f"qr{h}") for h in range(8)]
            P_b = [Buf(f"P{i}") for i in range(4)]
            y_b = [Buf(f"y{i}") for i in range(16)]
            aT_b = [Buf(f"aT{i}") for i in range(44)]
            kv_b = [Buf(f"kv{i}") for i in range(4)]
            ql_b = [Buf(f"ql{i}") for i in range(4)]
            mg_b = [Buf(f"mg{i}") for i in range(16)]
            lnc_b = Buf("lnc")
            rl_b = Buf("rl")
            kvslot = [k.newslot(f"kv{i}") for i in range(4)]
            lnslot = k.newslot("lnc")
            r1_att = qg_b + qn_b + qr_b + P_b + y_b
            r2_att = kv_b + ql_b
            for bb in r1_att + r2_att + aT_b + mg_b + [lnc_b, rl_b]:
                bb.guard = pA_guard
            kv_i = [0]
            hrhs = (lambda kc: hT[:, kc, :])
            hbufs = (lambda kc: [hT_b[kc]])
            SC_G = float(128.0 ** -0.5)
            SC_M = float(192.0 ** -0.5)

            for qt in range(nqt):
                r0 = qt * TQ
                for s in range(4):
                    k.dma(sp, xbuf[:, s, :], xq[r0 + s * 128:r0 + (s + 1) * 128, :], xslot[s], writes=[xbuf_b[s]])
                k.dma(sp, tabg[:, 0, :], cosgq_d[:, r0:r0 + TQ], tab_slot, writes=[tabg_b])
                k.dma(sp, tabg[:, 1, :], singq_d[:, r0:r0 + TQ], tab_slot, writes=[tabg_b])
                k.dma(sp, tabm[:, 0, :], cosmq_d[:, r0:r0 + TQ], tab_slot, writes=[tabm_b])
                k.dma(sp, tabm[:, 1, :], sinmq_d[:, r0:r0 + TQ], tab_slot, writes=[tabm_b])
                for s in range(4):
                    layer_norm_to_xn(s)
                transpose_modulate(0, 16)

                jobs = []
                for blk in range(4):
                    def ld(blk=blk):
                        return wload_packed(JOB_B2 + blk)

                    def cp(i, blk=blk):
                        wv = kview(wr[i], 16, 256)
                        for hh in range(2):
                            h = blk * 2 + hh
                            b = proj_T(wv, wr_b[i], hh * 128, 128, 16, hrhs, hbufs)
                            rope_rms_finalize(b, gq[:, 0:1], tabg, tabg_b, 0, qg[:, h, :], qg_b[h])
                    jobs.append((ld, cp))
                for blk in range(2):
                    def ld(blk=blk):
                        return wload_packed(JOB_B2 + 4 + blk)

                    def cp(i, blk=blk):
                        wv = kview(wr[i], 16, 256)
                        for jj in range(2):
                            j = blk * 2 + jj
                            b = proj_T(wv, wr_b[i], jj * 128, 128, 16, hrhs, hbufs)
                            latent_chunk(b, j, mqg, ql, ql_b)
                        if blk == 1:
                            latent_finish(ql, ql_b)
                    jobs.append((ld, cp))
                for blk in range(2):
                    def ld(blk=blk):
                        return wload_packed(JOB_B2 + 6 + blk, 3072)

                    def cp(i, blk=blk):
                        wv = kview(wr[i], 4, 768)
                        for hh in range(4):
                            h = blk * 4 + hh
                            b = proj_T(wv, wr_b[i], hh * 192, 128, 4, lambda j: ql[:, j, :], lambda j: [ql_b[j]])
                            if h % 2 == 0:
                                k.emit(act, lambda: A.activation(out=qn[:, h, :], in_=psA[b][:], func=AF.Identity),
                                       reads=[psA_b[b]], writes=[qn_b[h]])
                            else:
                                k.emit(dve, lambda: V.tensor_copy(out=qn[:, h, :], in_=psA[b][:]),
                                       reads=[psA_b[b]], writes=[qn_b[h]])
                            b = proj_T(wv, wr_b[i], hh * 192 + 128, 64, 4, lambda j: ql[:, j, :], lambda j: [ql_b[j]])
                            rope64_finalize(b, None, None, tabm, tabm_b, qr[:, h, :], qr_b[h])
                    jobs.append((ld, cp))
                stream(jobs)

                if stop_after == "B2" and qt == 0:
                    dbg_dump("dbg_qg", r1[:, 0:4096], [128, 4096], BF16, qg_b)
                    dbg_dump("dbg_qn", r1[:, 4096:8192], [128, 4096], BF16, qn_b)
                    dbg_dump("dbg_qr", r1[0:64, 8192:12288], [64, 4096], BF16, qr_b)
                    break

                NST = 16 * 64
                NPC = NST // 8
                piece_slot = {}

                def kv_load(pi):
                    hi, p = divmod(pi, 8)
                    sl = kv_i[0] % 4
                    kv_i[0] += 1
                    piece_slot[pi] = sl
                    if hi < 8:
                        j = hi // 4
                        k.dma(sp, Kr[:, sl, :], kg_s[j, :, p * KP:(p + 1) * KP], kvslot[sl], writes=[kv_b[sl]])
                        k.dma(sp, Vr[:, sl], vg_s[j, :, p * 8:(p + 1) * 8, :], kvslot[sl], writes=[kv_b[sl]])
                    else:
                        h = hi - 8
                        k.dma(sp, Kr[:, sl, :], kn_s[h, :, p * KP:(p + 1) * KP], kvslot[sl], writes=[kv_b[sl]])
                        k.dma(sp, KRr[:, sl, :], kr_s[:, p * KP:(p + 1) * KP], kvslot[sl], writes=[kv_b[sl]])
                        k.dma(sp, Vr[:, sl], vm_s[h, :, p * 8:(p + 1) * 8, :], kvslot[sl], writes=[kv_b[sl]])

                def S_step(n):
                    hi = n // 64
                    kk = n % 8
                    sl = piece_slot[n // 8]
                    b = n % 2
                    ksl = Kr[:, sl, kk * 128:(kk + 1) * 128]
                    if hi < 8:
                        mm(psA[b][:], ksl, qg[:, hi, :], True, True, [kv_b[sl], qg_b[hi]], [psA_b[b]])
                    else:
                        h = hi - 8
                        mm(psA[b][:], ksl, qn[:, h, :], True, False, [kv_b[sl], qn_b[h]], [psA_b[b]])
                        mm(psA[b][:], KRr[:, sl, kk * 128:(kk + 1) * 128], qr[:, h, :], False, True,
                           [kv_b[sl], qr_b[h]], [psA_b[b]])

                def E_step(n):
                    hi = n // 64
                    b = n % 2
                    sc = SC_G if hi < 8 else SC_M
                    k.emit(act, lambda: A.activation(out=Pb[:, n % 4, :], in_=psA[b][:], func=AF.Exp, scale=sc),
                           reads=[psA_b[b]], writes=[P_b[n % 4]])

                def PV_step(n):
                    hi, c = divmod(n, 64)
                    kk = n % 8
                    sl = piece_slot[n // 8]
                    ob, lb = 2 + hi % 2, 4 + hi % 2
                    mm(psA[ob][:], Vr[:, sl, kk, :], Pb[:, n % 4, :], c == 0, c == 63, [kv_b[sl], P_b[n % 4]], [psA_b[ob]])
                    mm(psA[lb][:], ones, Pb[:, n % 4, :], c == 0, c == 63, [cst_b, P_b[n % 4]], [psA_b[lb]])
                    if c == 63:
                        k.emit(dve, lambda: V.reciprocal(out=rl_t[:], in_=psA[lb][:]), reads=[psA_b[lb]], writes=[rl_b])
                        k.emit(dve, lambda: V.tensor_tensor(out=yT[:, hi, :], in0=psA[ob][:], in1=rl_t[:], op=ALU.mult),
                               reads=[psA_b[ob], rl_b], writes=[y_b[hi]])

                for pi in range(4):
                    kv_load(pi)
                S_step(0)
                S_step(1)
                for n in range(NST):
                    E_step(n)
                    PV_step(n)
                    if n + 2 < NST:
                        S_step(n + 2)
                    if (n + 1) % 8 == 0 and n // 8 + 4 < NPC:
                        kv_load(n // 8 + 4)

                if stop_after == "B4" and qt == 0:
                    dbg_dump("dbg_y", r1[:, 14336:22528], [128, 8192], BF16, y_b)
                    break

                g5 = k.fence()
                for bb in mg_b:
                    bb.guard = g5
                sets = tuple((o.t1, o.t2, o.t1_b, o.t2_b) for o in tsets)
                jobs = []
                for oc in range(16):
                    sg_t, sm_t, sg_b, sm_b = sets[oc % 2]

                    def ldx(oc=oc):
                        return wload_packed(JOB_B5 + 2 * oc)

                    def cpx(i, oc=oc, sg_t=sg_t, sm_t=sm_t, sg_b=sg_b, sm_b=sm_b):
                        wv = kview(wr[i], 16, 256)
                        b = proj_T(wv, wr_b[i], 0, 128, 16, hrhs, hbufs)
                        k.emit(act, lambda: A.activation(out=sg_t[:], in_=psA[b][:], func=AF.Sigmoid, bias=bgT[:, oc:oc + 1]),
                               reads=[psA_b[b], small_b], writes=[sg_b])
                        b = proj_T(wv, wr_b[i], 128, 128, 16, hrhs, hbufs)
                        k.emit(act, lambda: A.activation(out=sm_t[:], in_=psA[b][:], func=AF.Sigmoid,
                                                         bias=bgT[:, 16 + oc:17 + oc]),
                               reads=[psA_b[b], small_b], writes=[sm_b])

                    def ldy(oc=oc):
                        return wload_packed(JOB_B5 + 2 * oc + 1, 2048)

                    def cpy(i, oc=oc, sg_t=sg_t, sm_t=sm_t, sg_b=sg_b, sm_b=sm_b):
                        wv = kview(wr[i], 16, 128)
                        b = next_ps()
                        for kc in range(8):
                            mm(psA[b][:], wv[:, kc, :], yT[:, kc, :], kc == 0, kc == 7, [wr_b[i], y_b[kc]], [psA_b[b]])
                        k.emit(dve, lambda: V.tensor_tensor(out=sg_t[:], in0=psA[b][:], in1=sg_t[:], op=ALU.mult),
                               reads=[psA_b[b], sg_b], writes=[sg_b])
                        b = next_ps()
                        for kc in range(8):
                            mm(psA[b][:], wv[:, 8 + kc, :], yT[:, 8 + kc, :], kc == 0, kc == 7,
                               [wr_b[i], y_b[8 + kc]], [psA_b[b]])
                        k.emit(dve, lambda: V.tensor_tensor(out=sm_t[:], in0=psA[b][:], in1=sm_t[:], op=ALU.mult),
                               reads=[psA_b[b], sm_b], writes=[sm_b])
                        k.emit(dve, lambda: V.tensor_tensor(out=mg[:, oc, :], in0=sg_t[:], in1=sm_t[:], op=ALU.add),
                               reads=[sg_b, sm_b], writes=[mg_b[oc]])
                    jobs.append((ldx, cpx))
                    jobs.append((ldy, cpy))
                stream(jobs)

                k.dma(sp, lnc[:, 0, :], lnc_d[0], lnslot, writes=[lnc_b])
                k.dma(sp, lnc[:, 1, :], lnc_d[1], lnslot, writes=[lnc_b])
                jobs = []
                for blk in range(8):
                    def ld(blk=blk):
                        return wload_bf([(lambda t: kview(t, 16, 256),
                                          wout_s.rearrange("(k p) n -> p k n", p=128)[:, :, blk * 256:(blk + 1) * 256])],
                                        deps=fold_deps)

                    def cp(i, blk=blk):
                        wv = kview(wr[i], 16, 256)
                        for s in range(4):
                            b = next_ps()
                            for kc in range(16):
                                mm(psA[b][:, 0:256], mg[:, kc, s * 128:(s + 1) * 128], wv[:, kc, :], kc == 0, kc == 15,
                                   [mg_b[kc], wr_b[i]], [psA_b[b]])
                            xs = xbuf[:, s, blk * 256:(blk + 1) * 256]
                            k.emit(dve, lambda: V.scalar_tensor_tensor(out=xs, in0=xs, scalar=float(ALPHA),
                                                                       in1=psA[b][:, 0:256], op0=ALU.mult, op1=ALU.add),
                                   reads=[psA_b[b], xbuf_b[s]], writes=[xbuf_b[s]])
                    jobs.append((ld, cp))
                stream(jobs)
                g6 = k.fence()
                for bb in r2_att:
                    bb.guard = g6

                def ln_affine(s):
                    xs = xbuf[:, s, :]
                    for c4 in range(4):
                        k.emit(dve, lambda: V.bn_stats(out=stats[:, s, c4, :], in_=xs[:, c4 * 512:(c4 + 1) * 512]),
                               reads=[xbuf_b[s]], writes=[st_b[s]])
                    k.emit(dve, lambda: V.bn_aggr(out=mv[:, s, :], in_=stats[:, s].rearrange("p c f -> p (c f)")),
                           reads=[st_b[s]], writes=[st_b[s]])
                    rsqrt_small(rstd[:, s:s + 1], mv[:, s, 1:2], 0, [st_b[s]], st_b[s])
                    k.emit(dve, lambda: V.scalar_tensor_tensor(out=xs, in0=xs, scalar=mv[:, s, 0:1], in1=lnc[:, 0, :],
                                                               op0=ALU.subtract, op1=ALU.mult),
                           reads=[xbuf_b[s], st_b[s], lnc_b], writes=[xbuf_b[s]])
                    k.emit(dve, lambda: V.scalar_tensor_tensor(out=xs, in0=xs, scalar=rstd[:, s:s + 1], in1=lnc[:, 1, :],
                                                               op0=ALU.mult, op1=ALU.add),
                           reads=[xbuf_b[s], st_b[s], lnc_b], writes=[xbuf_b[s]])

                for s in range(4):
                    ln_affine(s)
                if stop_after == "B6" and qt == 0:
                    dbg_dump("dbg_mg", r2[:, 0:8192], [128, 8192], BF16, mg_b)
                    dbg_dump("dbg_x1", xbuf[:], [128, 4, D], F32, xbuf_b)
                    break
                for s in range(4):
                    layer_norm_to_xn(s)
                transpose_modulate(32, 48)
                k.dma(sp, lnc[:, 0, :], lnc_d[2], lnslot, writes=[lnc_b])
                k.dma(sp, lnc[:, 1, :], lnc_d[3], lnslot, writes=[lnc_b])

                g8 = k.fence()
                for bb in aT_b:
                    bb.guard = g8
                jobs = []
                for fc in range(44):
                    sg_t, _, sg_b, _ = sets[fc % 2]

                    def ld(fc=fc):
                        return wload_packed(JOB_B8 + fc)

                    def cp(i, fc=fc, sg_t=sg_t, sg_b=sg_b):
                        wv = kview(wr[i], 16, 256)
                        b = proj_T(wv, wr_b[i], 0, 128, 16, hrhs, hbufs)
                        k.emit(act, lambda: A.activation(out=sg_t[:], in_=psA[b][:], func=AF.Silu),
                               reads=[psA_b[b]], writes=[sg_b])
                        b = proj_T(wv, wr_b[i], 128, 128, 16, hrhs, hbufs)
                        k.emit(dve, lambda: V.tensor_tensor(out=aT[:, fc, :], in0=psA[b][:], in1=sg_t[:], op=ALU.mult),
                               reads=[psA_b[b], sg_b], writes=[aT_b[fc]])
                    jobs.append((ld, cp))
                stream(jobs)
                wdn_v = wdn_s.rearrange("(k p) n -> p k n", p=128)
                jobs = []
                for nb in range(4):
                    for g in range(6):
                        nf = 8 if g < 5 else 4

                        def ld(nb=nb, g=g, nf=nf):
                            return wload_bf([(lambda t: kview(t, 8, 512)[:, 0:nf, :],
                                              wdn_v[:, g * 8:g * 8 + nf, nb * 512:(nb + 1) * 512])], deps=fold_deps)

                        def cp(i, nb=nb, g=g, nf=nf):
                            wv = kview(wr[i], 8, 512)
                            for fi in range(nf):
                                fc = g * 8 + fi
                                for s in range(4):
                                    mm(psA[s][:], aT[:, fc, s * 128:(s + 1) * 128], wv[:, fi, :], fc == 0, fc == 43,
                                       [aT_b[fc], wr_b[i]], [psA_b[s]])
                            if g == 5:
                                for s in range(4):
                                    xs = xbuf[:, s, nb * 512:(nb + 1) * 512]
                                    k.emit(dve, lambda: V.scalar_tensor_tensor(out=xs, in0=xs, scalar=float(ALPHA),
                                                                               in1=psA[s][:], op0=ALU.mult, op1=ALU.add),
                                           reads=[psA_b[s], xbuf_b[s]], writes=[xbuf_b[s]])
                        jobs.append((ld, cp))
                stream(jobs)
                k.ps_i = 4
                g9 = k.fence()
                for bb in r1_att:
                    bb.guard = g9
                for s in range(4):
                    ln_affine(s)
                    k.dma(sp, y[r0 + s * 128:r0 + (s + 1) * 128, :], xbuf[:, s, :], yslot[s], reads=[xbuf_b[s]])

        for s_ in k.slots:
            if s_.count:
                sp.e.wait_ge(s_.sem, s_.count)
    return nc, dump_specs


def _perm(n):
    h = n // 2
    p = np.empty(n, np.int64)
    p[:h] = 2 * np.arange(h)
    p[h:] = 2 * np.arange(h) + 1
    return p


def _rope_tables(dim):
    quarter = dim // 4
    inv = 10000.0 ** (-np.arange(quarter, dtype=np.float64) / quarter)
    t = np.arange(S)
    ang = np.concatenate([(t // 64)[:, None] * inv[None, :], (t % 64)[:, None] * inv[None, :]], axis=1)
    c, s = np.cos(ang).T, np.sin(ang).T
    cosT = np.concatenate([c, c], axis=0)
    sinT = np.concatenate([-s, s], axis=0)
    return np.ascontiguousarray(cosT, np.float32), np.ascontiguousarray(sinT, np.float32)


def _swap(n):
    m = np.zeros((n, n), np.float32)
    idx = np.arange(n)
    m[(idx + n // 2) % n, idx] = 1.0
    return m


def make_in_maps(x, c, w_ada, b_ada, w_in, b_gates, gqa_q_gain, gqa_k_gain, mla_q_gain, mla_kv_gain,
                 w_mla_uq, w_mla_ukv, w_branch_gqa, w_branch_mla, w_out, ln1_g, ln1_b,
                 w_ffn_gate, w_ffn_up, w_ffn_down, ln2_g, ln2_b):
    f = lambda a: np.ascontiguousarray(np.asarray(a, dtype=np.float32))
    p128, p64 = _perm(128), _perm(64)
    cols = np.arange(6720)
    for h in range(8):
        cols[h * 128:(h + 1) * 128] = h * 128 + p128
    for h in range(2):
        cols[1024 + h * 128:1024 + (h + 1) * 128] = 1024 + h * 128 + p128
    cols[2560:2624] = 2560 + p64
    w_in_p = f(np.asarray(w_in)[0][:, cols])
    ucols = np.arange(1536)
    for h in range(8):
        ucols[h * 192 + 128:(h + 1) * 192] = h * 192 + 128 + p64
    w_uq_p = f(np.asarray(w_mla_uq)[0][:, ucols])
    ba = np.asarray(b_ada, np.float32)[0]
    badaT = f(np.concatenate([ba[0:D], ba[D:2 * D], ba[3 * D:4 * D], ba[4 * D:5 * D]]).reshape(64, 128).T)
    bgbc = f(np.broadcast_to(np.concatenate([ba[2 * D:3 * D], ba[5 * D:6 * D]])[None, :], (128, 2 * D)))
    lnc = f(np.stack([np.broadcast_to(np.asarray(v, np.float32)[0][None, :], (128, D))
                      for v in (ln1_g, ln1_b, ln2_g, ln2_b)]))
    cosg, sing = _rope_tables(128)
    cosm, sinm = _rope_tables(64)
    cst = f(np.stack([np.eye(128, dtype=np.float32), np.ones((128, 128), np.float32), _swap(128)], axis=1))
    common = dict(
        w_ada=f(np.asarray(w_ada)[0]), badaT=badaT, bgbc=bgbc, w_in=w_in_p,
        bgatesT=f(np.asarray(b_gates, np.float32)[0].reshape(32, 128).T),
        gq=f(np.asarray(gqa_q_gain, np.float32)[0][p128][:, None]),
        gk=f(np.asarray(gqa_k_gain, np.float32)[0][p128][:, None]),
        mqg=f(np.asarray(mla_q_gain, np.float32)[0].reshape(4, 128).T),
        mkg=f(np.asarray(mla_kv_gain, np.float32)[0].reshape(4, 128).T),
        w_uq=w_uq_p, w_ukv=f(np.asarray(w_mla_ukv)[0]), w_bg=f(np.asarray(w_branch_gqa)[0]),
        w_bm=f(np.asarray(w_branch_mla)[0]), w_out=f(np.asarray(w_out)[0]), lnc=lnc,
        w_fg=f(np.asarray(w_ffn_gate)[0]), w_fu=f(np.asarray(w_ffn_up)[0]), w_fd=f(np.asarray(w_ffn_down)[0]),
        cst=cst, pmat=_swap(64),
    )
    x = np.asarray(x, np.float32)
    c = np.asarray(c, np.float32)
    maps = []
    for core in range(8):
        b, half = divmod(core, 2)
        order = np.concatenate([np.arange(half * SQ, (half + 1) * SQ), np.arange((1 - half) * SQ, (2 - half) * SQ)])
        m = dict(common)
        m["xa"] = f(x[b][order])
        m["cT"] = f(c[b].reshape(16, 128).T)
        m["cosg"] = f(cosg[:, order])
        m["sing"] = f(sing[:, order])
        m["cosm"] = f(cosm[:, order])
        m["sinm"] = f(sinm[:, order])
        maps.append(m)
    return maps


_NC_CACHE = {}


def kernel(**inputs):
    if "nc" not in _NC_CACHE:
        _NC_CACHE["nc"] = build_program()[0]
    nc = _NC_CACHE["nc"]
    in_maps = make_in_maps(**inputs)
    res = run_bass_kernel_spmd(nc, in_maps, core_ids=list(range(8)))
    out = np.empty((4, S, D), np.float32)
    for core in range(8):
        b, half = divmod(core, 2)
        out[b, half * SQ:(half + 1) * SQ] = res.results[core]["y"]
    return out
```

```python
from contextlib import ExitStack
import numpy as np
import concourse.bass as bass
import concourse.mybir as mybir
from concourse.bass_utils import run_bass_kernel_spmd

F32 = mybir.dt.float32
BF16 = mybir.dt.bfloat16
AF = mybir.ActivationFunctionType
ALU = mybir.AluOpType

D = 2048
S = 8192
SQ = 4096
DFF = 5632
LN_EPS = 1e-5
RMS_EPS = 1e-6
ALPHA = 2.0 ** 0.25
TQ = 512
NQT = SQ // TQ
NKT = S // TQ
KP = 1024
NPIECE = S // KP
WSLOT = 4096


class Buf:
    __slots__ = ("name", "w", "r", "guard")

    def __init__(self, name):
        self.name = name
        self.w = None
        self.r = {}
        self.guard = ()


class Eng:
    def __init__(self, name, e, sem, is_pe=False):
        self.name, self.e, self.sem, self.is_pe = name, e, sem, is_pe
        self.n = 0
        self.seen = {}


class Slot:
    def __init__(self, sem):
        self.sem = sem
        self.count = 0


class K:
    def __init__(self, nc, es):
        self.nc = nc
        self.es = es
        self.nsem = 0
        self.pe = Eng("pe", nc.tensor, self.newsem("pe"), is_pe=True)
        self.act = Eng("act", nc.scalar, self.newsem("act"))
        self.dve = Eng("dve", nc.vector, self.newsem("dve"))
        self.pool = Eng("pool", nc.gpsimd, self.newsem("pool"))
        self.sp = Eng("sp", nc.sync, self.newsem("sp"))
        self.slots = []
        self.ps_i = 0

    def newsem(self, name):
        self.nsem += 1
        return self.es.enter_context(self.nc.semaphore(f"s{self.nsem}_{name}"))

    def newslot(self, name):
        s = Slot(self.newsem(name))
        self.slots.append(s)
        return s

    def _gather(self, eng, reads, writes, deps):
        need = {}

        def add(tok):
            if tok is None:
                return
            s, v = tok
            if s is eng.sem:
                if eng.is_pe or v <= eng.n - 3:
                    return
            if eng.seen.get(s, 0) >= v:
                return
            if need.get(s, 0) < v:
                need[s] = v

        for b in reads:
            add(b.w)
            for t in b.guard:
                add(t)
        for b in writes:
            add(b.w)
            for t in b.r.values():
                add(t)
            for t in b.guard:
                add(t)
        for t in deps:
            add(t)
        return list(need.items())

    def _apply_waits(self, eng, items, fn):
        for s, v in items[:-1]:
            eng.e.wait_ge(s, v)
            eng.seen[s] = v
        inst = fn()
        if items:
            s, v = items[-1]
            inst._wait_ge(s, v)
            eng.seen[s] = v
        return inst

    def emit(self, eng, fn, reads=(), writes=(), deps=()):
        items = self._gather(eng, reads, writes, deps)
        inst = self._apply_waits(eng, items, fn)
        eng.n += 1
        inst.then_inc(eng.sem, 1)
        tok = (eng.sem, eng.n)
        for b in reads:
            b.r[eng.sem] = tok
        for b in writes:
            b.w = tok
            b.r = {}
        return tok

    def dma(self, q, out, in_, slot, reads=(), writes=(), deps=()):
        items = self._gather(q, reads, writes, deps)
        inst = self._apply_waits(q, items, lambda: q.e.dma_start(out=out, in_=in_))
        slot.count += 16
        inst.then_inc(slot.sem, 16)
        tok = (slot.sem, slot.count)
        for b in reads:
            b.r[slot.sem] = tok
        for b in writes:
            b.w = tok
            b.r = {}
        return tok

    def fence(self):
        return tuple((e.sem, e.n) for e in (self.pe, self.act, self.dve, self.pool) if e.n > 0)


def build_program(nqt=NQT, nkt=NKT, stop_after=None, dumps=()):
    nc = bass.Bass("TRN2", target_bir_lowering=False)
    dumps = set(dumps)
    dump_specs = {}

    def din(name, shape, dt=F32):
        return nc.dram_tensor(name, list(shape), dt, kind="ExternalInput").ap()

    def dscr(name, shape, dt=BF16):
        kind = "ExternalOutput" if name in dumps else "Internal"
        if name in dumps:
            dump_specs[name] = (tuple(shape), dt)
        return nc.dram_tensor(name, list(shape), dt, kind=kind).ap()

    xa = din("xa", [S, D])
    xq = xa
    cT_d = din("cT", [128, 16])
    w_ada = din("w_ada", [D, 6 * D])
    badaT_d = din("badaT", [128, 64])
    bgbc_d = din("bgbc", [128, 2 * D])
    w_in = din("w_in", [D, 6720])
    bgT_d = din("bgatesT", [128, 32])
    gq_d = din("gq", [128, 1])
    gk_d = din("gk", [128, 1])
    mqg_d = din("mqg", [128, 4])
    mkg_d = din("mkg", [128, 4])
    w_uq = din("w_uq", [512, 1536])
    w_ukv = din("w_ukv", [512, 2048])
    w_bg = din("w_bg", [1024, D])
    w_bm = din("w_bm", [1024, D])
    w_out = din("w_out", [D, D])
    lnc_d = din("lnc", [4, 128, D])
    w_fg = din("w_fg", [D, DFF])
    w_fu = din("w_fu", [D, DFF])
    w_fd = din("w_fd", [DFF, D])
    cosg_d = din("cosg", [128, S])
    sing_d = din("sing", [128, S])
    cosm_d = din("cosm", [64, S])
    sinm_d = din("sinm", [64, S])
    cosgq_d, singq_d, cosmq_d, sinmq_d = cosg_d, sing_d, cosm_d, sinm_d
    cst_d = din("cst", [128, 3, 128])
    pm_d = din("pmat", [64, 64])

    y = nc.dram_tensor("y", [SQ, D], F32, kind="ExternalOutput").ap()

    kg_s = dscr("kg_s", [2, 128, S])
    vg_s = dscr("vg_s", [2, 128, S // 128, 128])
    kn_s = dscr("kn_s", [8, 128, S])
    kr_s = dscr("kr_s", [64, S])
    vm_s = dscr("vm_s", [8, 128, S // 128, 128])
    wout_s = dscr("wout_s", [D, D])
    wdn_s = dscr("wdn_s", [DFF, D])
    NJOB = 84
    wscr = dscr("wscr", [NJOB, 128, WSLOT])

    es = ExitStack()
    with es:
        k = K(nc, es)
        pe, act, dve, pool, sp = k.pe, k.act, k.dve, k.pool, k.sp
        T, V, A = nc.tensor, nc.vector, nc.scalar

        def sb(name, shape, dt, stack=es):
            return stack.enter_context(nc.sbuf_tensor("sb_" + name, list(shape), dt))

        psA = [es.enter_context(nc.psum_tensor(f"psA{i}", [128, 512], F32)) for i in range(8)]
        psA_b = [Buf(f"psA{i}") for i in range(8)]
        PS_ROT = (0, 1, 2, 3, 4, 6, 7)

        def next_ps():
            i = PS_ROT[k.ps_i % len(PS_ROT)]
            k.ps_i += 1
            return i

        cst = sb("cst", [128, 3, 128], BF16)
        pmat = sb("pmat", [64, 64], BF16)
        ident, ones, pg = cst[:, 0, :], cst[:, 1, :], cst[:, 2, :]
        cst_b, pm_b = Buf("cst"), Buf("pm")
        cT = sb("cT", [128, 16], F32)
        cact = sb("cact", [128, 16], BF16)
        cact_b = Buf("cact")
        modT = sb("modT", [128, 64], F32)
        modT_b = Buf("modT")
        badaT = sb("badaT", [128, 64], F32)
        bgT = sb("bgT", [128, 32], F32)
        gq = sb("gq", [128, 1], F32)
        gk = sb("gk", [128, 1], F32)
        mqg = sb("mqg", [128, 4], F32)
        mkg = sb("mkg", [128, 4], F32)
        small_b = Buf("small")
        epsc = sb("epsc", [128, 3], F32)
        epsc_b = Buf("epsc")
        xbuf = sb("xbuf", [128, 4, D], F32)
        xbuf_b = [Buf(f"xbuf{s}") for s in range(4)]
        xn = sb("xn", [128, 4, D], BF16)
        xn_b = [Buf(f"xn{s}") for s in range(4)]
        hT = sb("hT", [128, 16, TQ], BF16)
        hT_b = [Buf(f"hT{c}") for c in range(16)]
        NWS = 2
        wr = [sb(f"wr{i}", [128, WSLOT], BF16) for i in range(NWS)]
        wr_b = [Buf(f"wr{i}") for i in range(NWS)]
        wr_slot = [k.newslot(f"wr{i}") for i in range(NWS)]
        wr_i = [0]
        stats = sb("stats", [128, 4, 4, 6], F32)
        mv = sb("mv", [128, 4, 2], F32)
        rstd = sb("rstd", [128, 4], F32)
        nmr = sb("nmr", [128, 4], F32)
        st_b = [Buf(f"st{s}") for s in range(4)]
        tabg = sb("tabg", [128, 2, TQ], F32)
        tabm = sb("tabm", [64, 2, TQ], F32)
        tabg_b, tabm_b = Buf("tabg"), Buf("tabm")
        tab_slot = k.newslot("tab")
        class TSet:
            pass

        tsets = []
        for i in range(2):
            o = TSet()
            o.xg = sb(f"tmpxg{i}", [128, TQ], BF16)
            o.sq = sb(f"tmpsq{i}", [128, TQ], BF16)
            o.rs = sb(f"tmprs{i}", [128, TQ], F32)
            o.t1 = sb(f"tmpa{i}", [128, TQ], F32)
            o.t2 = sb(f"tmpb{i}", [128, TQ], F32)
            o.xg_b, o.sq_b, o.rs_b, o.t1_b, o.t2_b = (Buf(f"{n}{i}") for n in ("xg", "sq", "rs", "t1", "t2"))
            tsets.append(o)
        ts_i = [0]

        def next_tset():
            o = tsets[ts_i[0] % 2]
            ts_i[0] += 1
            return o

        rq_t = sb("rq_t", [128, TQ], F32)
        rq_b = Buf("rq")
        cslot = k.newslot("const")
        xslot = [k.newslot(f"x{s}") for s in range(4)]
        yslot = [k.newslot(f"y{s}") for s in range(4)]

        for ci, cv in enumerate((LN_EPS, 128.0 * RMS_EPS, 512.0 * RMS_EPS)):
            k.emit(dve, lambda: V.memset(epsc[:, ci:ci + 1], float(cv)), writes=[epsc_b])

        def rsqrt_small(out_ap, in_ap, eps_col, rbufs, wbuf):
            k.emit(act, lambda: A.activation(out=out_ap, in_=in_ap, func=AF.Sqrt, bias=epsc[:, eps_col:eps_col + 1]),
                   reads=list(rbufs) + [epsc_b], writes=[wbuf])
            k.emit(dve, lambda: V.reciprocal(out=out_ap, in_=out_ap), reads=[wbuf], writes=[wbuf])

        def wload(view_src_pairs):
            i = wr_i[0] % NWS
            wr_i[0] += 1
            for view_fn, src in view_src_pairs:
                k.dma(pool, view_fn(wr[i]), src, wr_slot[i], writes=[wr_b[i]])
            return i

        def wload_bf(view_src_pairs, deps=()):
            i = wr_i[0] % NWS
            wr_i[0] += 1
            for view_fn, src in view_src_pairs:
                k.dma(sp, view_fn(wr[i]), src, wr_slot[i], writes=[wr_b[i]], deps=deps)
            return i

        def stream(jobs, lookahead=NWS):
            n = len(jobs)
            slots = [None] * n
            for i in range(min(lookahead, n)):
                slots[i] = jobs[i][0]()
            for i in range(n):
                jobs[i][1](slots[i])
                if i + lookahead < n:
                    slots[i + lookahead] = jobs[i + lookahead][0]()

        def kview(t, kc, n):
            return t[:, 0:kc * n].rearrange("p (k n) -> p k n", k=kc)

        def wsrc(w, c0, n):
            return w.rearrange("(k p) n -> p k n", p=128)[:, :, c0:c0 + n]

        dbg_slot = k.newslot("dbg")

        def dbg_dump(name, ap, shape, dt, bufs):
            dmp = nc.dram_tensor(name, list(shape), dt, kind="ExternalOutput").ap()
            dump_specs[name] = (tuple(shape), dt)
            k.dma(sp, dmp, ap, dbg_slot, reads=bufs)

        def mm(out, lhsT, rhs, start, stop, reads, writes):
            return k.emit(pe, lambda: T.matmul(out, lhsT, rhs, start=start, stop=stop),
                          reads=reads, writes=writes)

        k.dma(pool, cst[:], cst_d, cslot, writes=[cst_b])
        k.dma(pool, pmat[:], pm_d, cslot, writes=[pm_b])
        for t_sb, t_d in ((cT, cT_d), (badaT, badaT_d), (bgT, bgT_d), (gq, gq_d), (gk, gk_d),
                          (mqg, mqg_d), (mkg, mkg_d)):
            k.dma(sp, t_sb[:], t_d, cslot, writes=[small_b])
        k.emit(act, lambda: A.activation(out=cact[:], in_=cT[:], func=AF.Silu), reads=[small_b], writes=[cact_b])
        k.emit(dve, lambda: V.tensor_scalar(out=mqg[:], in0=mqg[:], scalar1=float(512.0 ** 0.5), scalar2=None,
                                            op0=ALU.mult), reads=[small_b], writes=[small_b])
        k.emit(dve, lambda: V.tensor_scalar(out=mkg[:], in0=mkg[:], scalar1=float(512.0 ** 0.5), scalar2=None,
                                            op0=ALU.mult), reads=[small_b], writes=[small_b])

        p0 = ExitStack()
        with p0:
            crep = sb("crep", [128, 16, 128], BF16, p0)
            crep_b = Buf("crep")
            gbc = sb("gbc", [128, 2 * D], F32, p0)
            gbc_b = Buf("gbc")
            bgbc = sb("bgbc", [128, 2 * D], F32, p0)
            bgbc_b = Buf("bgbc")
            fst = [sb(f"fst{i}", [128, D], F32, p0) for i in range(2)]
            fbf = [sb(f"fbf{i}", [128, D], BF16, p0) for i in range(2)]
            fst_b = [Buf(f"fst{i}") for i in range(2)]
            fbf_b = [Buf(f"fbf{i}") for i in range(2)]
            fslot = [k.newslot(f"fst{i}") for i in range(2)]
            fsslot = [k.newslot(f"fbf{i}") for i in range(2)]

            k.dma(sp, bgbc[:], bgbc_d, cslot, writes=[bgbc_b])
            for kc in range(16):
                k.emit(dve, lambda: V.tensor_copy(out=crep[:, kc, :], in_=cact[:, kc:kc + 1].to_broadcast([128, 128])),
                       reads=[cact_b], writes=[crep_b])

            col_starts = [0, D, 3 * D, 4 * D]
            mod_ps = 5
            jobs = []
            for blk in range(32):
                seg, off = divmod(blk * 256, D)
                c0 = col_starts[seg] + off

                def ld(c0=c0):
                    return wload([(lambda t: kview(t, 16, 256), wsrc(w_ada, c0, 256))])

                def cp(i, blk=blk):
                    wv = kview(wr[i], 16, 256)
                    for cc in range(2):
                        j = blk * 2 + cc
                        for kc in range(16):
                            mm(psA[mod_ps][:, j:j + 1], wv[:, kc, cc * 128:(cc + 1) * 128], cact[:, kc:kc + 1],
                               kc == 0, kc == 15, [wr_b[i], cact_b], [psA_b[mod_ps]])
                jobs.append((ld, cp))
            stream(jobs)
            k.emit(dve, lambda: V.tensor_tensor(out=modT[:], in0=psA[mod_ps][:, 0:64], in1=badaT[:], op=ALU.add),
                   reads=[psA_b[mod_ps], small_b], writes=[modT_b])
            for lo in (16, 48):
                k.emit(dve, lambda: V.tensor_scalar(out=modT[:, lo:lo + 16], in0=modT[:, lo:lo + 16], scalar1=1.0,
                                                    scalar2=None, op0=ALU.add), reads=[modT_b], writes=[modT_b])
            jobs = []
            for blk in range(16):
                seg, off = divmod(blk * 256, D)
                c0 = (2 * D if seg == 0 else 5 * D) + off

                def ld(c0=c0):
                    return wload([(lambda t: kview(t, 16, 256), wsrc(w_ada, c0, 256))])

                def cp(i, blk=blk):
                    wv = kview(wr[i], 16, 256)
                    b = next_ps()
                    for kc in range(16):
                        mm(psA[b][:, 0:256], crep[:, kc, :], wv[:, kc, :], kc == 0, kc == 15,
                           [wr_b[i], crep_b], [psA_b[b]])
                    k.emit(dve, lambda: V.tensor_tensor(out=gbc[:, blk * 256:(blk + 1) * 256], in0=psA[b][:, 0:256],
                                                        in1=bgbc[:, blk * 256:(blk + 1) * 256], op=ALU.add),
                           reads=[psA_b[b], bgbc_b], writes=[gbc_b])
                jobs.append((ld, cp))
            stream(jobs)
            fold_toks = []
            nfold = 0
            for (wsrc_d, wdst, nrb, goff) in ((w_out, wout_s, 16, 0), (w_fd, wdn_s, 44, D)):
                for rb in range(nrb):
                    i = nfold % 2
                    nfold += 1
                    k.dma(sp, fst[i][:], wsrc_d[rb * 128:(rb + 1) * 128, :], fslot[i], writes=[fst_b[i]])
                    k.emit(dve, lambda: V.tensor_tensor(out=fbf[i][:], in0=fst[i][:], in1=gbc[:, goff:goff + D],
                                                        op=ALU.mult),
                           reads=[fst_b[i], gbc_b], writes=[fbf_b[i]])
                    fold_toks.append(k.dma(sp, wdst[rb * 128:(rb + 1) * 128, :], fbf[i][:], fsslot[i],
                                           reads=[fbf_b[i]]))
            fold_deps = tuple({t[0]: t for t in fold_toks}.values())
            p0_fence = k.fence()
        p0_guard = p0_fence + fold_deps

        if stop_after == "p0":
            dmp = nc.dram_tensor("dbg_modT", [128, 64], F32, kind="ExternalOutput").ap()
            dump_specs["dbg_modT"] = ((128, 64), F32)
            tk = k.dma(sp, dmp, modT[:], cslot, reads=[modT_b], deps=p0_guard)
            sp.e.wait_ge(cslot.sem, cslot.count)
            return nc, dump_specs

        def layer_norm_to_xn(s, src=None, src_b=None):
            xs = xbuf[:, s, :] if src is None else src
            src_b = xbuf_b[s] if src_b is None else src_b
            for c4 in range(4):
                k.emit(dve, lambda: V.bn_stats(out=stats[:, s, c4, :], in_=xs[:, c4 * 512:(c4 + 1) * 512]),
                       reads=[src_b], writes=[st_b[s]])
            k.emit(dve, lambda: V.bn_aggr(out=mv[:, s, :], in_=stats[:, s].rearrange("p c f -> p (c f)")),
                   reads=[st_b[s]], writes=[st_b[s]])
            rsqrt_small(rstd[:, s:s + 1], mv[:, s, 1:2], 0, [st_b[s]], st_b[s])
            k.emit(dve, lambda: V.scalar_tensor_tensor(out=nmr[:, s:s + 1], in0=mv[:, s, 0:1], scalar=-1.0,
                                                       in1=rstd[:, s:s + 1], op0=ALU.mult, op1=ALU.mult),
                   reads=[st_b[s]], writes=[st_b[s]])
            k.emit(dve, lambda: V.tensor_scalar(out=xn[:, s, :], in0=xs, scalar1=rstd[:, s:s + 1],
                                                scalar2=nmr[:, s:s + 1], op0=ALU.mult, op1=ALU.add),
                   reads=[src_b, st_b[s]], writes=[xn_b[s]])

        tr_i = [0]

        def transpose_modulate(shift_col, scale_col, hT=hT, hT_b=hT_b):
            for c in range(16):
                b = next_ps()
                dst = psA[b][:]
                for s in range(4):
                    k.emit(pe, lambda: T.matmul(psA[b][:, s * 128:(s + 1) * 128], xn[:, s, c * 128:(c + 1) * 128], ident,
                                                start=True, stop=True),
                           reads=[xn_b[s], cst_b], writes=[psA_b[b]])
                sc = modT[:, scale_col + c:scale_col + c + 1]
                sh = modT[:, shift_col + c:shift_col + c + 1]
                if c % 2 == 0:
                    k.emit(act, lambda: A.activation(out=hT[:, c, :], in_=dst, func=AF.Identity, bias=sh, scale=sc),
                           reads=[psA_b[b], modT_b], writes=[hT_b[c]])
                else:
                    k.emit(dve, lambda: V.tensor_scalar(out=hT[:, c, :], in0=dst, scalar1=sc, scalar2=sh,
                                                        op0=ALU.mult, op1=ALU.add),
                           reads=[psA_b[b], modT_b], writes=[hT_b[c]])

        deferred = []

        def flush_deferred():
            while deferred:
                deferred.pop(0)()

        def proj_T(wview, wbuf, ncol_lo, m, kchunks, rhs_fn, rhs_bufs):
            b = next_ps()
            for kc in range(kchunks):
                mm(psA[b][0:m, :], wview[:, kc, ncol_lo:ncol_lo + m], rhs_fn(kc), kc == 0, kc == kchunks - 1,
                   [wbuf] + rhs_bufs(kc), [psA_b[b]])
            flush_deferred()
            return b

        def rope_rms_finalize(b, gcol, tab, tab_b, col0, out_ap, out_b):
            ps = psA[b]
            o = next_tset()
            k.emit(act, lambda: A.activation(out=o.xg[:], in_=ps[:], func=AF.Identity, scale=gcol),
                   reads=[psA_b[b], small_b], writes=[o.xg_b])
            k.emit(act, lambda: A.activation(out=o.sq[:], in_=ps[:], func=AF.Square),
                   reads=[psA_b[b]], writes=[o.sq_b])
            def part2():
                b1 = next_ps()
                mm(psA[b1][:], ones, o.sq[:], True, True, [cst_b, o.sq_b], [psA_b[b1]])
                b2 = next_ps()
                mm(psA[b2][:], pg, o.xg[:], True, True, [cst_b, o.xg_b], [psA_b[b2]])
                rsqrt_small(o.rs[:], psA[b1][:], 1, [psA_b[b1]], o.rs_b)
                k.emit(dve, lambda: V.tensor_tensor(out=o.t1[:], in0=o.xg[:], in1=tab[:, 0, :], op=ALU.mult),
                       reads=[o.xg_b, tab_b], writes=[o.t1_b])
                k.emit(dve, lambda: V.tensor_tensor(out=o.t2[:], in0=psA[b2][:], in1=tab[:, 1, :], op=ALU.mult),
                       reads=[psA_b[b2], tab_b], writes=[o.t2_b])
                k.emit(dve, lambda: V.tensor_tensor(out=o.t1[:], in0=o.t1[:], in1=o.t2[:], op=ALU.add),
                       reads=[o.t1_b, o.t2_b], writes=[o.t1_b])
                k.emit(dve, lambda: V.scalar_tensor_tensor(out=out_ap, in0=o.t1[:], scalar=float(128.0 ** 0.5),
                                                           in1=o.rs[:], op0=ALU.mult, op1=ALU.mult),
                       reads=[o.t1_b, o.rs_b], writes=[out_b])
            deferred.append(part2)

        def rope64_finalize(b, scale_bc, scale_b, tab, tab_b, out_ap, out_b):
            ps = psA[b]
            o = next_tset()
            k.emit(act, lambda: A.activation(out=o.xg[0:64, :], in_=ps[0:64, :], func=AF.Identity),
                   reads=[psA_b[b]], writes=[o.xg_b])
            def part2():
                b2 = next_ps()
                mm(psA[b2][0:64, :], pmat[:], o.xg[0:64, :], True, True, [pm_b, o.xg_b], [psA_b[b2]])
                k.emit(dve, lambda: V.tensor_tensor(out=o.t1[0:64, :], in0=o.xg[0:64, :], in1=tab[:, 0, :], op=ALU.mult),
                       reads=[o.xg_b, tab_b], writes=[o.t1_b])
                k.emit(dve, lambda: V.tensor_tensor(out=o.t2[0:64, :], in0=psA[b2][0:64, :], in1=tab[:, 1, :], op=ALU.mult),
                       reads=[psA_b[b2], tab_b], writes=[o.t2_b])
                if scale_bc is None:
                    k.emit(dve, lambda: V.tensor_tensor(out=out_ap, in0=o.t1[0:64, :], in1=o.t2[0:64, :], op=ALU.add),
                           reads=[o.t1_b, o.t2_b], writes=[out_b])
                else:
                    k.emit(dve, lambda: V.tensor_tensor(out=o.t1[0:64, :], in0=o.t1[0:64, :], in1=o.t2[0:64, :], op=ALU.add),
                           reads=[o.t1_b, o.t2_b], writes=[o.t1_b])
                    k.emit(dve, lambda: V.tensor_tensor(out=out_ap, in0=o.t1[0:64, :], in1=scale_bc[0:64, :], op=ALU.mult),
                           reads=[o.t1_b, scale_b], writes=[out_b])
            deferred.append(part2)

        def latent_chunk(b, j, gains, lat, lat_b):
            o = next_tset()
            k.emit(act, lambda: A.activation(out=lat[:, j, :], in_=psA[b][:], func=AF.Identity,
                                             scale=gains[:, j:j + 1]),
                   reads=[psA_b[b], small_b], writes=[lat_b[j]])
            k.emit(act, lambda: A.activation(out=o.sq[:], in_=psA[b][:], func=AF.Square),
                   reads=[psA_b[b]], writes=[o.sq_b])
            deferred.append(lambda: mm(psA[5][:], ones, o.sq[:], j == 0, j == 3, [cst_b, o.sq_b], [psA_b[5]]))

        def latent_finish(lat, lat_b):
            flush_deferred()
            rsqrt_small(rq_t[:], psA[5][:], 2, [psA_b[5]], rq_b)
            for j in range(4):
                k.emit(dve, lambda: V.tensor_tensor(out=lat[:, j, :], in0=lat[:, j, :], in1=rq_t[:], op=ALU.mult),
                       reads=[lat_b[j], rq_b], writes=[lat_b[j]])

        def v16(t):
            return kview(t, 16, 256)

        job_specs = []
        for blk in range(4):
            job_specs.append([(v16, wsrc(w_in, blk * 256, 256))])
        for blk in range(2):
            job_specs.append([(v16, wsrc(w_in, 1536 + blk * 256, 256))])
        for blk in range(2):
            job_specs.append([(lambda t: kview(t, 4, 768), wsrc(w_uq, blk * 768, 768))])
        for oc in range(16):
            job_specs.append([(lambda t: v16(t)[:, :, 0:128], wsrc(w_in, 2624 + oc * 128, 128)),
                              (lambda t: v16(t)[:, :, 128:256], wsrc(w_in, 2624 + D + oc * 128, 128))])
            job_specs.append([(lambda t: kview(t, 16, 128)[:, 0:8, :], wsrc(w_bg, oc * 128, 128)),
                              (lambda t: kview(t, 16, 128)[:, 8:16, :], wsrc(w_bm, oc * 128, 128))])
        for fc in range(44):
            job_specs.append([(lambda t: v16(t)[:, :, 0:128], wsrc(w_fg, fc * 128, 128)),
                              (lambda t: v16(t)[:, :, 128:256], wsrc(w_fu, fc * 128, 128))])
        assert len(job_specs) == NJOB
        JOB_B2, JOB_B5, JOB_B8 = 0, 8, 40
        pk_slot = [k.newslot(f"pk{i}") for i in range(NWS)]
        prep_state = {"next": 0, "pending": None, "toks": {}}

        def prep_jobs(n):
            for _ in range(n):
                j = prep_state["next"]
                if j < NJOB:
                    i = wload(job_specs[j])
                    prep_state["next"] = j + 1
                else:
                    i = None
                pend = prep_state["pending"]
                if pend is not None:
                    pj, pi = pend
                    prep_state["toks"][pi] = k.dma(pool, wscr[pj], wr[pi][:], pk_slot[pi], reads=[wr_b[pi]])
                prep_state["pending"] = (j, i) if i is not None else None

        def wload_packed(j, n=WSLOT):
            i = wr_i[0] % NWS
            wr_i[0] += 1
            k.dma(pool, wr[i][:, 0:n], wscr[j, :, 0:n], wr_slot[i], writes=[wr_b[i]],
                  deps=tuple(prep_state["toks"].values()))
            return i

        pA = ExitStack()
        with pA:
            wkv = sb("wkv", [128, 16, 1088], BF16, pA)
            wkv_b = Buf("wkv")
            wuk = sb("wuk", [128, 4, 2048], BF16, pA)
            wuk_b = Buf("wuk")
            kvl = sb("kvl", [128, 4, TQ], BF16, pA)
            kvl_b = [Buf(f"kvl{j}") for j in range(4)]
            kgst = sb("kgst", [128, 2, TQ], BF16, pA)
            vgst = sb("vgst", [128, 4, 256], BF16, pA)
            knst = sb("knst", [128, 8, TQ], BF16, pA)
            krst = sb("krst", [64, TQ], BF16, pA)
            vmst = sb("vmst", [128, 4, 1024], BF16, pA)
            kgst_b, vgst_b, knst_b, krst_b, vmst_b = (Buf(n) for n in ("kgst", "vgst", "knst", "krst", "vmst"))
            for bb in kvl_b + [wkv_b, wuk_b, kgst_b, vgst_b, knst_b, krst_b, vmst_b]:
                bb.guard = p0_guard
            stslot = {n: k.newslot(n) for n in ("kg", "vg", "kn", "kr", "vm")}
            scr_toks = {}

            wv_in = w_in.rearrange("(k p) n -> p k n", p=128)
            k.dma(pool, wkv[:, :, 0:512], wv_in[:, :, 1024:1536], cslot, writes=[wkv_b])
            k.dma(pool, wkv[:, :, 512:1088], wv_in[:, :, 2048:2624], cslot, writes=[wkv_b])
            k.dma(pool, wuk[:], w_ukv.rearrange("(k p) n -> p k n", p=128), cslot, writes=[wuk_b])
            wuk_v = wuk[:].rearrange("p k (h t d) -> p k h t d", h=8, t=2)
            hT2 = sb("hT2", [128, 16, TQ], BF16, pA)
            hT2_b = [Buf(f"hT2_{c}") for c in range(16)]
            for bb in hT2_b:
                bb.guard = p0_guard
            hTs = ((hT, hT_b), (hT2, hT2_b))

            def pa_front(t):
                rr = t * TQ
                for s in range(4):
                    k.dma(sp, xbuf[:, s, :], xa[rr + s * 128:rr + (s + 1) * 128, :], xslot[s], writes=[xbuf_b[s]])
                for s in range(4):
                    layer_norm_to_xn(s)

            def pa_tables(t):
                rr = t * TQ
                k.dma(sp, tabg[:, 0, :], cosg_d[:, rr:rr + TQ], tab_slot, writes=[tabg_b])
                k.dma(sp, tabg[:, 1, :], sing_d[:, rr:rr + TQ], tab_slot, writes=[tabg_b])
                k.dma(sp, tabm[:, 0, :], cosm_d[:, rr:rr + TQ], tab_slot, writes=[tabm_b])
                k.dma(sp, tabm[:, 1, :], sinm_d[:, rr:rr + TQ], tab_slot, writes=[tabm_b])

            pa_front(0)
            pa_tables(0)
            transpose_modulate(0, 16, *hTs[0])
            for t in range(nkt):
                r0 = t * TQ
                prep_jobs(6 if nkt == NKT else (NJOB + 1) // nkt + 1)
                hTc, hTc_b = hTs[t % 2]
                hrhs = (lambda kc, hTc=hTc: hTc[:, kc, :])
                hbufs = (lambda kc, hTc_b=hTc_b: [hTc_b[kc]])
                if t + 1 < nkt:
                    pa_front(t + 1)
                for h in range(2):
                    b = proj_T(wkv, wkv_b, h * 128, 128, 16, hrhs, hbufs)
                    rope_rms_finalize(b, gk[:, 0:1], tabg, tabg_b, 0, kgst[:, h, :], kgst_b)
                flush_deferred()
                for h in range(2):
                    scr_toks["kg"] = k.dma(sp, kg_s[h, :, r0:r0 + TQ], kgst[:, h, :], stslot["kg"], reads=[kgst_b])
                for s in range(4):
                    b = next_ps()
                    for kc in range(16):
                        mm(psA[b][:, 0:256], hTc[:, kc, s * 128:(s + 1) * 128], wkv[:, kc, 256:512], kc == 0, kc == 15,
                           [hTc_b[kc], wkv_b], [psA_b[b]])
                    k.emit(act, lambda: A.activation(out=vgst[:, s, :], in_=psA[b][:, 0:256], func=AF.Identity),
                           reads=[psA_b[b]], writes=[vgst_b])
                for h in range(2):
                    scr_toks["vg"] = k.dma(sp, vg_s[h, :, t * 4:(t + 1) * 4, :], vgst[:, :, h * 128:(h + 1) * 128],
                                           stslot["vg"], reads=[vgst_b])
                for j in range(4):
                    b = proj_T(wkv, wkv_b, 512 + j * 128, 128, 16, hrhs, hbufs)
                    latent_chunk(b, j, mkg, kvl, kvl_b)
                latent_finish(kvl, kvl_b)
                b = proj_T(wkv, wkv_b, 1024, 64, 16, hrhs, hbufs)
                rope64_finalize(b, None, None, tabm, tabm_b, krst[:], krst_b)
                flush_deferred()
                scr_toks["kr"] = k.dma(sp, kr_s[:, r0:r0 + TQ], krst[:], stslot["kr"], reads=[krst_b])
                flush_deferred()
                for h in range(8):
                    b = next_ps()
                    for j in range(4):
                        mm(psA[b][:], wuk_v[:, j, h, 0, :], kvl[:, j, :], j == 0, j == 3, [wuk_b, kvl_b[j]], [psA_b[b]])
                    if h % 2 == 0:
                        k.emit(act, lambda: A.activation(out=knst[:, h, :], in_=psA[b][:], func=AF.Identity),
                               reads=[psA_b[b]], writes=[knst_b])
                    else:
                        k.emit(dve, lambda: V.tensor_copy(out=knst[:, h, :], in_=psA[b][:]),
                               reads=[psA_b[b]], writes=[knst_b])
                for h in range(8):
                    scr_toks["kn"] = k.dma(sp, kn_s[h, :, r0:r0 + TQ], knst[:, h, :], stslot["kn"], reads=[knst_b])
                for s in range(4):
                    for half in range(2):
                        b = next_ps()
                        for j in range(4):
                            mm(psA[b][:], kvl[:, j, s * 128:(s + 1) * 128], wuk_v[:, j, half * 4:(half + 1) * 4, 1, :],
                               j == 0, j == 3, [kvl_b[j], wuk_b], [psA_b[b]])
                        if half == 0:
                            k.emit(act, lambda: A.activation(out=vmst[:, s, 0:512], in_=psA[b][:], func=AF.Identity),
                                   reads=[psA_b[b]], writes=[vmst_b])
                        else:
                            k.emit(dve, lambda: V.tensor_copy(out=vmst[:, s, 512:1024], in_=psA[b][:]),
                                   reads=[psA_b[b]], writes=[vmst_b])
                for h in range(8):
                    scr_toks["vm"] = k.dma(sp, vm_s[h, :, t * 4:(t + 1) * 4, :], vmst[:, :, h * 128:(h + 1) * 128],
                                           stslot["vm"], reads=[vmst_b])
                if t + 1 < nkt:
                    pa_tables(t + 1)
                    transpose_modulate(0, 16, *hTs[(t + 1) % 2])
            while prep_state["next"] < NJOB or prep_state["pending"] is not None:
                prep_jobs(1)
            pA_fence = k.fence()
        scr_deps = tuple(scr_toks.values())
        pA_guard = pA_fence + scr_deps + p0_guard

        if stop_after == "pA":
            for s_ in k.slots:
                if s_.count:
                    sp.e.wait_ge(s_.sem, s_.count)
            return nc, dump_specs

        pB = ExitStack()
        with pB:
            r1 = sb("r1", [128, 22528], BF16, pB)
            r2 = sb("r2", [128, 14336], BF16, pB)
            lnc = sb("lnc", [128, 2, D], F32, pB)
            rl_t = sb("rl_t", [128, TQ], F32, pB)
            qg = r1[:, 0:4096].rearrange("p (h n) -> p h n", h=8)
            qn = r1[:, 4096:8192].rearrange("p (h n) -> p h n", h=8)
            qr = r1[0:64, 8192:12288].rearrange("p (h n) -> p h n", h=8)
            Pb = r1[:, 12288:14336].rearrange("p (h n) -> p h n", h=4)
            yT = r1[:, 14336:22528].rearrange("p (h n) -> p h n", h=16)
            aT = r1[:, 0:22528].rearrange("p (h n) -> p h n", h=44)
            Kr = r2[:, 0:4096].rearrange("p (s n) -> p s n", s=4)
            KRr = r2[0:64, 4096:8192].rearrange("p (s n) -> p s n", s=4)
            Vr = r2[:, 8192:12288].rearrange("p (s c d) -> p s c d", s=4, c=8)
            ql = r2[:, 12288:14336].rearrange("p (j n) -> p j n", j=4)
            mg = r2[:, 0:8192].rearrange("p (c n) -> p c n", c=16)
            qg_b = [Buf(f"qg{h}") for h in range(8)]
            qn_b = [Buf(f"qn{h}") for h in range(8)]
            qr_b = [Buf(f"qr{h}") for h in range(8)]
            P_b = [Buf(f"P{i}") for i in range(4)]
            y_b = [Buf(f"y{i}") for i in range(16)]
            aT_b = [Buf(f"aT{i}") for i in range(44)]
            kv_b = [Buf(f"kv{i}") for i in range(4)]
            ql_b = [Buf(f"ql{i}") for i in range(4)]
            mg_b = [Buf(f"mg{i}") for i in range(16)]
            lnc_b = Buf("lnc")
            rl_b = Buf("rl")
            kvslot = [k.newslot(f"kv{i}") for i in range(4)]
            lnslot = k.newslot("lnc")
            r1_att = qg_b + qn_b + qr_b + P_b + y_b
            r2_att = kv_b + ql_b
            for bb in r1_att + r2_att + aT_b + mg_b + [lnc_b, rl_b]:
                bb.guard = pA_guard
            kv_i = [0]
            hrhs = (lambda kc: hT[:, kc, :])
            hbufs = (lambda kc: [hT_b[kc]])
            xs_t = sb("xs_t", [128, D], F32, pB)
            xs_b = Buf("xs")
            xs_b.guard = pA_guard
            xs_slot = k.newslot("xs")

            def prefetch_tables(tn):
                rr = tn * TQ
                k.dma(sp, tabg[:, 0, :], cosgq_d[:, rr:rr + TQ], tab_slot, writes=[tabg_b])
                k.dma(sp, tabg[:, 1, :], singq_d[:, rr:rr + TQ], tab_slot, writes=[tabg_b])
                k.dma(sp, tabm[:, 0, :], cosmq_d[:, rr:rr + TQ], tab_slot, writes=[tabm_b])
                k.dma(sp, tabm[:, 1, :], sinmq_d[:, rr:rr + TQ], tab_slot, writes=[tabm_b])

            def prefetch_ln(tn, s):
                rr = tn * TQ + s * 128
                k.dma(sp, xs_t[:], xq[rr:rr + 128, :], xs_slot, writes=[xs_b])
                layer_norm_to_xn(s, xs_t[:], xs_b)

            def load_x_resid(tn):
                rr = tn * TQ
                for s in range(4):
                    k.dma(sp, xbuf[:, s, :], xq[rr + s * 128:rr + (s + 1) * 128, :], xslot[s], writes=[xbuf_b[s]])

            prefetch_tables(0)
            for s in range(4):
                prefetch_ln(0, s)
            transpose_modulate(0, 16)
            load_x_resid(0)
            SC_G = float(128.0 ** -0.5)
            SC_M = float(192.0 ** -0.5)

            for qt in range(nqt):
                r0 = qt * TQ
                has_next = qt + 1 < nqt

                jobs = []
                for blk in range(4):
                    def ld(blk=blk):
                        return wload_packed(JOB_B2 + blk)

                    def cp(i, blk=blk):
                        wv = kview(wr[i], 16, 256)
                        for hh in range(2):
                            h = blk * 2 + hh
                            b = proj_T(wv, wr_b[i], hh * 128, 128, 16, hrhs, hbufs)
                            rope_rms_finalize(b, gq[:, 0:1], tabg, tabg_b, 0, qg[:, h, :], qg_b[h])
                    jobs.append((ld, cp))
                for blk in range(2):
                    def ld(blk=blk):
                        return wload_packed(JOB_B2 + 4 + blk)

                    def cp(i, blk=blk):
                        wv = kview(wr[i], 16, 256)
                        for jj in range(2):
                            j = blk * 2 + jj
                            b = proj_T(wv, wr_b[i], jj * 128, 128, 16, hrhs, hbufs)
                            latent_chunk(b, j, mqg, ql, ql_b)
                        if blk == 1:
                            latent_finish(ql, ql_b)
                    jobs.append((ld, cp))
                for blk in range(2):
                    def ld(blk=blk):
                        return wload_packed(JOB_B2 + 6 + blk, 3072)

                    def cp(i, blk=blk):
                        wv = kview(wr[i], 4, 768)
                        for hh in range(4):
                            h = blk * 4 + hh
                            b = proj_T(wv, wr_b[i], hh * 192, 128, 4, lambda j: ql[:, j, :], lambda j: [ql_b[j]])
                            if h % 2 == 0:
                                k.emit(act, lambda: A.activation(out=qn[:, h, :], in_=psA[b][:], func=AF.Identity),
                                       reads=[psA_b[b]], writes=[qn_b[h]])
                            else:
                                k.emit(dve, lambda: V.tensor_copy(out=qn[:, h, :], in_=psA[b][:]),
                                       reads=[psA_b[b]], writes=[qn_b[h]])
                            b = proj_T(wv, wr_b[i], hh * 192 + 128, 64, 4, lambda j: ql[:, j, :], lambda j: [ql_b[j]])
                            rope64_finalize(b, None, None, tabm, tabm_b, qr[:, h, :], qr_b[h])
                    jobs.append((ld, cp))
                stream(jobs)
                flush_deferred()

                if stop_after == "B2" and qt == 0:
                    dbg_dump("dbg_qg", r1[:, 0:4096], [128, 4096], BF16, qg_b)
                    dbg_dump("dbg_qn", r1[:, 4096:8192], [128, 4096], BF16, qn_b)
                    dbg_dump("dbg_qr", r1[0:64, 8192:12288], [64, 4096], BF16, qr_b)
                    break

                NST = 16 * 64
                NPC = NST // 8
                piece_slot = {}

                def kv_load(pi):
                    hi, p = divmod(pi, 8)
                    sl = kv_i[0] % 4
                    kv_i[0] += 1
                    piece_slot[pi] = sl
                    if hi < 8:
                        j = hi // 4
                        k.dma(sp, Kr[:, sl, :], kg_s[j, :, p * KP:(p + 1) * KP], kvslot[sl], writes=[kv_b[sl]])
                        k.dma(sp, Vr[:, sl], vg_s[j, :, p * 8:(p + 1) * 8, :], kvslot[sl], writes=[kv_b[sl]])
                    else:
                        h = hi - 8
                        k.dma(sp, Kr[:, sl, :], kn_s[h, :, p * KP:(p + 1) * KP], kvslot[sl], writes=[kv_b[sl]])
                        k.dma(sp, KRr[:, sl, :], kr_s[:, p * KP:(p + 1) * KP], kvslot[sl], writes=[kv_b[sl]])
                        k.dma(sp, Vr[:, sl], vm_s[h, :, p * 8:(p + 1) * 8, :], kvslot[sl], writes=[kv_b[sl]])

                S_BANKS = (0, 1, 6, 7)

                def S_step(n):
                    hi = n // 64
                    kk = n % 8
                    sl = piece_slot[n // 8]
                    b = S_BANKS[n % 4]
                    ksl = Kr[:, sl, kk * 128:(kk + 1) * 128]
                    if hi < 8:
                        mm(psA[b][:], ksl, qg[:, hi, :], True, True, [kv_b[sl], qg_b[hi]], [psA_b[b]])
                    else:
                        h = hi - 8
                        mm(psA[b][:], ksl, qn[:, h, :], True, False, [kv_b[sl], qn_b[h]], [psA_b[b]])
                        mm(psA[b][:], KRr[:, sl, kk * 128:(kk + 1) * 128], qr[:, h, :], False, True,
                           [kv_b[sl], qr_b[h]], [psA_b[b]])

                def E_step(n):
                    hi = n // 64
                    b = S_BANKS[n % 4]
                    sc = SC_G if hi < 8 else SC_M
                    k.emit(act, lambda: A.activation(out=Pb[:, n % 4, :], in_=psA[b][:], func=AF.Exp, scale=sc),
                           reads=[psA_b[b]], writes=[P_b[n % 4]])

                def PV_step(n):
                    hi, c = divmod(n, 64)
                    kk = n % 8
                    sl = piece_slot[n // 8]
                    ob, lb = 2 + hi % 2, 4 + hi % 2
                    mm(psA[ob][:], Vr[:, sl, kk, :], Pb[:, n % 4, :], c == 0, c == 63, [kv_b[sl], P_b[n % 4]], [psA_b[ob]])
                    mm(psA[lb][:], ones, Pb[:, n % 4, :], c == 0, c == 63, [cst_b, P_b[n % 4]], [psA_b[lb]])
                    if c == 63:
                        k.emit(dve, lambda: V.reciprocal(out=rl_t[:], in_=psA[lb][:]), reads=[psA_b[lb]], writes=[rl_b])
                        k.emit(dve, lambda: V.tensor_tensor(out=yT[:, hi, :], in0=psA[ob][:], in1=rl_t[:], op=ALU.mult),
                               reads=[psA_b[ob], rl_b], writes=[y_b[hi]])

                for pi in range(4):
                    kv_load(pi)
                for n in range(4):
                    S_step(n)
                for n in range(NST):
                    E_step(n)
                    PV_step(n)
                    if n + 4 < NST:
                        S_step(n + 4)
                    if (n + 1) % 8 == 0 and n // 8 + 4 < NPC:
                        kv_load(n // 8 + 4)

                if stop_after == "B4" and qt == 0:
                    dbg_dump("dbg_y", r1[:, 14336:22528], [128, 8192], BF16, y_b)
                    break

                g5 = k.fence()
                for bb in mg_b:
                    bb.guard = g5
                sets = tuple((o.t1, o.t2, o.t1_b, o.t2_b) for o in tsets)
                jobs = []
                for oc in range(16):
                    sg_t, sm_t, sg_b, sm_b = sets[oc % 2]

                    def ldx(oc=oc):
                        return wload_packed(JOB_B5 + 2 * oc)

                    def cpx(i, oc=oc, sg_t=sg_t, sm_t=sm_t, sg_b=sg_b, sm_b=sm_b):
                        wv = kview(wr[i], 16, 256)
                        b = proj_T(wv, wr_b[i], 0, 128, 16, hrhs, hbufs)
                        k.emit(act, lambda: A.activation(out=sg_t[:], in_=psA[b][:], func=AF.Sigmoid, bias=bgT[:, oc:oc + 1]),
                               reads=[psA_b[b], small_b], writes=[sg_b])
                        b = proj_T(wv, wr_b[i], 128, 128, 16, hrhs, hbufs)
                        k.emit(act, lambda: A.activation(out=sm_t[:], in_=psA[b][:], func=AF.Sigmoid,
                                                         bias=bgT[:, 16 + oc:17 + oc]),
                               reads=[psA_b[b], small_b], writes=[sm_b])

                    def ldy(oc=oc):
                        return wload_packed(JOB_B5 + 2 * oc + 1, 2048)

                    def cpy(i, oc=oc, sg_t=sg_t, sm_t=sm_t, sg_b=sg_b, sm_b=sm_b):
                        wv = kview(wr[i], 16, 128)
                        b = next_ps()
                        for kc in range(8):
                            mm(psA[b][:], wv[:, kc, :], yT[:, kc, :], kc == 0, kc == 7, [wr_b[i], y_b[kc]], [psA_b[b]])
                        k.emit(dve, lambda: V.tensor_tensor(out=sg_t[:], in0=psA[b][:], in1=sg_t[:], op=ALU.mult),
                               reads=[psA_b[b], sg_b], writes=[sg_b])
                        b = next_ps()
                        for kc in range(8):
                            mm(psA[b][:], wv[:, 8 + kc, :], yT[:, 8 + kc, :], kc == 0, kc == 7,
                               [wr_b[i], y_b[8 + kc]], [psA_b[b]])
                        k.emit(dve, lambda: V.tensor_tensor(out=sm_t[:], in0=psA[b][:], in1=sm_t[:], op=ALU.mult),
                               reads=[psA_b[b], sm_b], writes=[sm_b])
                        k.emit(dve, lambda: V.tensor_tensor(out=mg[:, oc, :], in0=sg_t[:], in1=sm_t[:], op=ALU.add),
                               reads=[sg_b, sm_b], writes=[mg_b[oc]])
                    jobs.append((ldx, cpx))
                    jobs.append((ldy, cpy))
                stream(jobs)

                k.dma(sp, lnc[:, 0, :], lnc_d[0], lnslot, writes=[lnc_b])
                k.dma(sp, lnc[:, 1, :], lnc_d[1], lnslot, writes=[lnc_b])
                jobs = []
                for blk in range(8):
                    def ld(blk=blk):
                        return wload_bf([(lambda t: kview(t, 16, 256),
                                          wout_s.rearrange("(k p) n -> p k n", p=128)[:, :, blk * 256:(blk + 1) * 256])],
                                        deps=fold_deps)

                    def cp(i, blk=blk):
                        wv = kview(wr[i], 16, 256)
                        for s in range(4):
                            b = next_ps()
                            for kc in range(16):
                                mm(psA[b][:, 0:256], mg[:, kc, s * 128:(s + 1) * 128], wv[:, kc, :], kc == 0, kc == 15,
                                   [mg_b[kc], wr_b[i]], [psA_b[b]])
                            xs = xbuf[:, s, blk * 256:(blk + 1) * 256]
                            k.emit(dve, lambda: V.scalar_tensor_tensor(out=xs, in0=xs, scalar=float(ALPHA),
                                                                       in1=psA[b][:, 0:256], op0=ALU.mult, op1=ALU.add),
                                   reads=[psA_b[b], xbuf_b[s]], writes=[xbuf_b[s]])
                    jobs.append((ld, cp))
                stream(jobs)
                g6 = k.fence()
                for bb in r2_att:
                    bb.guard = g6

                def ln_affine(s):
                    xs = xbuf[:, s, :]
                    for c4 in range(4):
                        k.emit(dve, lambda: V.bn_stats(out=stats[:, s, c4, :], in_=xs[:, c4 * 512:(c4 + 1) * 512]),
                               reads=[xbuf_b[s]], writes=[st_b[s]])
                    k.emit(dve, lambda: V.bn_aggr(out=mv[:, s, :], in_=stats[:, s].rearrange("p c f -> p (c f)")),
                           reads=[st_b[s]], writes=[st_b[s]])
                    rsqrt_small(rstd[:, s:s + 1], mv[:, s, 1:2], 0, [st_b[s]], st_b[s])
                    k.emit(dve, lambda: V.scalar_tensor_tensor(out=xs, in0=xs, scalar=mv[:, s, 0:1], in1=lnc[:, 0, :],
                                                               op0=ALU.subtract, op1=ALU.mult),
                           reads=[xbuf_b[s], st_b[s], lnc_b], writes=[xbuf_b[s]])
                    k.emit(dve, lambda: V.scalar_tensor_tensor(out=xs, in0=xs, scalar=rstd[:, s:s + 1], in1=lnc[:, 1, :],
                                                               op0=ALU.mult, op1=ALU.add),
                           reads=[xbuf_b[s], st_b[s], lnc_b], writes=[xbuf_b[s]])

                for s in range(4):
                    ln_affine(s)
                if stop_after == "B6" and qt == 0:
                    dbg_dump("dbg_mg", r2[:, 0:8192], [128, 8192], BF16, mg_b)
                    dbg_dump("dbg_x1", xbuf[:], [128, 4, D], F32, xbuf_b)
                    break
                for s in range(4):
                    layer_norm_to_xn(s)
                transpose_modulate(32, 48)
                k.dma(sp, lnc[:, 0, :], lnc_d[2], lnslot, writes=[lnc_b])
                k.dma(sp, lnc[:, 1, :], lnc_d[3], lnslot, writes=[lnc_b])

                g8 = k.fence()
                for bb in aT_b:
                    bb.guard = g8
                jobs = []
                for fc in range(44):
                    sg_t, _, sg_b, _ = sets[fc % 2]

                    def ld(fc=fc):
                        return wload_packed(JOB_B8 + fc)

                    def cp(i, fc=fc, sg_t=sg_t, sg_b=sg_b):
                        wv = kview(wr[i], 16, 256)
                        if has_next and fc == 1:
                            prefetch_tables(qt + 1)
                        if has_next and fc in (2, 10, 18, 26):
                            prefetch_ln(qt + 1, (fc - 2) // 8)
                        b = proj_T(wv, wr_b[i], 0, 128, 16, hrhs, hbufs)
                        k.emit(act, lambda: A.activation(out=sg_t[:], in_=psA[b][:], func=AF.Silu),
                               reads=[psA_b[b]], writes=[sg_b])
                        b = proj_T(wv, wr_b[i], 128, 128, 16, hrhs, hbufs)
                        k.emit(dve, lambda: V.tensor_tensor(out=aT[:, fc, :], in0=psA[b][:], in1=sg_t[:], op=ALU.mult),
                               reads=[psA_b[b], sg_b], writes=[aT_b[fc]])
                    jobs.append((ld, cp))
                stream(jobs)
                if has_next:
                    transpose_modulate(0, 16)
                wdn_v = wdn_s.rearrange("(k p) n -> p k n", p=128)
                jobs = []
                for nb in range(4):
                    for g in range(6):
                        nf = 8 if g < 5 else 4

                        def ld(nb=nb, g=g, nf=nf):
                            return wload_bf([(lambda t: kview(t, 8, 512)[:, 0:nf, :],
                                              wdn_v[:, g * 8:g * 8 + nf, nb * 512:(nb + 1) * 512])], deps=fold_deps)

                        def cp(i, nb=nb, g=g, nf=nf):
                            wv = kview(wr[i], 8, 512)
                            for fi in range(nf):
                                fc = g * 8 + fi
                                for s in range(4):
                                    mm(psA[s][:], aT[:, fc, s * 128:(s + 1) * 128], wv[:, fi, :], fc == 0, fc == 43,
                                       [aT_b[fc], wr_b[i]], [psA_b[s]])
                            if g == 5:
                                for s in range(4):
                                    xs = xbuf[:, s, nb * 512:(nb + 1) * 512]
                                    k.emit(dve, lambda: V.scalar_tensor_tensor(out=xs, in0=xs, scalar=float(ALPHA),
                                                                               in1=psA[s][:], op0=ALU.mult, op1=ALU.add),
                                           reads=[psA_b[s], xbuf_b[s]], writes=[xbuf_b[s]])
                        jobs.append((ld, cp))
                stream(jobs)
                k.ps_i = 4
                g9 = k.fence()
                for bb in r1_att:
                    bb.guard = g9
                for s in range(4):
                    ln_affine(s)
                    k.dma(sp, y[r0 + s * 128:r0 + (s + 1) * 128, :], xbuf[:, s, :], yslot[s], reads=[xbuf_b[s]])
                if has_next:
                    load_x_resid(qt + 1)

        for s_ in k.slots:
            if s_.count:
                sp.e.wait_ge(s_.sem, s_.count)
    return nc, dump_specs


def _perm(n):
    h = n // 2
    p = np.empty(n, np.int64)
    p[:h] = 2 * np.arange(h)
    p[h:] = 2 * np.arange(h) + 1
    return p


def _rope_tables(dim):
    quarter = dim // 4
    inv = 10000.0 ** (-np.arange(quarter, dtype=np.float64) / quarter)
    t = np.arange(S)
    ang = np.concatenate([(t // 64)[:, None] * inv[None, :], (t % 64)[:, None] * inv[None, :]], axis=1)
    c, s = np.cos(ang).T, np.sin(ang).T
    cosT = np.concatenate([c, c], axis=0)
    sinT = np.concatenate([-s, s], axis=0)
    return np.ascontiguousarray(cosT, np.float32), np.ascontiguousarray(sinT, np.float32)


def _swap(n):
    m = np.zeros((n, n), np.float32)
    idx = np.arange(n)
    m[(idx + n // 2) % n, idx] = 1.0
    return m


def make_in_maps(x, c, w_ada, b_ada, w_in, b_gates, gqa_q_gain, gqa_k_gain, mla_q_gain, mla_kv_gain,
                 w_mla_uq, w_mla_ukv, w_branch_gqa, w_branch_mla, w_out, ln1_g, ln1_b,
                 w_ffn_gate, w_ffn_up, w_ffn_down, ln2_g, ln2_b):
    f = lambda a: np.ascontiguousarray(np.asarray(a, dtype=np.float32))
    p128, p64 = _perm(128), _perm(64)
    cols = np.arange(6720)
    for h in range(8):
        cols[h * 128:(h + 1) * 128] = h * 128 + p128
    for h in range(2):
        cols[1024 + h * 128:1024 + (h + 1) * 128] = 1024 + h * 128 + p128
    cols[2560:2624] = 2560 + p64
    w_in_p = f(np.asarray(w_in)[0][:, cols])
    ucols = np.arange(1536)
    for h in range(8):
        ucols[h * 192 + 128:(h + 1) * 192] = h * 192 + 128 + p64
    w_uq_p = f(np.asarray(w_mla_uq)[0][:, ucols])
    ba = np.asarray(b_ada, np.float32)[0]
    badaT = f(np.concatenate([ba[0:D], ba[D:2 * D], ba[3 * D:4 * D], ba[4 * D:5 * D]]).reshape(64, 128).T)
    bgbc = f(np.broadcast_to(np.concatenate([ba[2 * D:3 * D], ba[5 * D:6 * D]])[None, :], (128, 2 * D)))
    lnc = f(np.stack([np.broadcast_to(np.asarray(v, np.float32)[0][None, :], (128, D))
                      for v in (ln1_g, ln1_b, ln2_g, ln2_b)]))
    cosg, sing = _rope_tables(128)
    cosm, sinm = _rope_tables(64)
    cst = f(np.stack([np.eye(128, dtype=np.float32), np.ones((128, 128), np.float32), _swap(128)], axis=1))
    common = dict(
        w_ada=f(np.asarray(w_ada)[0]), badaT=badaT, bgbc=bgbc, w_in=w_in_p,
        bgatesT=f(np.asarray(b_gates, np.float32)[0].reshape(32, 128).T),
        gq=f(np.asarray(gqa_q_gain, np.float32)[0][p128][:, None]),
        gk=f(np.asarray(gqa_k_gain, np.float32)[0][p128][:, None]),
        mqg=f(np.asarray(mla_q_gain, np.float32)[0].reshape(4, 128).T),
        mkg=f(np.asarray(mla_kv_gain, np.float32)[0].reshape(4, 128).T),
        w_uq=w_uq_p, w_ukv=f(np.asarray(w_mla_ukv)[0]), w_bg=f(np.asarray(w_branch_gqa)[0]),
        w_bm=f(np.asarray(w_branch_mla)[0]), w_out=f(np.asarray(w_out)[0]), lnc=lnc,
        w_fg=f(np.asarray(w_ffn_gate)[0]), w_fu=f(np.asarray(w_ffn_up)[0]), w_fd=f(np.asarray(w_ffn_down)[0]),
        cst=cst, pmat=_swap(64),
    )
    x = np.asarray(x, np.float32)
    c = np.asarray(c, np.float32)
    maps = []
    for core in range(8):
        b, half = divmod(core, 2)
        order = np.concatenate([np.arange(half * SQ, (half + 1) * SQ), np.arange((1 - half) * SQ, (2 - half) * SQ)])
        m = dict(common)
        m["xa"] = f(x[b][order])
        m["cT"] = f(c[b].reshape(16, 128).T)
        m["cosg"] = f(cosg[:, order])
        m["sing"] = f(sing[:, order])
        m["cosm"] = f(cosm[:, order])
        m["sinm"] = f(sinm[:, order])
        maps.append(m)
    return maps


_NC_CACHE = {}


def kernel(**inputs):
    if "nc" not in _NC_CACHE:
        _NC_CACHE["nc"] = build_program()[0]
    nc = _NC_CACHE["nc"]
    in_maps = make_in_maps(**inputs)
    res = run_bass_kernel_spmd(nc, in_maps, core_ids=list(range(8)))
    out = np.empty((4, S, D), np.float32)
    for core in range(8):
        b, half = divmod(core, 2)
        out[b, half * SQ:(half + 1) * SQ] = res.results[core]["y"]
    return out
```

```python
from contextlib import ExitStack
import numpy as np
import concourse.bass as bass
import concourse.mybir as mybir
from concourse.bass_utils import run_bass_kernel_spmd

F32 = mybir.dt.float32
BF16 = mybir.dt.bfloat16
AF = mybir.ActivationFunctionType
ALU = mybir.AluOpType

D = 2048
S = 8192
SQ = 4096
DFF = 5632
LN_EPS = 1e-5
RMS_EPS = 1e-6
ALPHA = 2.0 ** 0.25
TQ = 512
NQT = SQ // TQ
NKT = S // TQ
KP = 1024
NPIECE = S // KP
WSLOT = 4096


class Buf:
    __slots__ = ("name", "w", "r", "guard")

    def __init__(self, name):
        self.name = name
        self.w = None
        self.r = {}
        self.guard = ()


class Eng:
    def __init__(self, name, e, sem, is_pe=False):
        self.name, self.e, self.sem, self.is_pe = name, e, sem, is_pe
        self.n = 0
        self.seen = {}


class Slot:
    def __init__(self, sem):
        self.sem = sem
        self.count = 0


class K:
    def __init__(self, nc, es):
        self.nc = nc
        self.es = es
        self.nsem = 0
        self.pe = Eng("pe", nc.tensor, self.newsem("pe"), is_pe=True)
        self.act = Eng("act", nc.scalar, self.newsem("act"))
        self.dve = Eng("dve", nc.vector, self.newsem("dve"))
        self.pool = Eng("pool", nc.gpsimd, self.newsem("pool"))
        self.sp = Eng("sp", nc.sync, self.newsem("sp"))
        self.slots = []
        self.ps_i = 0

    def newsem(self, name):
        self.nsem += 1
        return self.es.enter_context(self.nc.semaphore(f"s{self.nsem}_{name}"))

    def newslot(self, name):
        s = Slot(self.newsem(name))
        self.slots.append(s)
        return s

    def _gather(self, eng, reads, writes, deps):
        need = {}

        def add(tok):
            if tok is None:
                return
            s, v = tok
            if s is eng.sem:
                if eng.is_pe or v <= eng.n - 3:
                    return
            if eng.seen.get(s, 0) >= v:
                return
            if need.get(s, 0) < v:
                need[s] = v

        for b in reads:
            add(b.w)
            for t in b.guard:
                add(t)
        for b in writes:
            add(b.w)
            for t in b.r.values():
                add(t)
            for t in b.guard:
                add(t)
        for t in deps:
            add(t)
        return list(need.items())

    def _apply_waits(self, eng, items, fn):
        for s, v in items[:-1]:
            eng.e.wait_ge(s, v)
            eng.seen[s] = v
        inst = fn()
        if items:
            s, v = items[-1]
            inst._wait_ge(s, v)
            eng.seen[s] = v
        return inst

    def emit(self, eng, fn, reads=(), writes=(), deps=()):
        items = self._gather(eng, reads, writes, deps)
        inst = self._apply_waits(eng, items, fn)
        eng.n += 1
        inst.then_inc(eng.sem, 1)
        tok = (eng.sem, eng.n)
        for b in reads:
            b.r[eng.sem] = tok
        for b in writes:
            b.w = tok
            b.r = {}
        return tok

    def dma(self, q, out, in_, slot, reads=(), writes=(), deps=()):
        items = self._gather(q, reads, writes, deps)
        inst = self._apply_waits(q, items, lambda: q.e.dma_start(out=out, in_=in_))
        slot.count += 16
        inst.then_inc(slot.sem, 16)
        tok = (slot.sem, slot.count)
        for b in reads:
            b.r[slot.sem] = tok
        for b in writes:
            b.w = tok
            b.r = {}
        return tok

    def fence(self):
        return tuple((e.sem, e.n) for e in (self.pe, self.act, self.dve, self.pool) if e.n > 0)


def build_program(nqt=NQT, nkt=NKT, stop_after=None, dumps=()):
    nc = bass.Bass("TRN2", target_bir_lowering=False)
    dumps = set(dumps)
    dump_specs = {}

    def din(name, shape, dt=F32):
        return nc.dram_tensor(name, list(shape), dt, kind="ExternalInput").ap()

    def dscr(name, shape, dt=BF16):
        kind = "ExternalOutput" if name in dumps else "Internal"
        if name in dumps:
            dump_specs[name] = (tuple(shape), dt)
        return nc.dram_tensor(name, list(shape), dt, kind=kind).ap()

    xa = din("xa", [S, D])
    xq = xa
    cT_d = din("cT", [128, 16])
    w_ada = din("w_ada", [D, 6 * D])
    badaT_d = din("badaT", [128, 64])
    bgbc_d = din("bgbc", [128, 2 * D])
    w_in = din("w_in", [D, 6720])
    bgT_d = din("bgatesT", [128, 32])
    gq_d = din("gq", [128, 1])
    gk_d = din("gk", [128, 1])
    mqg_d = din("mqg", [128, 4])
    mkg_d = din("mkg", [128, 4])
    w_uq = din("w_uq", [512, 1536])
    w_ukv = din("w_ukv", [512, 2048])
    w_bg = din("w_bg", [1024, D])
    w_bm = din("w_bm", [1024, D])
    w_out = din("w_out", [D, D])
    lnc_d = din("lnc", [4, 128, D])
    w_fg = din("w_fg", [D, DFF])
    w_fu = din("w_fu", [D, DFF])
    w_fd = din("w_fd", [DFF, D])
    cosg_d = din("cosg", [128, S])
    sing_d = din("sing", [128, S])
    cosm_d = din("cosm", [64, S])
    sinm_d = din("sinm", [64, S])
    cosgq_d, singq_d, cosmq_d, sinmq_d = cosg_d, sing_d, cosm_d, sinm_d
    cst_d = din("cst", [128, 3, 128])
    pm_d = din("pmat", [64, 64])

    y = nc.dram_tensor("y", [SQ, D], F32, kind="ExternalOutput").ap()

    kg_s = dscr("kg_s", [2, 128, S])
    vg_s = dscr("vg_s", [2, 128, S // 128, 128])
    kn_s = dscr("kn_s", [8, 128, S])
    kr_s = dscr("kr_s", [64, S])
    vm_s = dscr("vm_s", [8, 128, S // 128, 128])
    wout_s = dscr("wout_s", [D, D])
    wdn_s = dscr("wdn_s", [DFF, D])
    NJOB = 76
    wscr = dscr("wscr", [NJOB, 128, WSLOT])

    es = ExitStack()
    with es:
        k = K(nc, es)
        pe, act, dve, pool, sp = k.pe, k.act, k.dve, k.pool, k.sp
        T, V, A = nc.tensor, nc.vector, nc.scalar

        def sb(name, shape, dt, stack=es):
            return stack.enter_context(nc.sbuf_tensor("sb_" + name, list(shape), dt))

        psA = [es.enter_context(nc.psum_tensor(f"psA{i}", [128, 512], F32)) for i in range(8)]
        psA_b = [Buf(f"psA{i}") for i in range(8)]
        PS_ROT = (0, 1, 2, 3, 4, 6, 7)

        def next_ps():
            i = PS_ROT[k.ps_i % len(PS_ROT)]
            k.ps_i += 1
            return i

        cst = sb("cst", [128, 3, 128], BF16)
        pmat = sb("pmat", [64, 64], BF16)
        ident, ones, pg = cst[:, 0, :], cst[:, 1, :], cst[:, 2, :]
        cst_b, pm_b = Buf("cst"), Buf("pm")
        cT = sb("cT", [128, 16], F32)
        cact = sb("cact", [128, 16], BF16)
        cact_b = Buf("cact")
        modT = sb("modT", [128, 64], F32)
        modT_b = Buf("modT")
        badaT = sb("badaT", [128, 64], F32)
        bgT = sb("bgT", [128, 32], F32)
        gq = sb("gq", [128, 1], F32)
        gk = sb("gk", [128, 1], F32)
        mqg = sb("mqg", [128, 4], F32)
        mkg = sb("mkg", [128, 4], F32)
        small_b = Buf("small")
        epsc = sb("epsc", [128, 3], F32)
        epsc_b = Buf("epsc")
        xbuf = sb("xbuf", [128, 4, D], F32)
        xbuf_b = [Buf(f"xbuf{s}") for s in range(4)]
        xn = sb("xn", [128, 4, D], BF16)
        xn_b = [Buf(f"xn{s}") for s in range(4)]
        hT = sb("hT", [128, 16, TQ], BF16)
        hT_b = [Buf(f"hT{c}") for c in range(16)]
        NWS = 2
        wr = [sb(f"wr{i}", [128, WSLOT], BF16) for i in range(NWS)]
        wr_b = [Buf(f"wr{i}") for i in range(NWS)]
        wr_slot = [k.newslot(f"wr{i}") for i in range(NWS)]
        wr_i = [0]
        stats = sb("stats", [128, 4, 4, 6], F32)
        mv = sb("mv", [128, 4, 2], F32)
        rstd = sb("rstd", [128, 4], F32)
        nmr = sb("nmr", [128, 4], F32)
        st_b = [Buf(f"st{s}") for s in range(4)]
        tabg = sb("tabg", [128, 2, TQ], F32)
        tabm = sb("tabm", [64, 2, TQ], F32)
        tabg_b, tabm_b = Buf("tabg"), Buf("tabm")
        tab_slot = k.newslot("tab")
        class TSet:
            pass

        tsets = []
        for i in range(2):
            o = TSet()
            o.xg = sb(f"tmpxg{i}", [128, TQ], BF16)
            o.sq = sb(f"tmpsq{i}", [128, TQ], BF16)
            o.rs = sb(f"tmprs{i}", [128, TQ], F32)
            o.t1 = sb(f"tmpa{i}", [128, TQ], F32)
            o.t2 = sb(f"tmpb{i}", [128, TQ], F32)
            o.xg_b, o.sq_b, o.rs_b, o.t1_b, o.t2_b = (Buf(f"{n}{i}") for n in ("xg", "sq", "rs", "t1", "t2"))
            tsets.append(o)
        ts_i = [0]

        def next_tset():
            o = tsets[ts_i[0] % 2]
            ts_i[0] += 1
            return o

        rq_t = sb("rq_t", [128, TQ], F32)
        rq_b = Buf("rq")
        cslot = k.newslot("const")
        xslot = [k.newslot(f"x{s}") for s in range(4)]
        yslot = [k.newslot(f"y{s}") for s in range(4)]

        for ci, cv in enumerate((LN_EPS, 128.0 * RMS_EPS, 512.0 * RMS_EPS)):
            k.emit(dve, lambda: V.memset(epsc[:, ci:ci + 1], float(cv)), writes=[epsc_b])

        def rsqrt_small(out_ap, in_ap, eps_col, rbufs, wbuf):
            k.emit(act, lambda: A.activation(out=out_ap, in_=in_ap, func=AF.Sqrt, bias=epsc[:, eps_col:eps_col + 1]),
                   reads=list(rbufs) + [epsc_b], writes=[wbuf])
            k.emit(dve, lambda: V.reciprocal(out=out_ap, in_=out_ap), reads=[wbuf], writes=[wbuf])

        def wload(view_src_pairs):
            i = wr_i[0] % NWS
            wr_i[0] += 1
            for view_fn, src in view_src_pairs:
                k.dma(pool, view_fn(wr[i]), src, wr_slot[i], writes=[wr_b[i]])
            return i

        def wload_bf(view_src_pairs, deps=()):
            i = wr_i[0] % NWS
            wr_i[0] += 1
            for view_fn, src in view_src_pairs:
                k.dma(sp, view_fn(wr[i]), src, wr_slot[i], writes=[wr_b[i]], deps=deps)
            return i

        def stream(jobs, lookahead=NWS):
            n = len(jobs)
            slots = [None] * n
            for i in range(min(lookahead, n)):
                slots[i] = jobs[i][0]()
            for i in range(n):
                jobs[i][1](slots[i])
                if i + lookahead < n:
                    slots[i + lookahead] = jobs[i + lookahead][0]()

        def kview(t, kc, n):
            return t[:, 0:kc * n].rearrange("p (k n) -> p k n", k=kc)

        def wsrc(w, c0, n):
            return w.rearrange("(k p) n -> p k n", p=128)[:, :, c0:c0 + n]

        dbg_slot = k.newslot("dbg")

        def dbg_dump(name, ap, shape, dt, bufs):
            dmp = nc.dram_tensor(name, list(shape), dt, kind="ExternalOutput").ap()
            dump_specs[name] = (tuple(shape), dt)
            k.dma(sp, dmp, ap, dbg_slot, reads=bufs)

        def mm(out, lhsT, rhs, start, stop, reads, writes):
            return k.emit(pe, lambda: T.matmul(out, lhsT, rhs, start=start, stop=stop),
                          reads=reads, writes=writes)

        k.dma(pool, cst[:], cst_d, cslot, writes=[cst_b])
        k.dma(pool, pmat[:], pm_d, cslot, writes=[pm_b])
        for t_sb, t_d in ((cT, cT_d), (badaT, badaT_d), (bgT, bgT_d), (gq, gq_d), (gk, gk_d),
                          (mqg, mqg_d), (mkg, mkg_d)):
            k.dma(sp, t_sb[:], t_d, cslot, writes=[small_b])
        k.emit(act, lambda: A.activation(out=cact[:], in_=cT[:], func=AF.Silu), reads=[small_b], writes=[cact_b])
        k.emit(dve, lambda: V.tensor_scalar(out=mqg[:], in0=mqg[:], scalar1=float(512.0 ** 0.5), scalar2=None,
                                            op0=ALU.mult), reads=[small_b], writes=[small_b])
        k.emit(dve, lambda: V.tensor_scalar(out=mkg[:], in0=mkg[:], scalar1=float(512.0 ** 0.5), scalar2=None,
                                            op0=ALU.mult), reads=[small_b], writes=[small_b])

        p0 = ExitStack()
        with p0:
            crep = sb("crep", [128, 16, 128], BF16, p0)
            crep_b = Buf("crep")
            gbc = sb("gbc", [128, 2 * D], F32, p0)
            gbc_b = Buf("gbc")
            bgbc = sb("bgbc", [128, 2 * D], F32, p0)
            bgbc_b = Buf("bgbc")
            fst = [sb(f"fst{i}", [128, D], F32, p0) for i in range(2)]
            fbf = [sb(f"fbf{i}", [128, D], BF16, p0) for i in range(2)]
            fst_b = [Buf(f"fst{i}") for i in range(2)]
            fbf_b = [Buf(f"fbf{i}") for i in range(2)]
            fslot = [k.newslot(f"fst{i}") for i in range(2)]
            fsslot = [k.newslot(f"fbf{i}") for i in range(2)]

            k.dma(sp, bgbc[:], bgbc_d, cslot, writes=[bgbc_b])
            for kc in range(16):
                k.emit(dve, lambda: V.tensor_copy(out=crep[:, kc, :], in_=cact[:, kc:kc + 1].to_broadcast([128, 128])),
                       reads=[cact_b], writes=[crep_b])

            col_starts = [0, D, 3 * D, 4 * D]
            mod_ps = 5
            jobs = []
            for blk in range(32):
                seg, off = divmod(blk * 256, D)
                c0 = col_starts[seg] + off

                def ld(c0=c0):
                    return wload([(lambda t: kview(t, 16, 256), wsrc(w_ada, c0, 256))])

                def cp(i, blk=blk):
                    wv = kview(wr[i], 16, 256)
                    for cc in range(2):
                        j = blk * 2 + cc
                        for kc in range(16):
                            mm(psA[mod_ps][:, j:j + 1], wv[:, kc, cc * 128:(cc + 1) * 128], cact[:, kc:kc + 1],
                               kc == 0, kc == 15, [wr_b[i], cact_b], [psA_b[mod_ps]])
                jobs.append((ld, cp))
            stream(jobs)
            k.emit(dve, lambda: V.tensor_tensor(out=modT[:], in0=psA[mod_ps][:, 0:64], in1=badaT[:], op=ALU.add),
                   reads=[psA_b[mod_ps], small_b], writes=[modT_b])
            for lo in (16, 48):
                k.emit(dve, lambda: V.tensor_scalar(out=modT[:, lo:lo + 16], in0=modT[:, lo:lo + 16], scalar1=1.0,
                                                    scalar2=None, op0=ALU.add), reads=[modT_b], writes=[modT_b])
            jobs = []
            for blk in range(16):
                seg, off = divmod(blk * 256, D)
                c0 = (2 * D if seg == 0 else 5 * D) + off

                def ld(c0=c0):
                    return wload([(lambda t: kview(t, 16, 256), wsrc(w_ada, c0, 256))])

                def cp(i, blk=blk):
                    wv = kview(wr[i], 16, 256)
                    b = next_ps()
                    for kc in range(16):
                        mm(psA[b][:, 0:256], crep[:, kc, :], wv[:, kc, :], kc == 0, kc == 15,
                           [wr_b[i], crep_b], [psA_b[b]])
                    k.emit(dve, lambda: V.tensor_tensor(out=gbc[:, blk * 256:(blk + 1) * 256], in0=psA[b][:, 0:256],
                                                        in1=bgbc[:, blk * 256:(blk + 1) * 256], op=ALU.add),
                           reads=[psA_b[b], bgbc_b], writes=[gbc_b])
                jobs.append((ld, cp))
            stream(jobs)
            fold_toks = []
            nfold = 0
            for (wsrc_d, wdst, nrb, goff) in ((w_out, wout_s, 16, 0), (w_fd, wdn_s, 44, D)):
                for rb in range(nrb):
                    i = nfold % 2
                    nfold += 1
                    k.dma(sp, fst[i][:], wsrc_d[rb * 128:(rb + 1) * 128, :], fslot[i], writes=[fst_b[i]])
                    k.emit(dve, lambda: V.tensor_tensor(out=fbf[i][:], in0=fst[i][:], in1=gbc[:, goff:goff + D],
                                                        op=ALU.mult),
                           reads=[fst_b[i], gbc_b], writes=[fbf_b[i]])
                    fold_toks.append(k.dma(sp, wdst[rb * 128:(rb + 1) * 128, :], fbf[i][:], fsslot[i],
                                           reads=[fbf_b[i]]))
            fold_deps = tuple({t[0]: t for t in fold_toks}.values())
            p0_fence = k.fence()
        p0_guard = p0_fence + fold_deps

        if stop_after == "p0":
            dmp = nc.dram_tensor("dbg_modT", [128, 64], F32, kind="ExternalOutput").ap()
            dump_specs["dbg_modT"] = ((128, 64), F32)
            tk = k.dma(sp, dmp, modT[:], cslot, reads=[modT_b], deps=p0_guard)
            sp.e.wait_ge(cslot.sem, cslot.count)
            return nc, dump_specs

        def layer_norm_to_xn(s, src=None, src_b=None):
            xs = xbuf[:, s, :] if src is None else src
            src_b = xbuf_b[s] if src_b is None else src_b
            for c4 in range(4):
                k.emit(dve, lambda: V.bn_stats(out=stats[:, s, c4, :], in_=xs[:, c4 * 512:(c4 + 1) * 512]),
                       reads=[src_b], writes=[st_b[s]])
            k.emit(dve, lambda: V.bn_aggr(out=mv[:, s, :], in_=stats[:, s].rearrange("p c f -> p (c f)")),
                   reads=[st_b[s]], writes=[st_b[s]])
            rsqrt_small(rstd[:, s:s + 1], mv[:, s, 1:2], 0, [st_b[s]], st_b[s])
            k.emit(dve, lambda: V.scalar_tensor_tensor(out=nmr[:, s:s + 1], in0=mv[:, s, 0:1], scalar=-1.0,
                                                       in1=rstd[:, s:s + 1], op0=ALU.mult, op1=ALU.mult),
                   reads=[st_b[s]], writes=[st_b[s]])
            k.emit(dve, lambda: V.tensor_scalar(out=xn[:, s, :], in0=xs, scalar1=rstd[:, s:s + 1],
                                                scalar2=nmr[:, s:s + 1], op0=ALU.mult, op1=ALU.add),
                   reads=[src_b, st_b[s]], writes=[xn_b[s]])

        tr_i = [0]

        def transpose_modulate(shift_col, scale_col, hT=hT, hT_b=hT_b):
            for c in range(16):
                b = next_ps()
                dst = psA[b][:]
                for s in range(4):
                    k.emit(pe, lambda: T.matmul(psA[b][:, s * 128:(s + 1) * 128], xn[:, s, c * 128:(c + 1) * 128], ident,
                                                start=True, stop=True),
                           reads=[xn_b[s], cst_b], writes=[psA_b[b]])
                sc = modT[:, scale_col + c:scale_col + c + 1]
                sh = modT[:, shift_col + c:shift_col + c + 1]
                if c % 2 == 0:
                    k.emit(act, lambda: A.activation(out=hT[:, c, :], in_=dst, func=AF.Identity, bias=sh, scale=sc),
                           reads=[psA_b[b], modT_b], writes=[hT_b[c]])
                else:
                    k.emit(dve, lambda: V.tensor_scalar(out=hT[:, c, :], in0=dst, scalar1=sc, scalar2=sh,
                                                        op0=ALU.mult, op1=ALU.add),
                           reads=[psA_b[b], modT_b], writes=[hT_b[c]])

        deferred = []

        def flush_deferred():
            while deferred:
                deferred.pop(0)()

        def proj_T(wview, wbuf, ncol_lo, m, kchunks, rhs_fn, rhs_bufs):
            b = next_ps()
            for kc in range(kchunks):
                mm(psA[b][0:m, :], wview[:, kc, ncol_lo:ncol_lo + m], rhs_fn(kc), kc == 0, kc == kchunks - 1,
                   [wbuf] + rhs_bufs(kc), [psA_b[b]])
            flush_deferred()
            return b

        def rope_rms_finalize(b, gcol, tab, tab_b, col0, out_ap, out_b):
            ps = psA[b]
            o = next_tset()
            k.emit(act, lambda: A.activation(out=o.xg[:], in_=ps[:], func=AF.Identity, scale=gcol),
                   reads=[psA_b[b], small_b], writes=[o.xg_b])
            k.emit(act, lambda: A.activation(out=o.sq[:], in_=ps[:], func=AF.Square),
                   reads=[psA_b[b]], writes=[o.sq_b])
            def part2():
                b1 = next_ps()
                mm(psA[b1][:], ones, o.sq[:], True, True, [cst_b, o.sq_b], [psA_b[b1]])
                b2 = next_ps()
                mm(psA[b2][:], pg, o.xg[:], True, True, [cst_b, o.xg_b], [psA_b[b2]])
                rsqrt_small(o.rs[:], psA[b1][:], 1, [psA_b[b1]], o.rs_b)
                k.emit(dve, lambda: V.tensor_tensor(out=o.t1[:], in0=o.xg[:], in1=tab[:, 0, :], op=ALU.mult),
                       reads=[o.xg_b, tab_b], writes=[o.t1_b])
                k.emit(dve, lambda: V.tensor_tensor(out=o.t2[:], in0=psA[b2][:], in1=tab[:, 1, :], op=ALU.mult),
                       reads=[psA_b[b2], tab_b], writes=[o.t2_b])
                k.emit(dve, lambda: V.tensor_tensor(out=o.t1[:], in0=o.t1[:], in1=o.t2[:], op=ALU.add),
                       reads=[o.t1_b, o.t2_b], writes=[o.t1_b])
                k.emit(dve, lambda: V.scalar_tensor_tensor(out=out_ap, in0=o.t1[:], scalar=float(128.0 ** 0.5),
                                                           in1=o.rs[:], op0=ALU.mult, op1=ALU.mult),
                       reads=[o.t1_b, o.rs_b], writes=[out_b])
            deferred.append(part2)

        def rope64_finalize(b, scale_bc, scale_b, tab, tab_b, out_ap, out_b):
            ps = psA[b]
            o = next_tset()
            k.emit(act, lambda: A.activation(out=o.xg[0:64, :], in_=ps[0:64, :], func=AF.Identity),
                   reads=[psA_b[b]], writes=[o.xg_b])
            def part2():
                b2 = next_ps()
                mm(psA[b2][0:64, :], pmat[:], o.xg[0:64, :], True, True, [pm_b, o.xg_b], [psA_b[b2]])
                k.emit(dve, lambda: V.tensor_tensor(out=o.t1[0:64, :], in0=o.xg[0:64, :], in1=tab[:, 0, :], op=ALU.mult),
                       reads=[o.xg_b, tab_b], writes=[o.t1_b])
                k.emit(dve, lambda: V.tensor_tensor(out=o.t2[0:64, :], in0=psA[b2][0:64, :], in1=tab[:, 1, :], op=ALU.mult),
                       reads=[psA_b[b2], tab_b], writes=[o.t2_b])
                if scale_bc is None:
                    k.emit(dve, lambda: V.tensor_tensor(out=out_ap, in0=o.t1[0:64, :], in1=o.t2[0:64, :], op=ALU.add),
                           reads=[o.t1_b, o.t2_b], writes=[out_b])
                else:
                    k.emit(dve, lambda: V.tensor_tensor(out=o.t1[0:64, :], in0=o.t1[0:64, :], in1=o.t2[0:64, :], op=ALU.add),
                           reads=[o.t1_b, o.t2_b], writes=[o.t1_b])
                    k.emit(dve, lambda: V.tensor_tensor(out=out_ap, in0=o.t1[0:64, :], in1=scale_bc[0:64, :], op=ALU.mult),
                           reads=[o.t1_b, scale_b], writes=[out_b])
            deferred.append(part2)

        def latent_chunk(b, j, gains, lat, lat_b):
            o = next_tset()
            k.emit(act, lambda: A.activation(out=lat[:, j, :], in_=psA[b][:], func=AF.Identity,
                                             scale=gains[:, j:j + 1]),
                   reads=[psA_b[b], small_b], writes=[lat_b[j]])
            k.emit(act, lambda: A.activation(out=o.sq[:], in_=psA[b][:], func=AF.Square),
                   reads=[psA_b[b]], writes=[o.sq_b])
            deferred.append(lambda: mm(psA[5][:], ones, o.sq[:], j == 0, j == 3, [cst_b, o.sq_b], [psA_b[5]]))

        def latent_finish(lat, lat_b):
            flush_deferred()
            rsqrt_small(rq_t[:], psA[5][:], 2, [psA_b[5]], rq_b)
            for j in range(4):
                k.emit(dve, lambda: V.tensor_tensor(out=lat[:, j, :], in0=lat[:, j, :], in1=rq_t[:], op=ALU.mult),
                       reads=[lat_b[j], rq_b], writes=[lat_b[j]])

        def v16(t):
            return kview(t, 16, 256)

        job_specs = []
        for blk in range(4):
            job_specs.append([(v16, wsrc(w_in, blk * 256, 256))])
        for blk in range(2):
            job_specs.append([(v16, wsrc(w_in, 1536 + blk * 256, 256))])
        for blk in range(2):
            job_specs.append([(lambda t: kview(t, 4, 768), wsrc(w_uq, blk * 768, 768))])
        for p in range(8):
            job_specs.append([(v16, wsrc(w_in, 2624 + p * 256, 256))])
            job_specs.append([(v16, wsrc(w_in, 2624 + D + p * 256, 256))])
            job_specs.append([(lambda t: kview(t, 16, 256)[:, 0:8, :], wsrc(w_bg, p * 256, 256)),
                              (lambda t: kview(t, 16, 256)[:, 8:16, :], wsrc(w_bm, p * 256, 256))])
        for fc in range(44):
            job_specs.append([(lambda t: v16(t)[:, :, 0:128], wsrc(w_fg, fc * 128, 128)),
                              (lambda t: v16(t)[:, :, 128:256], wsrc(w_fu, fc * 128, 128))])
        assert len(job_specs) == NJOB
        JOB_B2, JOB_B5, JOB_B8 = 0, 8, 32
        pk_slot = [k.newslot(f"pk{i}") for i in range(NWS)]
        prep_state = {"next": 0, "pending": None, "toks": {}}

        def prep_jobs(n):
            for _ in range(n):
                j = prep_state["next"]
                if j < NJOB:
                    i = wload(job_specs[j])
                    prep_state["next"] = j + 1
                else:
                    i = None
                pend = prep_state["pending"]
                if pend is not None:
                    pj, pi = pend
                    prep_state["toks"][pi] = k.dma(pool, wscr[pj], wr[pi][:], pk_slot[pi], reads=[wr_b[pi]])
                prep_state["pending"] = (j, i) if i is not None else None

        def prep_flush():
            pend = prep_state["pending"]
            if pend is not None:
                pj, pi = pend
                prep_state["toks"][pi] = k.dma(pool, wscr[pj], wr[pi][:], pk_slot[pi], reads=[wr_b[pi]])
                prep_state["pending"] = None

        def wload_packed(j, n=WSLOT):
            i = wr_i[0] % NWS
            wr_i[0] += 1
            k.dma(pool, wr[i][:, 0:n], wscr[j, :, 0:n], wr_slot[i], writes=[wr_b[i]],
                  deps=tuple(prep_state["toks"].values()))
            return i

        pA = ExitStack()
        with pA:
            wkv = sb("wkv", [128, 16, 1088], BF16, pA)
            wkv_b = Buf("wkv")
            wuk = sb("wuk", [128, 4, 2048], BF16, pA)
            wuk_b = Buf("wuk")
            kvl = sb("kvl", [128, 4, TQ], BF16, pA)
            kvl_b = [Buf(f"kvl{j}") for j in range(4)]
            kgst = sb("kgst", [128, 2, TQ], BF16, pA)
            vgst = sb("vgst", [128, 4, 256], BF16, pA)
            knst = sb("knst", [128, 8, TQ], BF16, pA)
            krst = sb("krst", [64, TQ], BF16, pA)
            vmst = sb("vmst", [128, 4, 1024], BF16, pA)
            kgst_b, vgst_b, knst_b, krst_b, vmst_b = (Buf(n) for n in ("kgst", "vgst", "knst", "krst", "vmst"))
            for bb in kvl_b + [wkv_b, wuk_b, kgst_b, vgst_b, knst_b, krst_b, vmst_b]:
                bb.guard = p0_guard
            stslot = {n: k.newslot(n) for n in ("kg", "vg", "kn", "kr", "vm")}
            scr_toks = {}

            wv_in = w_in.rearrange("(k p) n -> p k n", p=128)
            k.dma(pool, wkv[:, :, 0:512], wv_in[:, :, 1024:1536], cslot, writes=[wkv_b])
            k.dma(pool, wkv[:, :, 512:1088], wv_in[:, :, 2048:2624], cslot, writes=[wkv_b])
            k.dma(pool, wuk[:], w_ukv.rearrange("(k p) n -> p k n", p=128), cslot, writes=[wuk_b])
            wuk_v = wuk[:].rearrange("p k (h t d) -> p k h t d", h=8, t=2)
            hT2 = sb("hT2", [128, 16, TQ], BF16, pA)
            hT2_b = [Buf(f"hT2_{c}") for c in range(16)]
            for bb in hT2_b:
                bb.guard = p0_guard
            hTs = ((hT, hT_b), (hT2, hT2_b))

            def pa_loads(t):
                rr = t * TQ
                for s in range(4):
                    k.dma(sp, xbuf[:, s, :], xa[rr + s * 128:rr + (s + 1) * 128, :], xslot[s], writes=[xbuf_b[s]])

            def pa_front(t):
                pa_loads(t)
                for s in range(4):
                    layer_norm_to_xn(s)

            def pa_tables(t):
                rr = t * TQ
                k.dma(sp, tabg[:, 0, :], cosg_d[:, rr:rr + TQ], tab_slot, writes=[tabg_b])
                k.dma(sp, tabg[:, 1, :], sing_d[:, rr:rr + TQ], tab_slot, writes=[tabg_b])
                k.dma(sp, tabm[:, 0, :], cosm_d[:, rr:rr + TQ], tab_slot, writes=[tabm_b])
                k.dma(sp, tabm[:, 1, :], sinm_d[:, rr:rr + TQ], tab_slot, writes=[tabm_b])

            pa_front(0)
            pa_tables(0)
            transpose_modulate(0, 16, *hTs[0])
            for t in range(nkt):
                r0 = t * TQ
                if prep_state["next"] < JOB_B8:
                    prep_jobs(min(3 if nkt == NKT else JOB_B8, JOB_B8 - prep_state["next"]))
                hTc, hTc_b = hTs[t % 2]
                hrhs = (lambda kc, hTc=hTc: hTc[:, kc, :])
                hbufs = (lambda kc, hTc_b=hTc_b: [hTc_b[kc]])
                nxt = t + 1 < nkt
                if nxt:
                    pa_loads(t + 1)
                for h in range(2):
                    b = proj_T(wkv, wkv_b, h * 128, 128, 16, hrhs, hbufs)
                    rope_rms_finalize(b, gk[:, 0:1], tabg, tabg_b, 0, kgst[:, h, :], kgst_b)
                flush_deferred()
                for h in range(2):
                    scr_toks["kg"] = k.dma(sp, kg_s[h, :, r0:r0 + TQ], kgst[:, h, :], stslot["kg"], reads=[kgst_b])
                if nxt:
                    layer_norm_to_xn(0)
                for s in range(4):
                    b = next_ps()
                    for kc in range(16):
                        mm(psA[b][:, 0:256], hTc[:, kc, s * 128:(s + 1) * 128], wkv[:, kc, 256:512], kc == 0, kc == 15,
                           [hTc_b[kc], wkv_b], [psA_b[b]])
                    k.emit(act, lambda: A.activation(out=vgst[:, s, :], in_=psA[b][:, 0:256], func=AF.Identity),
                           reads=[psA_b[b]], writes=[vgst_b])
                for h in range(2):
                    scr_toks["vg"] = k.dma(sp, vg_s[h, :, t * 4:(t + 1) * 4, :], vgst[:, :, h * 128:(h + 1) * 128],
                                           stslot["vg"], reads=[vgst_b])
                if nxt:
                    layer_norm_to_xn(1)
                for j in range(4):
                    b = proj_T(wkv, wkv_b, 512 + j * 128, 128, 16, hrhs, hbufs)
                    latent_chunk(b, j, mkg, kvl, kvl_b)
                latent_finish(kvl, kvl_b)
                if nxt:
                    layer_norm_to_xn(2)
                b = proj_T(wkv, wkv_b, 1024, 64, 16, hrhs, hbufs)
                rope64_finalize(b, None, None, tabm, tabm_b, krst[:], krst_b)
                flush_deferred()
                scr_toks["kr"] = k.dma(sp, kr_s[:, r0:r0 + TQ], krst[:], stslot["kr"], reads=[krst_b])
                flush_deferred()
                if nxt:
                    layer_norm_to_xn(3)
                for h in range(8):
                    b = next_ps()
                    for j in range(4):
                        mm(psA[b][:], wuk_v[:, j, h, 0, :], kvl[:, j, :], j == 0, j == 3, [wuk_b, kvl_b[j]], [psA_b[b]])
                    if h % 2 == 0:
                        k.emit(act, lambda: A.activation(out=knst[:, h, :], in_=psA[b][:], func=AF.Identity),
                               reads=[psA_b[b]], writes=[knst_b])
                    else:
                        k.emit(dve, lambda: V.tensor_copy(out=knst[:, h, :], in_=psA[b][:]),
                               reads=[psA_b[b]], writes=[knst_b])
                for h in range(8):
                    scr_toks["kn"] = k.dma(sp, kn_s[h, :, r0:r0 + TQ], knst[:, h, :], stslot["kn"], reads=[knst_b])
                for s in range(4):
                    for half in range(2):
                        b = next_ps()
                        for j in range(4):
                            mm(psA[b][:], kvl[:, j, s * 128:(s + 1) * 128], wuk_v[:, j, half * 4:(half + 1) * 4, 1, :],
                               j == 0, j == 3, [kvl_b[j], wuk_b], [psA_b[b]])
                        if half == 0:
                            k.emit(act, lambda: A.activation(out=vmst[:, s, 0:512], in_=psA[b][:], func=AF.Identity),
                                   reads=[psA_b[b]], writes=[vmst_b])
                        else:
                            k.emit(dve, lambda: V.tensor_copy(out=vmst[:, s, 512:1024], in_=psA[b][:]),
                                   reads=[psA_b[b]], writes=[vmst_b])
                for h in range(8):
                    scr_toks["vm"] = k.dma(sp, vm_s[h, :, t * 4:(t + 1) * 4, :], vmst[:, :, h * 128:(h + 1) * 128],
                                           stslot["vm"], reads=[vmst_b])
                if t + 1 < nkt:
                    pa_tables(t + 1)
                    transpose_modulate(0, 16, *hTs[(t + 1) % 2])
            while prep_state["next"] < JOB_B8:
                prep_jobs(1)
            prep_flush()
            pA_fence = k.fence()
        scr_deps = tuple(scr_toks.values())
        pA_guard = pA_fence + scr_deps + p0_guard

        if stop_after == "pA":
            for s_ in k.slots:
                if s_.count:
                    sp.e.wait_ge(s_.sem, s_.count)
            return nc, dump_specs

        pB = ExitStack()
        with pB:
            r1 = sb("r1", [128, 22528], BF16, pB)
            r2 = sb("r2", [128, 14336], BF16, pB)
            lnc = sb("lnc", [128, 2, D], F32, pB)
            rl_t = sb("rl_t", [128, TQ], F32, pB)
            qg = r1[:, 0:4096].rearrange("p (h n) -> p h n", h=8)
            qn = r1[:, 4096:8192].rearrange("p (h n) -> p h n", h=8)
            qr = r1[0:64, 8192:12288].rearrange("p (h n) -> p h n", h=8)
            Pb = r1[:, 12288:14336].rearrange("p (h n) -> p h n", h=4)
            yT = r1[:, 14336:22528].rearrange("p (h n) -> p h n", h=16)
            aT = r1[:, 0:22528].rearrange("p (h n) -> p h n", h=44)
            Kr = r2[:, 0:4096].rearrange("p (s n) -> p s n", s=4)
            KRr = r2[0:64, 4096:8192].rearrange("p (s n) -> p s n", s=4)
            Vr = r2[:, 8192:12288].rearrange("p (s c d) -> p s c d", s=4, c=8)
            ql = r2[:, 12288:14336].rearrange("p (j n) -> p j n", j=4)
            mg = r2[:, 0:8192].rearrange("p (c n) -> p c n", c=16)
            qg_b = [Buf(f"qg{h}") for h in range(8)]
            qn_b = [Buf(f"qn{h}") for h in range(8)]
            qr_b = [Buf(f"qr{h}") for h in range(8)]
            P_b = [Buf(f"P{i}") for i in range(4)]
            y_b = [Buf(f"y{i}") for i in range(16)]
            aT_b = [Buf(f"aT{i}") for i in range(44)]
            kv_b = [Buf(f"kv{i}") for i in range(4)]
            ql_b = [Buf(f"ql{i}") for i in range(4)]
            mg_b = [Buf(f"mg{i}") for i in range(16)]
            lnc_b = Buf("lnc")
            rl_b = Buf("rl")
            kvslot = [k.newslot(f"kv{i}") for i in range(4)]
            lnslot = k.newslot("lnc")
            r1_att = qg_b + qn_b + qr_b + P_b + y_b
            r2_att = kv_b + ql_b
            for bb in r1_att + r2_att + aT_b + mg_b + [lnc_b, rl_b]:
                bb.guard = pA_guard
            kv_i = [0]
            hrhs = (lambda kc: hT[:, kc, :])
            hbufs = (lambda kc: [hT_b[kc]])
            xs_t = sb("xs_t", [128, D], F32, pB)
            xs_b = Buf("xs")
            xs_b.guard = pA_guard
            xs_slot = k.newslot("xs")

            def prefetch_tables(tn):
                rr = tn * TQ
                k.dma(sp, tabg[:, 0, :], cosgq_d[:, rr:rr + TQ], tab_slot, writes=[tabg_b])
                k.dma(sp, tabg[:, 1, :], singq_d[:, rr:rr + TQ], tab_slot, writes=[tabg_b])
                k.dma(sp, tabm[:, 0, :], cosmq_d[:, rr:rr + TQ], tab_slot, writes=[tabm_b])
                k.dma(sp, tabm[:, 1, :], sinmq_d[:, rr:rr + TQ], tab_slot, writes=[tabm_b])

            def prefetch_ln(tn, s):
                rr = tn * TQ + s * 128
                k.dma(sp, xs_t[:], xq[rr:rr + 128, :], xs_slot, writes=[xs_b])
                layer_norm_to_xn(s, xs_t[:], xs_b)

            def load_x_resid(tn):
                rr = tn * TQ
                for s in range(4):
                    k.dma(sp, xbuf[:, s, :], xq[rr + s * 128:rr + (s + 1) * 128, :], xslot[s], writes=[xbuf_b[s]])

            prefetch_tables(0)
            for s in range(4):
                prefetch_ln(0, s)
            transpose_modulate(0, 16)
            load_x_resid(0)
            SC_G = float(128.0 ** -0.5)
            SC_M = float(192.0 ** -0.5)

            for qt in range(nqt):
                r0 = qt * TQ
                has_next = qt + 1 < nqt

                jobs = []
                for blk in range(4):
                    def ld(blk=blk):
                        return wload_packed(JOB_B2 + blk)

                    def cp(i, blk=blk):
                        wv = kview(wr[i], 16, 256)
                        for hh in range(2):
                            h = blk * 2 + hh
                            b = proj_T(wv, wr_b[i], hh * 128, 128, 16, hrhs, hbufs)
                            rope_rms_finalize(b, gq[:, 0:1], tabg, tabg_b, 0, qg[:, h, :], qg_b[h])
                    jobs.append((ld, cp))
                for blk in range(2):
                    def ld(blk=blk):
                        return wload_packed(JOB_B2 + 4 + blk)

                    def cp(i, blk=blk):
                        wv = kview(wr[i], 16, 256)
                        for jj in range(2):
                            j = blk * 2 + jj
                            b = proj_T(wv, wr_b[i], jj * 128, 128, 16, hrhs, hbufs)
                            latent_chunk(b, j, mqg, ql, ql_b)
                        if blk == 1:
                            latent_finish(ql, ql_b)
                    jobs.append((ld, cp))
                for blk in range(2):
                    def ld(blk=blk):
                        return wload_packed(JOB_B2 + 6 + blk, 3072)

                    def cp(i, blk=blk):
                        wv = kview(wr[i], 4, 768)
                        for hh in range(4):
                            h = blk * 4 + hh
                            b = proj_T(wv, wr_b[i], hh * 192, 128, 4, lambda j: ql[:, j, :], lambda j: [ql_b[j]])
                            if h % 2 == 0:
                                k.emit(act, lambda: A.activation(out=qn[:, h, :], in_=psA[b][:], func=AF.Identity),
                                       reads=[psA_b[b]], writes=[qn_b[h]])
                            else:
                                k.emit(dve, lambda: V.tensor_copy(out=qn[:, h, :], in_=psA[b][:]),
                                       reads=[psA_b[b]], writes=[qn_b[h]])
                            b = proj_T(wv, wr_b[i], hh * 192 + 128, 64, 4, lambda j: ql[:, j, :], lambda j: [ql_b[j]])
                            rope64_finalize(b, None, None, tabm, tabm_b, qr[:, h, :], qr_b[h])
                    jobs.append((ld, cp))
                stream(jobs)
                flush_deferred()

                if stop_after == "B2" and qt == 0:
                    dbg_dump("dbg_qg", r1[:, 0:4096], [128, 4096], BF16, qg_b)
                    dbg_dump("dbg_qn", r1[:, 4096:8192], [128, 4096], BF16, qn_b)
                    dbg_dump("dbg_qr", r1[0:64, 8192:12288], [64, 4096], BF16, qr_b)
                    break

                if qt == 0:
                    while prep_state["next"] < NJOB:
                        prep_jobs(1)
                    prep_flush()
                NST = 16 * 64
                NPC = NST // 8
                piece_slot = {}

                def kv_load(pi):
                    hi, p = divmod(pi, 8)
                    sl = kv_i[0] % 4
                    kv_i[0] += 1
                    piece_slot[pi] = sl
                    if hi < 8:
                        j = hi // 4
                        k.dma(sp, Kr[:, sl, :], kg_s[j, :, p * KP:(p + 1) * KP], kvslot[sl], writes=[kv_b[sl]])
                        k.dma(sp, Vr[:, sl], vg_s[j, :, p * 8:(p + 1) * 8, :], kvslot[sl], writes=[kv_b[sl]])
                    else:
                        h = hi - 8
                        k.dma(sp, Kr[:, sl, :], kn_s[h, :, p * KP:(p + 1) * KP], kvslot[sl], writes=[kv_b[sl]])
                        k.dma(sp, KRr[:, sl, :], kr_s[:, p * KP:(p + 1) * KP], kvslot[sl], writes=[kv_b[sl]])
                        k.dma(sp, Vr[:, sl], vm_s[h, :, p * 8:(p + 1) * 8, :], kvslot[sl], writes=[kv_b[sl]])

                S_BANKS = (0, 1, 6, 7)

                def S_step(n):
                    hi = n // 64
                    kk = n % 8
                    sl = piece_slot[n // 8]
                    b = S_BANKS[n % 4]
                    ksl = Kr[:, sl, kk * 128:(kk + 1) * 128]
                    if hi < 8:
                        mm(psA[b][:], ksl, qg[:, hi, :], True, True, [kv_b[sl], qg_b[hi]], [psA_b[b]])
                    else:
                        h = hi - 8
                        mm(psA[b][:], ksl, qn[:, h, :], True, False, [kv_b[sl], qn_b[h]], [psA_b[b]])
                        mm(psA[b][:], KRr[:, sl, kk * 128:(kk + 1) * 128], qr[:, h, :], False, True,
                           [kv_b[sl], qr_b[h]], [psA_b[b]])

                def E_step(n):
                    hi = n // 64
                    b = S_BANKS[n % 4]
                    sc = SC_G if hi < 8 else SC_M
                    k.emit(act, lambda: A.activation(out=Pb[:, n % 4, :], in_=psA[b][:], func=AF.Exp, scale=sc),
                           reads=[psA_b[b]], writes=[P_b[n % 4]])

                def PV_step(n):
                    hi, c = divmod(n, 64)
                    kk = n % 8
                    sl = piece_slot[n // 8]
                    ob, lb = 2 + hi % 2, 4 + hi % 2
                    mm(psA[ob][:], Vr[:, sl, kk, :], Pb[:, n % 4, :], c == 0, c == 63, [kv_b[sl], P_b[n % 4]], [psA_b[ob]])
                    mm(psA[lb][:], ones, Pb[:, n % 4, :], c == 0, c == 63, [cst_b, P_b[n % 4]], [psA_b[lb]])
                    if c == 63:
                        k.emit(dve, lambda: V.reciprocal(out=rl_t[:], in_=psA[lb][:]), reads=[psA_b[lb]], writes=[rl_b])
                        k.emit(dve, lambda: V.tensor_tensor(out=yT[:, hi, :], in0=psA[ob][:], in1=rl_t[:], op=ALU.mult),
                               reads=[psA_b[ob], rl_b], writes=[y_b[hi]])

                for pi in range(4):
                    kv_load(pi)
                for n in range(4):
                    S_step(n)
                for n in range(NST):
                    E_step(n)
                    PV_step(n)
                    if n + 4 < NST:
                        S_step(n + 4)
                    if (n + 1) % 8 == 0 and n // 8 + 4 < NPC:
                        kv_load(n // 8 + 4)

                if stop_after == "B4" and qt == 0:
                    dbg_dump("dbg_y", r1[:, 14336:22528], [128, 8192], BF16, y_b)
                    break

                g5 = k.fence()
                for bb in mg_b:
                    bb.guard = g5
                sets = tuple((o.t1, o.t2, o.t1_b, o.t2_b) for o in tsets)
                jobs = []
                for p in range(8):
                    def ldg(p=p):
                        return wload_packed(JOB_B5 + 3 * p)

                    def cpg(i, p=p):
                        wv = kview(wr[i], 16, 256)
                        for e in range(2):
                            oc = 2 * p + e
                            sg_t, _, sg_b, _ = sets[e]
                            b = proj_T(wv, wr_b[i], e * 128, 128, 16, hrhs, hbufs)
                            k.emit(act, lambda: A.activation(out=sg_t[:], in_=psA[b][:], func=AF.Sigmoid,
                                                             bias=bgT[:, oc:oc + 1]),
                                   reads=[psA_b[b], small_b], writes=[sg_b])

                    def ldm(p=p):
                        return wload_packed(JOB_B5 + 3 * p + 1)

                    def cpm(i, p=p):
                        wv = kview(wr[i], 16, 256)
                        for e in range(2):
                            oc = 2 * p + e
                            _, sm_t, _, sm_b = sets[e]
                            b = proj_T(wv, wr_b[i], e * 128, 128, 16, hrhs, hbufs)
                            k.emit(act, lambda: A.activation(out=sm_t[:], in_=psA[b][:], func=AF.Sigmoid,
                                                             bias=bgT[:, 16 + oc:17 + oc]),
                                   reads=[psA_b[b], small_b], writes=[sm_b])

                    def ldy(p=p):
                        return wload_packed(JOB_B5 + 3 * p + 2)

                    def cpy(i, p=p):
                        wv = kview(wr[i], 16, 256)
                        for e in range(2):
                            oc = 2 * p + e
                            sg_t, sm_t, sg_b, sm_b = sets[e]
                            b = next_ps()
                            for kc in range(8):
                                mm(psA[b][:], wv[:, kc, e * 128:(e + 1) * 128], yT[:, kc, :], kc == 0, kc == 7,
                                   [wr_b[i], y_b[kc]], [psA_b[b]])
                            k.emit(dve, lambda: V.tensor_tensor(out=sg_t[:], in0=psA[b][:], in1=sg_t[:], op=ALU.mult),
                                   reads=[psA_b[b], sg_b], writes=[sg_b])
                            b = next_ps()
                            for kc in range(8):
                                mm(psA[b][:], wv[:, 8 + kc, e * 128:(e + 1) * 128], yT[:, 8 + kc, :], kc == 0, kc == 7,
                                   [wr_b[i], y_b[8 + kc]], [psA_b[b]])
                            k.emit(dve, lambda: V.tensor_tensor(out=sm_t[:], in0=psA[b][:], in1=sm_t[:], op=ALU.mult),
                                   reads=[psA_b[b], sm_b], writes=[sm_b])
                            k.emit(dve, lambda: V.tensor_tensor(out=mg[:, oc, :], in0=sg_t[:], in1=sm_t[:], op=ALU.add),
                                   reads=[sg_b, sm_b], writes=[mg_b[oc]])
                    jobs.append((ldg, cpg))
                    jobs.append((ldm, cpm))
                    jobs.append((ldy, cpy))
                stream(jobs)

                k.dma(sp, lnc[:, 0, :], lnc_d[0], lnslot, writes=[lnc_b])
                k.dma(sp, lnc[:, 1, :], lnc_d[1], lnslot, writes=[lnc_b])
                jobs = []
                for blk in range(8):
                    def ld(blk=blk):
                        return wload_bf([(lambda t: kview(t, 16, 256),
                                          wout_s.rearrange("(k p) n -> p k n", p=128)[:, :, blk * 256:(blk + 1) * 256])],
                                        deps=fold_deps)

                    def cp(i, blk=blk):
                        wv = kview(wr[i], 16, 256)
                        for s in range(4):
                            b = next_ps()
                            for kc in range(16):
                                mm(psA[b][:, 0:256], mg[:, kc, s * 128:(s + 1) * 128], wv[:, kc, :], kc == 0, kc == 15,
                                   [mg_b[kc], wr_b[i]], [psA_b[b]])
                            xs = xbuf[:, s, blk * 256:(blk + 1) * 256]
                            k.emit(dve, lambda: V.scalar_tensor_tensor(out=xs, in0=xs, scalar=float(ALPHA),
                                                                       in1=psA[b][:, 0:256], op0=ALU.mult, op1=ALU.add),
                                   reads=[psA_b[b], xbuf_b[s]], writes=[xbuf_b[s]])
                    jobs.append((ld, cp))
                stream(jobs)
                g6 = k.fence()
                for bb in r2_att:
                    bb.guard = g6

                def ln_affine(s):
                    xs = xbuf[:, s, :]
                    for c4 in range(4):
                        k.emit(dve, lambda: V.bn_stats(out=stats[:, s, c4, :], in_=xs[:, c4 * 512:(c4 + 1) * 512]),
                               reads=[xbuf_b[s]], writes=[st_b[s]])
                    k.emit(dve, lambda: V.bn_aggr(out=mv[:, s, :], in_=stats[:, s].rearrange("p c f -> p (c f)")),
                           reads=[st_b[s]], writes=[st_b[s]])
                    rsqrt_small(rstd[:, s:s + 1], mv[:, s, 1:2], 0, [st_b[s]], st_b[s])
                    k.emit(dve, lambda: V.scalar_tensor_tensor(out=xs, in0=xs, scalar=mv[:, s, 0:1], in1=lnc[:, 0, :],
                                                               op0=ALU.subtract, op1=ALU.mult),
                           reads=[xbuf_b[s], st_b[s], lnc_b], writes=[xbuf_b[s]])
                    k.emit(dve, lambda: V.scalar_tensor_tensor(out=xs, in0=xs, scalar=rstd[:, s:s + 1], in1=lnc[:, 1, :],
                                                               op0=ALU.mult, op1=ALU.add),
                           reads=[xbuf_b[s], st_b[s], lnc_b], writes=[xbuf_b[s]])

                for s in range(4):
                    ln_affine(s)
                if stop_after == "B6" and qt == 0:
                    dbg_dump("dbg_mg", r2[:, 0:8192], [128, 8192], BF16, mg_b)
                    dbg_dump("dbg_x1", xbuf[:], [128, 4, D], F32, xbuf_b)
                    break
                for s in range(4):
                    layer_norm_to_xn(s)
                transpose_modulate(32, 48)
                k.dma(sp, lnc[:, 0, :], lnc_d[2], lnslot, writes=[lnc_b])
                k.dma(sp, lnc[:, 1, :], lnc_d[3], lnslot, writes=[lnc_b])

                g8 = k.fence()
                for bb in aT_b:
                    bb.guard = g8
                jobs = []
                for fc in range(44):
                    sg_t, _, sg_b, _ = sets[fc % 2]

                    def ld(fc=fc):
                        return wload_packed(JOB_B8 + fc)

                    def cp(i, fc=fc, sg_t=sg_t, sg_b=sg_b):
                        wv = kview(wr[i], 16, 256)
                        if has_next and fc == 1:
                            prefetch_tables(qt + 1)
                        if has_next and fc in (2, 10, 18, 26):
                            prefetch_ln(qt + 1, (fc - 2) // 8)
                        b = proj_T(wv, wr_b[i], 0, 128, 16, hrhs, hbufs)
                        k.emit(act, lambda: A.activation(out=sg_t[:], in_=psA[b][:], func=AF.Silu),
                               reads=[psA_b[b]], writes=[sg_b])
                        b = proj_T(wv, wr_b[i], 128, 128, 16, hrhs, hbufs)
                        k.emit(dve, lambda: V.tensor_tensor(out=aT[:, fc, :], in0=psA[b][:], in1=sg_t[:], op=ALU.mult),
                               reads=[psA_b[b], sg_b], writes=[aT_b[fc]])
                    jobs.append((ld, cp))
                stream(jobs)
                if has_next:
                    transpose_modulate(0, 16)
                wdn_v = wdn_s.rearrange("(k p) n -> p k n", p=128)
                jobs = []
                for nb in range(4):
                    for g in range(6):
                        nf = 8 if g < 5 else 4

                        def ld(nb=nb, g=g, nf=nf):
                            return wload_bf([(lambda t: kview(t, 8, 512)[:, 0:nf, :],
                                              wdn_v[:, g * 8:g * 8 + nf, nb * 512:(nb + 1) * 512])], deps=fold_deps)

                        def cp(i, nb=nb, g=g, nf=nf):
                            wv = kview(wr[i], 8, 512)
                            for fi in range(nf):
                                fc = g * 8 + fi
                                for s in range(4):
                                    mm(psA[s][:], aT[:, fc, s * 128:(s + 1) * 128], wv[:, fi, :], fc == 0, fc == 43,
                                       [aT_b[fc], wr_b[i]], [psA_b[s]])
                            if g == 5:
                                for s in range(4):
                                    xs = xbuf[:, s, nb * 512:(nb + 1) * 512]
                                    k.emit(dve, lambda: V.scalar_tensor_tensor(out=xs, in0=xs, scalar=float(ALPHA),
                                                                               in1=psA[s][:], op0=ALU.mult, op1=ALU.add),
                                           reads=[psA_b[s], xbuf_b[s]], writes=[xbuf_b[s]])
                        jobs.append((ld, cp))
                stream(jobs)
                k.ps_i = 4
                g9 = k.fence()
                for bb in r1_att:
                    bb.guard = g9
                for s in range(4):
                    ln_affine(s)
                    k.dma(sp, y[r0 + s * 128:r0 + (s + 1) * 128, :], xbuf[:, s, :], yslot[s], reads=[xbuf_b[s]])
                if has_next:
                    load_x_resid(qt + 1)

        for s_ in k.slots:
            if s_.count:
                sp.e.wait_ge(s_.sem, s_.count)
    return nc, dump_specs


def _perm(n):
    h = n // 2
    p = np.empty(n, np.int64)
    p[:h] = 2 * np.arange(h)
    p[h:] = 2 * np.arange(h) + 1
    return p


def _rope_tables(dim):
    quarter = dim // 4
    inv = 10000.0 ** (-np.arange(quarter, dtype=np.float64) / quarter)
    t = np.arange(S)
    ang = np.concatenate([(t // 64)[:, None] * inv[None, :], (t % 64)[:, None] * inv[None, :]], axis=1)
    c, s = np.cos(ang).T, np.sin(ang).T
    cosT = np.concatenate([c, c], axis=0)
    sinT = np.concatenate([-s, s], axis=0)
    return np.ascontiguousarray(cosT, np.float32), np.ascontiguousarray(sinT, np.float32)


def _swap(n):
    m = np.zeros((n, n), np.float32)
    idx = np.arange(n)
    m[(idx + n // 2) % n, idx] = 1.0
    return m


def make_in_maps(x, c, w_ada, b_ada, w_in, b_gates, gqa_q_gain, gqa_k_gain, mla_q_gain, mla_kv_gain,
                 w_mla_uq, w_mla_ukv, w_branch_gqa, w_branch_mla, w_out, ln1_g, ln1_b,
                 w_ffn_gate, w_ffn_up, w_ffn_down, ln2_g, ln2_b):
    f = lambda a: np.ascontiguousarray(np.asarray(a, dtype=np.float32))
    p128, p64 = _perm(128), _perm(64)
    cols = np.arange(6720)
    for h in range(8):
        cols[h * 128:(h + 1) * 128] = h * 128 + p128
    for h in range(2):
        cols[1024 + h * 128:1024 + (h + 1) * 128] = 1024 + h * 128 + p128
    cols[2560:2624] = 2560 + p64
    w_in_p = f(np.asarray(w_in)[0][:, cols])
    ucols = np.arange(1536)
    for h in range(8):
        ucols[h * 192 + 128:(h + 1) * 192] = h * 192 + 128 + p64
    w_uq_p = f(np.asarray(w_mla_uq)[0][:, ucols])
    ba = np.asarray(b_ada, np.float32)[0]
    badaT = f(np.concatenate([ba[0:D], ba[D:2 * D], ba[3 * D:4 * D], ba[4 * D:5 * D]]).reshape(64, 128).T)
    bgbc = f(np.broadcast_to(np.concatenate([ba[2 * D:3 * D], ba[5 * D:6 * D]])[None, :], (128, 2 * D)))
    lnc = f(np.stack([np.broadcast_to(np.asarray(v, np.float32)[0][None, :], (128, D))
                      for v in (ln1_g, ln1_b, ln2_g, ln2_b)]))
    cosg, sing = _rope_tables(128)
    cosm, sinm = _rope_tables(64)
    cst = f(np.stack([np.eye(128, dtype=np.float32), np.ones((128, 128), np.float32), _swap(128)], axis=1))
    common = dict(
        w_ada=f(np.asarray(w_ada)[0]), badaT=badaT, bgbc=bgbc, w_in=w_in_p,
        bgatesT=f(np.asarray(b_gates, np.float32)[0].reshape(32, 128).T),
        gq=f(np.asarray(gqa_q_gain, np.float32)[0][p128][:, None]),
        gk=f(np.asarray(gqa_k_gain, np.float32)[0][p128][:, None]),
        mqg=f(np.asarray(mla_q_gain, np.float32)[0].reshape(4, 128).T),
        mkg=f(np.asarray(mla_kv_gain, np.float32)[0].reshape(4, 128).T),
        w_uq=w_uq_p, w_ukv=f(np.asarray(w_mla_ukv)[0]), w_bg=f(np.asarray(w_branch_gqa)[0]),
        w_bm=f(np.asarray(w_branch_mla)[0]), w_out=f(np.asarray(w_out)[0]), lnc=lnc,
        w_fg=f(np.asarray(w_ffn_gate)[0]), w_fu=f(np.asarray(w_ffn_up)[0]), w_fd=f(np.asarray(w_ffn_down)[0]),
        cst=cst, pmat=_swap(64),
    )
    x = np.asarray(x, np.float32)
    c = np.asarray(c, np.float32)
    maps = []
    for core in range(8):
        b, half = divmod(core, 2)
        order = np.concatenate([np.arange(half * SQ, (half + 1) * SQ), np.arange((1 - half) * SQ, (2 - half) * SQ)])
        m = dict(common)
        m["xa"] = f(x[b][order])
        m["cT"] = f(c[b].reshape(16, 128).T)
        m["cosg"] = f(cosg[:, order])
        m["sing"] = f(sing[:, order])
        m["cosm"] = f(cosm[:, order])
        m["sinm"] = f(sinm[:, order])
        maps.append(m)
    return maps


_NC_CACHE = {}


def kernel(**inputs):
    if "nc" not in _NC_CACHE:
        _NC_CACHE["nc"] = build_program()[0]
    nc = _NC_CACHE["nc"]
    in_maps = make_in_maps(**inputs)
    res = run_bass_kernel_spmd(nc, in_maps, core_ids=list(range(8)))
    out = np.empty((4, S, D), np.float32)
    for core in range(8):
        b, half = divmod(core, 2)
        out[b, half * SQ:(half + 1) * SQ] = res.results[core]["y"]
    return out
```

```python
from contextlib import ExitStack
import numpy as np
import concourse.bass as bass
import concourse.mybir as mybir
from concourse.bass_utils import run_bass_kernel_spmd

F32 = mybir.dt.float32
BF16 = mybir.dt.bfloat16
AF = mybir.ActivationFunctionType
ALU = mybir.AluOpType

D = 2048
S = 8192
SQ = 4096
DFF = 5632
LN_EPS = 1e-5
RMS_EPS = 1e-6
ALPHA = 2.0 ** 0.25
TQ = 512
NQT = SQ // TQ
NKT = S // TQ
KP = 1024
NPIECE = S // KP
WSLOT = 4096


class Buf:
    __slots__ = ("name", "w", "r", "guard")

    def __init__(self, name):
        self.name = name
        self.w = None
        self.r = {}
        self.guard = ()


class Eng:
    def __init__(self, name, e, sem, is_pe=False):
        self.name, self.e, self.sem, self.is_pe = name, e, sem, is_pe
        self.n = 0
        self.seen = {}


class Slot:
    def __init__(self, sem):
        self.sem = sem
        self.count = 0


class K:
    def __init__(self, nc, es):
        self.nc = nc
        self.es = es
        self.nsem = 0
        self.pe = Eng("pe", nc.tensor, self.newsem("pe"), is_pe=True)
        self.act = Eng("act", nc.scalar, self.newsem("act"))
        self.dve = Eng("dve", nc.vector, self.newsem("dve"))
        self.pool = Eng("pool", nc.gpsimd, self.newsem("pool"))
        self.sp = Eng("sp", nc.sync, self.newsem("sp"))
        self.slots = []
        self.ps_i = 0

    def newsem(self, name):
        self.nsem += 1
        return self.es.enter_context(self.nc.semaphore(f"s{self.nsem}_{name}"))

    def newslot(self, name):
        s = Slot(self.newsem(name))
        self.slots.append(s)
        return s

    def _gather(self, eng, reads, writes, deps):
        need = {}

        def add(tok):
            if tok is None:
                return
            s, v = tok
            if s is eng.sem:
                if eng.is_pe or v <= eng.n - 3:
                    return
            if eng.seen.get(s, 0) >= v:
                return
            if need.get(s, 0) < v:
                need[s] = v

        for b in reads:
            add(b.w)
            for t in b.guard:
                add(t)
        for b in writes:
            add(b.w)
            for t in b.r.values():
                add(t)
            for t in b.guard:
                add(t)
        for t in deps:
            add(t)
        return list(need.items())

    def _apply_waits(self, eng, items, fn):
        for s, v in items[:-1]:
            eng.e.wait_ge(s, v)
            eng.seen[s] = v
        inst = fn()
        if items:
            s, v = items[-1]
            inst._wait_ge(s, v)
            eng.seen[s] = v
        return inst

    def emit(self, eng, fn, reads=(), writes=(), deps=()):
        items = self._gather(eng, reads, writes, deps)
        inst = self._apply_waits(eng, items, fn)
        eng.n += 1
        inst.then_inc(eng.sem, 1)
        tok = (eng.sem, eng.n)
        for b in reads:
            b.r[eng.sem] = tok
        for b in writes:
            b.w = tok
            b.r = {}
        return tok

    def dma(self, q, out, in_, slot, reads=(), writes=(), deps=()):
        items = self._gather(q, reads, writes, deps)
        inst = self._apply_waits(q, items, lambda: q.e.dma_start(out=out, in_=in_))
        slot.count += 16
        inst.then_inc(slot.sem, 16)
        tok = (slot.sem, slot.count)
        for b in reads:
            b.r[slot.sem] = tok
        for b in writes:
            b.w = tok
            b.r = {}
        return tok

    def fence(self):
        return tuple((e.sem, e.n) for e in (self.pe, self.act, self.dve, self.pool) if e.n > 0)


def build_program(nqt=NQT, nkt=NKT, stop_after=None, dumps=()):
    nc = bass.Bass("TRN2", target_bir_lowering=False)
    dumps = set(dumps)
    dump_specs = {}

    def din(name, shape, dt=F32):
        return nc.dram_tensor(name, list(shape), dt, kind="ExternalInput").ap()

    def dscr(name, shape, dt=BF16):
        kind = "ExternalOutput" if name in dumps else "Internal"
        if name in dumps:
            dump_specs[name] = (tuple(shape), dt)
        return nc.dram_tensor(name, list(shape), dt, kind=kind).ap()

    xa = din("xa", [S, D])
    xq = xa
    cT_d = din("cT", [128, 16])
    w_ada = din("w_ada", [D, 6 * D])
    badaT_d = din("badaT", [128, 64])
    bgbc_d = din("bgbc", [128, 2 * D])
    w_in = din("w_in", [D, 6720])
    bgT_d = din("bgatesT", [128, 32])
    gq_d = din("gq", [128, 1])
    gk_d = din("gk", [128, 1])
    mqg_d = din("mqg", [128, 4])
    mkg_d = din("mkg", [128, 4])
    w_uq = din("w_uq", [512, 1536])
    w_ukv = din("w_ukv", [512, 2048])
    w_bg = din("w_bg", [1024, D])
    w_bm = din("w_bm", [1024, D])
    w_out = din("w_out", [D, D])
    lnc_d = din("lnc", [4, 128, D])
    w_fg = din("w_fg", [D, DFF])
    w_fu = din("w_fu", [D, DFF])
    w_fd = din("w_fd", [DFF, D])
    cosg_d = din("cosg", [128, S])
    sing_d = din("sing", [128, S])
    cosm_d = din("cosm", [64, S])
    sinm_d = din("sinm", [64, S])
    cosgq_d, singq_d, cosmq_d, sinmq_d = cosg_d, sing_d, cosm_d, sinm_d
    cst_d = din("cst", [128, 3, 128])
    pm_d = din("pmat", [64, 64])

    y = nc.dram_tensor("y", [SQ, D], F32, kind="ExternalOutput").ap()

    kg_s = dscr("kg_s", [2, 128, S])
    vg_s = dscr("vg_s", [2, 128, S // 128, 128])
    kn_s = dscr("kn_s", [8, 128, S])
    kr_s = dscr("kr_s", [64, S])
    vm_s = dscr("vm_s", [8, 128, S // 128, 128])
    wout_s = dscr("wout_s", [D, D])
    wdn_s = dscr("wdn_s", [DFF, D])
    NJOB = 76
    wscr = dscr("wscr", [NJOB, 128, WSLOT])

    es = ExitStack()
    with es:
        k = K(nc, es)
        pe, act, dve, pool, sp = k.pe, k.act, k.dve, k.pool, k.sp
        T, V, A = nc.tensor, nc.vector, nc.scalar

        def sb(name, shape, dt, stack=es):
            return stack.enter_context(nc.sbuf_tensor("sb_" + name, list(shape), dt))

        psA = [es.enter_context(nc.psum_tensor(f"psA{i}", [128, 512], F32)) for i in range(8)]
        psA_b = [Buf(f"psA{i}") for i in range(8)]
        PS_ROT = (0, 1, 2, 3, 4, 6, 7)

        def next_ps():
            i = PS_ROT[k.ps_i % len(PS_ROT)]
            k.ps_i += 1
            return i

        cst = sb("cst", [128, 3, 128], BF16)
        pmat = sb("pmat", [64, 64], BF16)
        ident, ones, pg = cst[:, 0, :], cst[:, 1, :], cst[:, 2, :]
        cst_b, pm_b = Buf("cst"), Buf("pm")
        cT = sb("cT", [128, 16], F32)
        cact = sb("cact", [128, 16], BF16)
        cact_b = Buf("cact")
        modT = sb("modT", [128, 64], F32)
        modT_b = Buf("modT")
        badaT = sb("badaT", [128, 64], F32)
        bgT = sb("bgT", [128, 32], F32)
        gq = sb("gq", [128, 1], F32)
        gk = sb("gk", [128, 1], F32)
        mqg = sb("mqg", [128, 4], F32)
        mkg = sb("mkg", [128, 4], F32)
        small_b = Buf("small")
        epsc = sb("epsc", [128, 3], F32)
        epsc_b = Buf("epsc")
        xbuf = sb("xbuf", [128, 4, D], F32)
        xbuf_b = [Buf(f"xbuf{s}") for s in range(4)]
        xn = sb("xn", [128, 4, D], BF16)
        xn_b = [Buf(f"xn{s}") for s in range(4)]
        hT = sb("hT", [128, 16, TQ], BF16)
        hT_b = [Buf(f"hT{c}") for c in range(16)]
        NWS = 2
        wr = [sb(f"wr{i}", [128, WSLOT], BF16) for i in range(NWS)]
        wr_b = [Buf(f"wr{i}") for i in range(NWS)]
        wr_slot = [k.newslot(f"wr{i}") for i in range(NWS)]
        wr_i = [0]
        stats = sb("stats", [128, 4, 4, 6], F32)
        mv = sb("mv", [128, 4, 2], F32)
        rstd = sb("rstd", [128, 4], F32)
        nmr = sb("nmr", [128, 4], F32)
        st_b = [Buf(f"st{s}") for s in range(4)]
        tabg = sb("tabg", [128, 2, TQ], F32)
        tabm = sb("tabm", [64, 2, TQ], F32)
        tabg_b, tabm_b = Buf("tabg"), Buf("tabm")
        tabg_slot, tabm_slot = k.newslot("tabg"), k.newslot("tabm")
        class TSet:
            pass

        tsets = []
        for i in range(2):
            o = TSet()
            o.xg = sb(f"tmpxg{i}", [128, TQ], BF16)
            o.sq = sb(f"tmpsq{i}", [128, TQ], BF16)
            o.rs = sb(f"tmprs{i}", [128, TQ], F32)
            o.t1 = sb(f"tmpa{i}", [128, TQ], F32)
            o.t2 = sb(f"tmpb{i}", [128, TQ], F32)
            o.xg_b, o.sq_b, o.rs_b, o.t1_b, o.t2_b = (Buf(f"{n}{i}") for n in ("xg", "sq", "rs", "t1", "t2"))
            tsets.append(o)
        ts_i = [0]

        def next_tset():
            o = tsets[ts_i[0] % 2]
            ts_i[0] += 1
            return o

        rq_t = sb("rq_t", [128, TQ], F32)
        rq_b = Buf("rq")
        cs_n = [0]

        def cs():
            cs_n[0] += 1
            return k.newslot(f"c{cs_n[0]}")

        xslot = [k.newslot(f"x{s}") for s in range(4)]
        yslot = [k.newslot(f"y{s}") for s in range(4)]

        for ci, cv in enumerate((LN_EPS, 128.0 * RMS_EPS, 512.0 * RMS_EPS)):
            k.emit(dve, lambda: V.memset(epsc[:, ci:ci + 1], float(cv)), writes=[epsc_b])

        def rsqrt_small(out_ap, in_ap, eps_col, rbufs, wbuf):
            k.emit(act, lambda: A.activation(out=out_ap, in_=in_ap, func=AF.Sqrt, bias=epsc[:, eps_col:eps_col + 1]),
                   reads=list(rbufs) + [epsc_b], writes=[wbuf])
            k.emit(dve, lambda: V.reciprocal(out=out_ap, in_=out_ap), reads=[wbuf], writes=[wbuf])

        def wload(view_src_pairs):
            i = wr_i[0] % NWS
            wr_i[0] += 1
            for view_fn, src in view_src_pairs:
                k.dma(pool, view_fn(wr[i]), src, wr_slot[i], writes=[wr_b[i]])
            return i

        def wload_bf(view_src_pairs, deps=()):
            i = wr_i[0] % NWS
            wr_i[0] += 1
            for view_fn, src in view_src_pairs:
                k.dma(pool, view_fn(wr[i]), src, wr_slot[i], writes=[wr_b[i]], deps=deps)
            return i

        def stream(jobs, lookahead=NWS):
            n = len(jobs)
            slots = [None] * n
            for i in range(min(lookahead, n)):
                slots[i] = jobs[i][0]()
            for i in range(n):
                jobs[i][1](slots[i])
                if i + lookahead < n:
                    slots[i + lookahead] = jobs[i + lookahead][0]()

        def kview(t, kc, n):
            return t[:, 0:kc * n].rearrange("p (k n) -> p k n", k=kc)

        def wsrc(w, c0, n):
            return w.rearrange("(k p) n -> p k n", p=128)[:, :, c0:c0 + n]

        dbg_slot = k.newslot("dbg")

        def dbg_dump(name, ap, shape, dt, bufs):
            dmp = nc.dram_tensor(name, list(shape), dt, kind="ExternalOutput").ap()
            dump_specs[name] = (tuple(shape), dt)
            k.dma(sp, dmp, ap, dbg_slot, reads=bufs)

        def mm(out, lhsT, rhs, start, stop, reads, writes):
            return k.emit(pe, lambda: T.matmul(out, lhsT, rhs, start=start, stop=stop),
                          reads=reads, writes=writes)

        k.dma(pool, cst[:], cst_d, cs(), writes=[cst_b])
        k.dma(pool, pmat[:], pm_d, cs(), writes=[pm_b])
        for t_sb, t_d in ((cT, cT_d), (badaT, badaT_d), (bgT, bgT_d), (gq, gq_d), (gk, gk_d),
                          (mqg, mqg_d), (mkg, mkg_d)):
            k.dma(sp, t_sb[:], t_d, cs(), writes=[small_b])
        k.emit(act, lambda: A.activation(out=cact[:], in_=cT[:], func=AF.Silu), reads=[small_b], writes=[cact_b])
        k.emit(dve, lambda: V.tensor_scalar(out=mqg[:], in0=mqg[:], scalar1=float(512.0 ** 0.5), scalar2=None,
                                            op0=ALU.mult), reads=[small_b], writes=[small_b])
        k.emit(dve, lambda: V.tensor_scalar(out=mkg[:], in0=mkg[:], scalar1=float(512.0 ** 0.5), scalar2=None,
                                            op0=ALU.mult), reads=[small_b], writes=[small_b])

        p0 = ExitStack()
        with p0:
            crep = sb("crep", [128, 16, 128], BF16, p0)
            crep_b = Buf("crep")
            gbc = sb("gbc", [128, 2 * D], F32, p0)
            gbc_b = Buf("gbc")
            bgbc = sb("bgbc", [128, 2 * D], F32, p0)
            bgbc_b = Buf("bgbc")
            fst = [sb(f"fst{i}", [128, D], F32, p0) for i in range(2)]
            fbf = [sb(f"fbf{i}", [128, D], BF16, p0) for i in range(2)]
            fst_b = [Buf(f"fst{i}") for i in range(2)]
            fbf_b = [Buf(f"fbf{i}") for i in range(2)]
            fslot = [k.newslot(f"fst{i}") for i in range(2)]
            fsslot = [k.newslot(f"fbf{i}") for i in range(2)]

            k.dma(sp, bgbc[:], bgbc_d, cs(), writes=[bgbc_b])
            for kc in range(16):
                k.emit(dve, lambda: V.tensor_copy(out=crep[:, kc, :], in_=cact[:, kc:kc + 1].to_broadcast([128, 128])),
                       reads=[cact_b], writes=[crep_b])

            col_starts = [0, D, 3 * D, 4 * D]
            mod_ps = 5
            jobs = []
            for blk in range(32):
                seg, off = divmod(blk * 256, D)
                c0 = col_starts[seg] + off

                def ld(c0=c0):
                    return wload([(lambda t: kview(t, 16, 256), wsrc(w_ada, c0, 256))])

                def cp(i, blk=blk):
                    wv = kview(wr[i], 16, 256)
                    for cc in range(2):
                        j = blk * 2 + cc
                        for kc in range(16):
                            mm(psA[mod_ps][:, j:j + 1], wv[:, kc, cc * 128:(cc + 1) * 128], cact[:, kc:kc + 1],
                               kc == 0, kc == 15, [wr_b[i], cact_b], [psA_b[mod_ps]])
                jobs.append((ld, cp))
            stream(jobs)
            k.emit(dve, lambda: V.tensor_tensor(out=modT[:], in0=psA[mod_ps][:, 0:64], in1=badaT[:], op=ALU.add),
                   reads=[psA_b[mod_ps], small_b], writes=[modT_b])
            for lo in (16, 48):
                k.emit(dve, lambda: V.tensor_scalar(out=modT[:, lo:lo + 16], in0=modT[:, lo:lo + 16], scalar1=1.0,
                                                    scalar2=None, op0=ALU.add), reads=[modT_b], writes=[modT_b])
            jobs = []
            for blk in range(16):
                seg, off = divmod(blk * 256, D)
                c0 = (2 * D if seg == 0 else 5 * D) + off

                def ld(c0=c0):
                    return wload([(lambda t: kview(t, 16, 256), wsrc(w_ada, c0, 256))])

                def cp(i, blk=blk):
                    wv = kview(wr[i], 16, 256)
                    b = next_ps()
                    for kc in range(16):
                        mm(psA[b][:, 0:256], crep[:, kc, :], wv[:, kc, :], kc == 0, kc == 15,
                           [wr_b[i], crep_b], [psA_b[b]])
                    k.emit(dve, lambda: V.tensor_tensor(out=gbc[:, blk * 256:(blk + 1) * 256], in0=psA[b][:, 0:256],
                                                        in1=bgbc[:, blk * 256:(blk + 1) * 256], op=ALU.add),
                           reads=[psA_b[b], bgbc_b], writes=[gbc_b])
                jobs.append((ld, cp))
            stream(jobs)
            fold_toks = []
            nfold = 0
            for (wsrc_d, wdst, nrb, goff) in ((w_out, wout_s, 16, 0), (w_fd, wdn_s, 44, D)):
                for rb in range(nrb):
                    i = nfold % 2
                    nfold += 1
                    k.dma(sp, fst[i][:], wsrc_d[rb * 128:(rb + 1) * 128, :], fslot[i], writes=[fst_b[i]])
                    k.emit(dve, lambda: V.tensor_tensor(out=fbf[i][:], in0=fst[i][:], in1=gbc[:, goff:goff + D],
                                                        op=ALU.mult),
                           reads=[fst_b[i], gbc_b], writes=[fbf_b[i]])
                    fold_toks.append(k.dma(sp, wdst[rb * 128:(rb + 1) * 128, :], fbf[i][:], fsslot[i],
                                           reads=[fbf_b[i]]))
            fold_deps = tuple({t[0]: t for t in fold_toks}.values())
            p0_fence = k.fence()
        p0_guard = p0_fence + fold_deps

        if stop_after == "p0":
            dmp = nc.dram_tensor("dbg_modT", [128, 64], F32, kind="ExternalOutput").ap()
            dump_specs["dbg_modT"] = ((128, 64), F32)
            dslot_ = cs()
            tk = k.dma(sp, dmp, modT[:], dslot_, reads=[modT_b], deps=p0_guard)
            sp.e.wait_ge(dslot_.sem, dslot_.count)
            return nc, dump_specs

        def layer_norm_to_xn(s, src=None, src_b=None):
            xs = xbuf[:, s, :] if src is None else src
            src_b = xbuf_b[s] if src_b is None else src_b
            for c4 in range(4):
                k.emit(dve, lambda: V.bn_stats(out=stats[:, s, c4, :], in_=xs[:, c4 * 512:(c4 + 1) * 512]),
                       reads=[src_b], writes=[st_b[s]])
            k.emit(dve, lambda: V.bn_aggr(out=mv[:, s, :], in_=stats[:, s].rearrange("p c f -> p (c f)")),
                   reads=[st_b[s]], writes=[st_b[s]])
            rsqrt_small(rstd[:, s:s + 1], mv[:, s, 1:2], 0, [st_b[s]], st_b[s])
            k.emit(dve, lambda: V.scalar_tensor_tensor(out=nmr[:, s:s + 1], in0=mv[:, s, 0:1], scalar=-1.0,
                                                       in1=rstd[:, s:s + 1], op0=ALU.mult, op1=ALU.mult),
                   reads=[st_b[s]], writes=[st_b[s]])
            k.emit(dve, lambda: V.tensor_scalar(out=xn[:, s, :], in0=xs, scalar1=rstd[:, s:s + 1],
                                                scalar2=nmr[:, s:s + 1], op0=ALU.mult, op1=ALU.add),
                   reads=[src_b, st_b[s]], writes=[xn_b[s]])

        tr_i = [0]

        def transpose_modulate(shift_col, scale_col, hT=hT, hT_b=hT_b):
            for c in range(16):
                b = next_ps()
                dst = psA[b][:]
                for s in range(4):
                    k.emit(pe, lambda: T.matmul(psA[b][:, s * 128:(s + 1) * 128], xn[:, s, c * 128:(c + 1) * 128], ident,
                                                start=True, stop=True),
                           reads=[xn_b[s], cst_b], writes=[psA_b[b]])
                sc = modT[:, scale_col + c:scale_col + c + 1]
                sh = modT[:, shift_col + c:shift_col + c + 1]
                if c % 2 == 0:
                    k.emit(act, lambda: A.activation(out=hT[:, c, :], in_=dst, func=AF.Identity, bias=sh, scale=sc),
                           reads=[psA_b[b], modT_b], writes=[hT_b[c]])
                else:
                    k.emit(dve, lambda: V.tensor_scalar(out=hT[:, c, :], in0=dst, scalar1=sc, scalar2=sh,
                                                        op0=ALU.mult, op1=ALU.add),
                           reads=[psA_b[b], modT_b], writes=[hT_b[c]])

        deferred = []

        def flush_deferred():
            while deferred:
                deferred.pop(0)()

        def proj_T(wview, wbuf, ncol_lo, m, kchunks, rhs_fn, rhs_bufs):
            b = next_ps()
            for kc in range(kchunks):
                mm(psA[b][0:m, :], wview[:, kc, ncol_lo:ncol_lo + m], rhs_fn(kc), kc == 0, kc == kchunks - 1,
                   [wbuf] + rhs_bufs(kc), [psA_b[b]])
            flush_deferred()
            return b

        def rope_rms_finalize(b, gcol, tab, tab_b, col0, out_ap, out_b):
            ps = psA[b]
            o = next_tset()
            k.emit(act, lambda: A.activation(out=o.xg[:], in_=ps[:], func=AF.Identity, scale=gcol),
                   reads=[psA_b[b], small_b], writes=[o.xg_b])
            k.emit(act, lambda: A.activation(out=o.sq[:], in_=ps[:], func=AF.Square),
                   reads=[psA_b[b]], writes=[o.sq_b])
            def part2():
                b1 = next_ps()
                mm(psA[b1][:], ones, o.sq[:], True, True, [cst_b, o.sq_b], [psA_b[b1]])
                b2 = next_ps()
                mm(psA[b2][:], pg, o.xg[:], True, True, [cst_b, o.xg_b], [psA_b[b2]])
                rsqrt_small(o.rs[:], psA[b1][:], 1, [psA_b[b1]], o.rs_b)
                k.emit(dve, lambda: V.tensor_tensor(out=o.t1[:], in0=o.xg[:], in1=tab[:, 0, :], op=ALU.mult),
                       reads=[o.xg_b, tab_b], writes=[o.t1_b])
                k.emit(dve, lambda: V.tensor_tensor(out=o.t2[:], in0=psA[b2][:], in1=tab[:, 1, :], op=ALU.mult),
                       reads=[psA_b[b2], tab_b], writes=[o.t2_b])
                k.emit(dve, lambda: V.tensor_tensor(out=o.t1[:], in0=o.t1[:], in1=o.t2[:], op=ALU.add),
                       reads=[o.t1_b, o.t2_b], writes=[o.t1_b])
                k.emit(dve, lambda: V.scalar_tensor_tensor(out=out_ap, in0=o.t1[:], scalar=float(128.0 ** 0.5),
                                                           in1=o.rs[:], op0=ALU.mult, op1=ALU.mult),
                       reads=[o.t1_b, o.rs_b], writes=[out_b])
            deferred.append(part2)

        def rope64_finalize(b, scale_bc, scale_b, tab, tab_b, out_ap, out_b):
            ps = psA[b]
            o = next_tset()
            k.emit(act, lambda: A.activation(out=o.xg[0:64, :], in_=ps[0:64, :], func=AF.Identity),
                   reads=[psA_b[b]], writes=[o.xg_b])
            def part2():
                b2 = next_ps()
                mm(psA[b2][0:64, :], pmat[:], o.xg[0:64, :], True, True, [pm_b, o.xg_b], [psA_b[b2]])
                k.emit(dve, lambda: V.tensor_tensor(out=o.t1[0:64, :], in0=o.xg[0:64, :], in1=tab[:, 0, :], op=ALU.mult),
                       reads=[o.xg_b, tab_b], writes=[o.t1_b])
                k.emit(dve, lambda: V.tensor_tensor(out=o.t2[0:64, :], in0=psA[b2][0:64, :], in1=tab[:, 1, :], op=ALU.mult),
                       reads=[psA_b[b2], tab_b], writes=[o.t2_b])
                if scale_bc is None:
                    k.emit(dve, lambda: V.tensor_tensor(out=out_ap, in0=o.t1[0:64, :], in1=o.t2[0:64, :], op=ALU.add),
                           reads=[o.t1_b, o.t2_b], writes=[out_b])
                else:
                    k.emit(dve, lambda: V.tensor_tensor(out=o.t1[0:64, :], in0=o.t1[0:64, :], in1=o.t2[0:64, :], op=ALU.add),
                           reads=[o.t1_b, o.t2_b], writes=[o.t1_b])
                    k.emit(dve, lambda: V.tensor_tensor(out=out_ap, in0=o.t1[0:64, :], in1=scale_bc[0:64, :], op=ALU.mult),
                           reads=[o.t1_b, scale_b], writes=[out_b])
            deferred.append(part2)

        def latent_chunk(b, j, gains, lat, lat_b):
            o = next_tset()
            k.emit(act, lambda: A.activation(out=lat[:, j, :], in_=psA[b][:], func=AF.Identity,
                                             scale=gains[:, j:j + 1]),
                   reads=[psA_b[b], small_b], writes=[lat_b[j]])
            k.emit(act, lambda: A.activation(out=o.sq[:], in_=psA[b][:], func=AF.Square),
                   reads=[psA_b[b]], writes=[o.sq_b])
            deferred.append(lambda: mm(psA[5][:], ones, o.sq[:], j == 0, j == 3, [cst_b, o.sq_b], [psA_b[5]]))

        def latent_finish(lat, lat_b):
            flush_deferred()
            rsqrt_small(rq_t[:], psA[5][:], 2, [psA_b[5]], rq_b)
            for j in range(4):
                k.emit(dve, lambda: V.tensor_tensor(out=lat[:, j, :], in0=lat[:, j, :], in1=rq_t[:], op=ALU.mult),
                       reads=[lat_b[j], rq_b], writes=[lat_b[j]])

        def v16(t):
            return kview(t, 16, 256)

        job_specs = []
        for blk in range(4):
            job_specs.append([(v16, wsrc(w_in, blk * 256, 256))])
        for blk in range(2):
            job_specs.append([(v16, wsrc(w_in, 1536 + blk * 256, 256))])
        for blk in range(2):
            job_specs.append([(lambda t: kview(t, 4, 768), wsrc(w_uq, blk * 768, 768))])
        for p in range(8):
            job_specs.append([(v16, wsrc(w_in, 2624 + p * 256, 256))])
            job_specs.append([(v16, wsrc(w_in, 2624 + D + p * 256, 256))])
            job_specs.append([(lambda t: kview(t, 16, 256)[:, 0:8, :], wsrc(w_bg, p * 256, 256)),
                              (lambda t: kview(t, 16, 256)[:, 8:16, :], wsrc(w_bm, p * 256, 256))])
        for fc in range(44):
            job_specs.append([(lambda t: v16(t)[:, :, 0:128], wsrc(w_fg, fc * 128, 128)),
                              (lambda t: v16(t)[:, :, 128:256], wsrc(w_fu, fc * 128, 128))])
        assert len(job_specs) == NJOB
        JOB_B2, JOB_B5, JOB_B8 = 0, 8, 32
        pk_slot = [k.newslot(f"pk{i}") for i in range(NWS)]
        prep_state = {"next": 0, "pending": None, "toks": {}}

        def prep_jobs(n):
            for _ in range(n):
                j = prep_state["next"]
                if j < NJOB:
                    i = wload(job_specs[j])
                    prep_state["next"] = j + 1
                else:
                    i = None
                pend = prep_state["pending"]
                if pend is not None:
                    pj, pi = pend
                    prep_state["toks"][pi] = k.dma(pool, wscr[pj], wr[pi][:], pk_slot[pi], reads=[wr_b[pi]])
                prep_state["pending"] = (j, i) if i is not None else None

        def prep_flush():
            pend = prep_state["pending"]
            if pend is not None:
                pj, pi = pend
                prep_state["toks"][pi] = k.dma(pool, wscr[pj], wr[pi][:], pk_slot[pi], reads=[wr_b[pi]])
                prep_state["pending"] = None

        def wload_packed(j, n=WSLOT):
            i = wr_i[0] % NWS
            wr_i[0] += 1
            k.dma(pool, wr[i][:, 0:n], wscr[j, :, 0:n], wr_slot[i], writes=[wr_b[i]],
                  deps=tuple(prep_state["toks"].values()))
            return i

        pA = ExitStack()
        with pA:
            wkv = sb("wkv", [128, 16, 1088], BF16, pA)
            wkv_b = Buf("wkv")
            wuk = sb("wuk", [128, 4, 2048], BF16, pA)
            wuk_b = Buf("wuk")
            kvl = sb("kvl", [128, 4, TQ], BF16, pA)
            kvl_b = [Buf(f"kvl{j}") for j in range(4)]
            kgst = sb("kgst", [128, 2, TQ], BF16, pA)
            vgst = sb("vgst", [128, 4, 256], BF16, pA)
            knst = sb("knst", [128, 8, TQ], BF16, pA)
            krst = sb("krst", [64, TQ], BF16, pA)
            vmst = sb("vmst", [128, 4, 1024], BF16, pA)
            kgst_b, vgst_b, knst_b, krst_b, vmst_b = (Buf(n) for n in ("kgst", "vgst", "knst", "krst", "vmst"))
            for bb in kvl_b + [wkv_b, wuk_b, kgst_b, vgst_b, knst_b, krst_b, vmst_b]:
                bb.guard = p0_guard
            stslot = {n: k.newslot(n) for n in ("kg", "vg", "kn", "kr", "vm")}
            scr_toks = {}

            wv_in = w_in.rearrange("(k p) n -> p k n", p=128)
            k.dma(pool, wkv[:, :, 0:512], wv_in[:, :, 1024:1536], cs(), writes=[wkv_b])
            k.dma(pool, wkv[:, :, 512:1088], wv_in[:, :, 2048:2624], cs(), writes=[wkv_b])
            k.dma(pool, wuk[:], w_ukv.rearrange("(k p) n -> p k n", p=128), cs(), writes=[wuk_b])
            wuk_v = wuk[:].rearrange("p k (h t d) -> p k h t d", h=8, t=2)
            hT2 = sb("hT2", [128, 16, TQ], BF16, pA)
            hT2_b = [Buf(f"hT2_{c}") for c in range(16)]
            for bb in hT2_b:
                bb.guard = p0_guard
            hTs = ((hT, hT_b), (hT2, hT2_b))

            def pa_loads(t):
                rr = t * TQ
                for s in range(4):
                    k.dma(sp, xbuf[:, s, :], xa[rr + s * 128:rr + (s + 1) * 128, :], xslot[s], writes=[xbuf_b[s]])

            def pa_front(t):
                pa_loads(t)
                for s in range(4):
                    layer_norm_to_xn(s)

            def pa_tables(t):
                rr = t * TQ
                k.dma(sp, tabg[:, 0, :], cosg_d[:, rr:rr + TQ], tabg_slot, writes=[tabg_b])
                k.dma(sp, tabg[:, 1, :], sing_d[:, rr:rr + TQ], tabg_slot, writes=[tabg_b])
                k.dma(sp, tabm[:, 0, :], cosm_d[:, rr:rr + TQ], tabm_slot, writes=[tabm_b])
                k.dma(sp, tabm[:, 1, :], sinm_d[:, rr:rr + TQ], tabm_slot, writes=[tabm_b])

            pa_front(0)
            pa_tables(0)
            transpose_modulate(0, 16, *hTs[0])
            for t in range(nkt):
                r0 = t * TQ
                if prep_state["next"] < JOB_B8:
                    prep_jobs(min(3 if nkt == NKT else JOB_B8, JOB_B8 - prep_state["next"]))
                hTc, hTc_b = hTs[t % 2]
                hrhs = (lambda kc, hTc=hTc: hTc[:, kc, :])
                hbufs = (lambda kc, hTc_b=hTc_b: [hTc_b[kc]])
                nxt = t + 1 < nkt
                if nxt:
                    pa_loads(t + 1)
                for h in range(2):
                    b = proj_T(wkv, wkv_b, h * 128, 128, 16, hrhs, hbufs)
                    rope_rms_finalize(b, gk[:, 0:1], tabg, tabg_b, 0, kgst[:, h, :], kgst_b)
                flush_deferred()
                for h in range(2):
                    scr_toks["kg"] = k.dma(sp, kg_s[h, :, r0:r0 + TQ], kgst[:, h, :], stslot["kg"], reads=[kgst_b])
                if nxt:
                    layer_norm_to_xn(0)
                for s in range(4):
                    b = next_ps()
                    for kc in range(16):
                        mm(psA[b][:, 0:256], hTc[:, kc, s * 128:(s + 1) * 128], wkv[:, kc, 256:512], kc == 0, kc == 15,
                           [hTc_b[kc], wkv_b], [psA_b[b]])
                    k.emit(act, lambda: A.activation(out=vgst[:, s, :], in_=psA[b][:, 0:256], func=AF.Identity),
                           reads=[psA_b[b]], writes=[vgst_b])
                for h in range(2):
                    scr_toks["vg"] = k.dma(sp, vg_s[h, :, t * 4:(t + 1) * 4, :], vgst[:, :, h * 128:(h + 1) * 128],
                                           stslot["vg"], reads=[vgst_b])
                if nxt:
                    layer_norm_to_xn(1)
                for j in range(4):
                    b = proj_T(wkv, wkv_b, 512 + j * 128, 128, 16, hrhs, hbufs)
                    latent_chunk(b, j, mkg, kvl, kvl_b)
                latent_finish(kvl, kvl_b)
                if nxt:
                    layer_norm_to_xn(2)
                b = proj_T(wkv, wkv_b, 1024, 64, 16, hrhs, hbufs)
                rope64_finalize(b, None, None, tabm, tabm_b, krst[:], krst_b)
                flush_deferred()
                scr_toks["kr"] = k.dma(sp, kr_s[:, r0:r0 + TQ], krst[:], stslot["kr"], reads=[krst_b])
                flush_deferred()
                if nxt:
                    layer_norm_to_xn(3)
                for h in range(8):
                    b = next_ps()
                    for j in range(4):
                        mm(psA[b][:], wuk_v[:, j, h, 0, :], kvl[:, j, :], j == 0, j == 3, [wuk_b, kvl_b[j]], [psA_b[b]])
                    if h % 2 == 0:
                        k.emit(act, lambda: A.activation(out=knst[:, h, :], in_=psA[b][:], func=AF.Identity),
                               reads=[psA_b[b]], writes=[knst_b])
                    else:
                        k.emit(dve, lambda: V.tensor_copy(out=knst[:, h, :], in_=psA[b][:]),
                               reads=[psA_b[b]], writes=[knst_b])
                for h in range(8):
                    scr_toks["kn"] = k.dma(sp, kn_s[h, :, r0:r0 + TQ], knst[:, h, :], stslot["kn"], reads=[knst_b])
                for s in range(4):
                    for half in range(2):
                        b = next_ps()
                        for j in range(4):
                            mm(psA[b][:], kvl[:, j, s * 128:(s + 1) * 128], wuk_v[:, j, half * 4:(half + 1) * 4, 1, :],
                               j == 0, j == 3, [kvl_b[j], wuk_b], [psA_b[b]])
                        if half == 0:
                            k.emit(act, lambda: A.activation(out=vmst[:, s, 0:512], in_=psA[b][:], func=AF.Identity),
                                   reads=[psA_b[b]], writes=[vmst_b])
                        else:
                            k.emit(dve, lambda: V.tensor_copy(out=vmst[:, s, 512:1024], in_=psA[b][:]),
                                   reads=[psA_b[b]], writes=[vmst_b])
                for h in range(8):
                    scr_toks["vm"] = k.dma(sp, vm_s[h, :, t * 4:(t + 1) * 4, :], vmst[:, :, h * 128:(h + 1) * 128],
                                           stslot["vm"], reads=[vmst_b])
                if t + 1 < nkt:
                    pa_tables(t + 1)
                    transpose_modulate(0, 16, *hTs[(t + 1) % 2])
            while prep_state["next"] < JOB_B8:
                prep_jobs(1)
            prep_flush()
            pA_fence = k.fence()
        scr_deps = tuple(scr_toks.values())
        pA_guard = pA_fence + scr_deps + p0_guard

        if stop_after == "pA":
            for s_ in k.slots:
                if s_.count:
                    sp.e.wait_ge(s_.sem, s_.count)
            return nc, dump_specs

        pB = ExitStack()
        with pB:
            r1 = sb("r1", [128, 22528], BF16, pB)
            r2 = sb("r2", [128, 14336], BF16, pB)
            lnc = sb("lnc", [128, 2, D], F32, pB)
            rl_t = sb("rl_t", [128, TQ], F32, pB)
            qg = r1[:, 0:4096].rearrange("p (h n) -> p h n", h=8)
            qn = r1[:, 4096:8192].rearrange("p (h n) -> p h n", h=8)
            qr = r1[0:64, 8192:12288].rearrange("p (h n) -> p h n", h=8)
            qr128 = r1[:, 8192:12288].rearrange("p (h n) -> p h n", h=8)
            Pb = r1[:, 12288:14336].rearrange("p (h n) -> p h n", h=4)
            yT = r1[:, 14336:22528].rearrange("p (h n) -> p h n", h=16)
            aT = r1[:, 0:22528].rearrange("p (h n) -> p h n", h=44)
            Kr = r2[:, 0:4096].rearrange("p (s n) -> p s n", s=4)
            KRr = r2[0:64, 4096:8192].rearrange("p (s n) -> p s n", s=4)
            KRr128 = r2[:, 4096:8192].rearrange("p (s n) -> p s n", s=4)
            Vr = r2[:, 8192:12288].rearrange("p (s c d) -> p s c d", s=4, c=8)
            ql = r2[:, 12288:14336].rearrange("p (j n) -> p j n", j=4)
            mg = r2[:, 0:8192].rearrange("p (c n) -> p c n", c=16)
            qg_b = [Buf(f"qg{h}") for h in range(8)]
            qn_b = [Buf(f"qn{h}") for h in range(8)]
            qr_b = [Buf(f"qr{h}") for h in range(8)]
            P_b = [Buf(f"P{i}") for i in range(4)]
            y_b = [Buf(f"y{i}") for i in range(16)]
            aT_b = [Buf(f"aT{i}") for i in range(44)]
            kv_b = [Buf(f"kv{i}") for i in range(4)]
            ql_b = [Buf(f"ql{i}") for i in range(4)]
            mg_b = [Buf(f"mg{i}") for i in range(16)]
            lnc_b = Buf("lnc")
            rl_b = Buf("rl")
            kvslot = [k.newslot(f"kv{i}") for i in range(4)]
            lnslot = k.newslot("lnc")
            qrz_b, krz_b = Buf("qrz"), Buf("krz")
            r1_att = qg_b + qn_b + qr_b + P_b + y_b + [qrz_b]
            r2_att = kv_b + ql_b + [krz_b]
            for bb in r1_att + r2_att + aT_b + mg_b + [lnc_b, rl_b]:
                bb.guard = pA_guard
            kv_i = [0]
            hrhs = (lambda kc: hT[:, kc, :])
            hbufs = (lambda kc: [hT_b[kc]])
            xs_t = sb("xs_t", [128, D], F32, pB)
            xs_b = Buf("xs")
            xs_b.guard = pA_guard
            xs_slot = k.newslot("xs")

            def prefetch_tables(tn):
                rr = tn * TQ
                k.dma(sp, tabg[:, 0, :], cosgq_d[:, rr:rr + TQ], tabg_slot, writes=[tabg_b])
                k.dma(sp, tabg[:, 1, :], singq_d[:, rr:rr + TQ], tabg_slot, writes=[tabg_b])
                k.dma(sp, tabm[:, 0, :], cosmq_d[:, rr:rr + TQ], tabm_slot, writes=[tabm_b])
                k.dma(sp, tabm[:, 1, :], sinmq_d[:, rr:rr + TQ], tabm_slot, writes=[tabm_b])

            def prefetch_ln(tn, s):
                rr = tn * TQ + s * 128
                k.dma(sp, xs_t[:], xq[rr:rr + 128, :], xs_slot, writes=[xs_b])
                layer_norm_to_xn(s, xs_t[:], xs_b)

            def load_x_resid(tn):
                rr = tn * TQ
                for s in range(4):
                    k.dma(sp, xbuf[:, s, :], xq[rr + s * 128:rr + (s + 1) * 128, :], xslot[s], writes=[xbuf_b[s]])

            prefetch_tables(0)
            for s in range(4):
                prefetch_ln(0, s)
            transpose_modulate(0, 16)
            load_x_resid(0)
            SC_G = float(128.0 ** -0.5)
            SC_M = float(192.0 ** -0.5)

            for qt in range(nqt):
                r0 = qt * TQ
                has_next = qt + 1 < nqt

                k.emit(dve, lambda: V.memset(r1[64:128, 8192:12288], 0.0), writes=[qrz_b])
                k.emit(dve, lambda: V.memset(r2[64:128, 4096:8192], 0.0), writes=[krz_b])
                jobs = []
                for blk in range(4):
                    def ld(blk=blk):
                        return wload_packed(JOB_B2 + blk)

                    def cp(i, blk=blk):
                        wv = kview(wr[i], 16, 256)
                        for hh in range(2):
                            h = blk * 2 + hh
                            b = proj_T(wv, wr_b[i], hh * 128, 128, 16, hrhs, hbufs)
                            rope_rms_finalize(b, gq[:, 0:1], tabg, tabg_b, 0, qg[:, h, :], qg_b[h])
                    jobs.append((ld, cp))
                for blk in range(2):
                    def ld(blk=blk):
                        return wload_packed(JOB_B2 + 4 + blk)

                    def cp(i, blk=blk):
                        wv = kview(wr[i], 16, 256)
                        for jj in range(2):
                            j = blk * 2 + jj
                            b = proj_T(wv, wr_b[i], jj * 128, 128, 16, hrhs, hbufs)
                            latent_chunk(b, j, mqg, ql, ql_b)
                        if blk == 1:
                            latent_finish(ql, ql_b)
                    jobs.append((ld, cp))
                for blk in range(2):
                    def ld(blk=blk):
                        return wload_packed(JOB_B2 + 6 + blk, 3072)

                    def cp(i, blk=blk):
                        wv = kview(wr[i], 4, 768)
                        for hh in range(4):
                            h = blk * 4 + hh
                            b = proj_T(wv, wr_b[i], hh * 192, 128, 4, lambda j: ql[:, j, :], lambda j: [ql_b[j]])
                            if h % 2 == 0:
                                k.emit(act, lambda: A.activation(out=qn[:, h, :], in_=psA[b][:], func=AF.Identity),
                                       reads=[psA_b[b]], writes=[qn_b[h]])
                            else:
                                k.emit(dve, lambda: V.tensor_copy(out=qn[:, h, :], in_=psA[b][:]),
                                       reads=[psA_b[b]], writes=[qn_b[h]])
                            b = proj_T(wv, wr_b[i], hh * 192 + 128, 64, 4, lambda j: ql[:, j, :], lambda j: [ql_b[j]])
                            rope64_finalize(b, None, None, tabm, tabm_b, qr[:, h, :], qr_b[h])
                    jobs.append((ld, cp))
                stream(jobs)
                flush_deferred()

                if stop_after == "B2" and qt == 0:
                    dbg_dump("dbg_qg", r1[:, 0:4096], [128, 4096], BF16, qg_b)
                    dbg_dump("dbg_qn", r1[:, 4096:8192], [128, 4096], BF16, qn_b)
                    dbg_dump("dbg_qr", r1[0:64, 8192:12288], [64, 4096], BF16, qr_b)
                    break

                if qt == 0:
                    while prep_state["next"] < NJOB:
                        prep_jobs(1)
                    prep_flush()
                NST = 16 * 64
                NPC = NST // 8
                piece_slot = {}

                def kv_load(pi):
                    hi, p = divmod(pi, 8)
                    sl = kv_i[0] % 4
                    kv_i[0] += 1
                    piece_slot[pi] = sl
                    if hi < 8:
                        j = hi // 4
                        k.dma(sp, Kr[:, sl, :], kg_s[j, :, p * KP:(p + 1) * KP], kvslot[sl], writes=[kv_b[sl]])
                        k.dma(sp, Vr[:, sl], vg_s[j, :, p * 8:(p + 1) * 8, :], kvslot[sl], writes=[kv_b[sl]])
                    else:
                        h = hi - 8
                        k.dma(sp, Kr[:, sl, :], kn_s[h, :, p * KP:(p + 1) * KP], kvslot[sl], writes=[kv_b[sl]])
                        k.dma(sp, KRr[:, sl, :], kr_s[:, p * KP:(p + 1) * KP], kvslot[sl], writes=[kv_b[sl]])
                        k.dma(sp, Vr[:, sl], vm_s[h, :, p * 8:(p + 1) * 8, :], kvslot[sl], writes=[kv_b[sl]])

                S_BANKS = (0, 1, 6, 7)

                def S_step(n):
                    hi = n // 64
                    kk = n % 8
                    sl = piece_slot[n // 8]
                    b = S_BANKS[n % 4]
                    ksl = Kr[:, sl, kk * 128:(kk + 1) * 128]
                    if hi < 8:
                        mm(psA[b][:], ksl, qg[:, hi, :], True, True, [kv_b[sl], qg_b[hi]], [psA_b[b]])
                    else:
                        h = hi - 8
                        mm(psA[b][:], ksl, qn[:, h, :], True, False, [kv_b[sl], qn_b[h]], [psA_b[b]])
                        mm(psA[b][:], KRr128[:, sl, kk * 128:(kk + 1) * 128], qr128[:, h, :], False, True,
                           [kv_b[sl], qr_b[h], qrz_b, krz_b], [psA_b[b]])

                def E_step(n):
                    hi = n // 64
                    b = S_BANKS[n % 4]
                    sc = SC_G if hi < 8 else SC_M
                    k.emit(act, lambda: A.activation(out=Pb[:, n % 4, :], in_=psA[b][:], func=AF.Exp, scale=sc),
                           reads=[psA_b[b]], writes=[P_b[n % 4]])

                def PV_step(n):
                    hi, c = divmod(n, 64)
                    kk = n % 8
                    sl = piece_slot[n // 8]
                    ob, lb = 2 + hi % 2, 4 + hi % 2
                    mm(psA[ob][:], Vr[:, sl, kk, :], Pb[:, n % 4, :], c == 0, c == 63, [kv_b[sl], P_b[n % 4]], [psA_b[ob]])
                    mm(psA[lb][:], ones, Pb[:, n % 4, :], c == 0, c == 63, [cst_b, P_b[n % 4]], [psA_b[lb]])
                    if c == 63:
                        k.emit(dve, lambda: V.reciprocal(out=rl_t[:], in_=psA[lb][:]), reads=[psA_b[lb]], writes=[rl_b])
                        k.emit(dve, lambda: V.tensor_tensor(out=yT[:, hi, :], in0=psA[ob][:], in1=rl_t[:], op=ALU.mult),
                               reads=[psA_b[ob], rl_b], writes=[y_b[hi]])

                for pi in range(4):
                    kv_load(pi)
                for n in range(4):
                    S_step(n)
                for n in range(NST):
                    E_step(n)
                    PV_step(n)
                    if n + 4 < NST:
                        S_step(n + 4)
                    if (n + 1) % 8 == 0 and n // 8 + 4 < NPC:
                        kv_load(n // 8 + 4)

                if stop_after == "B4" and qt == 0:
                    dbg_dump("dbg_y", r1[:, 14336:22528], [128, 8192], BF16, y_b)
                    break

                g5 = k.fence()
                for bb in mg_b:
                    bb.guard = g5
                sets = tuple((o.t1, o.t2, o.t1_b, o.t2_b) for o in tsets)
                jobs = []
                for p in range(8):
                    def ldg(p=p):
                        return wload_packed(JOB_B5 + 3 * p)

                    def cpg(i, p=p):
                        wv = kview(wr[i], 16, 256)
                        for e in range(2):
                            oc = 2 * p + e
                            sg_t, _, sg_b, _ = sets[e]
                            b = proj_T(wv, wr_b[i], e * 128, 128, 16, hrhs, hbufs)
                            k.emit(act, lambda: A.activation(out=sg_t[:], in_=psA[b][:], func=AF.Sigmoid,
                                                             bias=bgT[:, oc:oc + 1]),
                                   reads=[psA_b[b], small_b], writes=[sg_b])

                    def ldm(p=p):
                        return wload_packed(JOB_B5 + 3 * p + 1)

                    def cpm(i, p=p):
                        wv = kview(wr[i], 16, 256)
                        for e in range(2):
                            oc = 2 * p + e
                            _, sm_t, _, sm_b = sets[e]
                            b = proj_T(wv, wr_b[i], e * 128, 128, 16, hrhs, hbufs)
                            k.emit(act, lambda: A.activation(out=sm_t[:], in_=psA[b][:], func=AF.Sigmoid,
                                                             bias=bgT[:, 16 + oc:17 + oc]),
                                   reads=[psA_b[b], small_b], writes=[sm_b])

                    def ldy(p=p):
                        return wload_packed(JOB_B5 + 3 * p + 2)

                    def cpy(i, p=p):
                        wv = kview(wr[i], 16, 256)
                        for e in range(2):
                            oc = 2 * p + e
                            sg_t, sm_t, sg_b, sm_b = sets[e]
                            b = next_ps()
                            for kc in range(8):
                                mm(psA[b][:], wv[:, kc, e * 128:(e + 1) * 128], yT[:, kc, :], kc == 0, kc == 7,
                                   [wr_b[i], y_b[kc]], [psA_b[b]])
                            k.emit(dve, lambda: V.tensor_tensor(out=sg_t[:], in0=psA[b][:], in1=sg_t[:], op=ALU.mult),
                                   reads=[psA_b[b], sg_b], writes=[sg_b])
                            b = next_ps()
                            for kc in range(8):
                                mm(psA[b][:], wv[:, 8 + kc, e * 128:(e + 1) * 128], yT[:, 8 + kc, :], kc == 0, kc == 7,
                                   [wr_b[i], y_b[8 + kc]], [psA_b[b]])
                            k.emit(dve, lambda: V.tensor_tensor(out=sm_t[:], in0=psA[b][:], in1=sm_t[:], op=ALU.mult),
                                   reads=[psA_b[b], sm_b], writes=[sm_b])
                            k.emit(dve, lambda: V.tensor_tensor(out=mg[:, oc, :], in0=sg_t[:], in1=sm_t[:], op=ALU.add),
                                   reads=[sg_b, sm_b], writes=[mg_b[oc]])
                    jobs.append((ldg, cpg))
                    jobs.append((ldm, cpm))
                    jobs.append((ldy, cpy))
                stream(jobs)

                k.dma(sp, lnc[:, 0, :], lnc_d[0], lnslot, writes=[lnc_b])
                k.dma(sp, lnc[:, 1, :], lnc_d[1], lnslot, writes=[lnc_b])
                jobs = []
                for blk in range(8):
                    def ld(blk=blk):
                        return wload_bf([(lambda t: kview(t, 16, 256),
                                          wout_s.rearrange("(k p) n -> p k n", p=128)[:, :, blk * 256:(blk + 1) * 256])],
                                        deps=fold_deps)

                    def cp(i, blk=blk):
                        wv = kview(wr[i], 16, 256)
                        for s in range(4):
                            b = next_ps()
                            for kc in range(16):
                                mm(psA[b][:, 0:256], mg[:, kc, s * 128:(s + 1) * 128], wv[:, kc, :], kc == 0, kc == 15,
                                   [mg_b[kc], wr_b[i]], [psA_b[b]])
                            xs = xbuf[:, s, blk * 256:(blk + 1) * 256]
                            k.emit(dve, lambda: V.scalar_tensor_tensor(out=xs, in0=xs, scalar=float(ALPHA),
                                                                       in1=psA[b][:, 0:256], op0=ALU.mult, op1=ALU.add),
                                   reads=[psA_b[b], xbuf_b[s]], writes=[xbuf_b[s]])
                    jobs.append((ld, cp))
                stream(jobs)
                g6 = k.fence()
                for bb in r2_att:
                    bb.guard = g6

                def ln_affine(s):
                    xs = xbuf[:, s, :]
                    for c4 in range(4):
                        k.emit(dve, lambda: V.bn_stats(out=stats[:, s, c4, :], in_=xs[:, c4 * 512:(c4 + 1) * 512]),
                               reads=[xbuf_b[s]], writes=[st_b[s]])
                    k.emit(dve, lambda: V.bn_aggr(out=mv[:, s, :], in_=stats[:, s].rearrange("p c f -> p (c f)")),
                           reads=[st_b[s]], writes=[st_b[s]])
                    rsqrt_small(rstd[:, s:s + 1], mv[:, s, 1:2], 0, [st_b[s]], st_b[s])
                    k.emit(dve, lambda: V.scalar_tensor_tensor(out=xs, in0=xs, scalar=mv[:, s, 0:1], in1=lnc[:, 0, :],
                                                               op0=ALU.subtract, op1=ALU.mult),
                           reads=[xbuf_b[s], st_b[s], lnc_b], writes=[xbuf_b[s]])
                    k.emit(dve, lambda: V.scalar_tensor_tensor(out=xs, in0=xs, scalar=rstd[:, s:s + 1], in1=lnc[:, 1, :],
                                                               op0=ALU.mult, op1=ALU.add),
                           reads=[xbuf_b[s], st_b[s], lnc_b], writes=[xbuf_b[s]])

                for s in range(4):
                    ln_affine(s)
                if stop_after == "B6" and qt == 0:
                    dbg_dump("dbg_mg", r2[:, 0:8192], [128, 8192], BF16, mg_b)
                    dbg_dump("dbg_x1", xbuf[:], [128, 4, D], F32, xbuf_b)
                    break
                for s in range(4):
                    layer_norm_to_xn(s)
                transpose_modulate(32, 48)
                k.dma(sp, lnc[:, 0, :], lnc_d[2], lnslot, writes=[lnc_b])
                k.dma(sp, lnc[:, 1, :], lnc_d[3], lnslot, writes=[lnc_b])

                g8 = k.fence()
                for bb in aT_b:
                    bb.guard = g8
                jobs = []
                for fc in range(44):
                    sg_t, _, sg_b, _ = sets[fc % 2]

                    def ld(fc=fc):
                        return wload_packed(JOB_B8 + fc)

                    def cp(i, fc=fc, sg_t=sg_t, sg_b=sg_b):
                        wv = kview(wr[i], 16, 256)
                        if has_next and fc == 1:
                            prefetch_tables(qt + 1)
                        if has_next and fc in (2, 10, 18, 26):
                            prefetch_ln(qt + 1, (fc - 2) // 8)
                        b = proj_T(wv, wr_b[i], 0, 128, 16, hrhs, hbufs)
                        k.emit(act, lambda: A.activation(out=sg_t[:], in_=psA[b][:], func=AF.Silu),
                               reads=[psA_b[b]], writes=[sg_b])
                        b = proj_T(wv, wr_b[i], 128, 128, 16, hrhs, hbufs)
                        k.emit(dve, lambda: V.tensor_tensor(out=aT[:, fc, :], in0=psA[b][:], in1=sg_t[:], op=ALU.mult),
                               reads=[psA_b[b], sg_b], writes=[aT_b[fc]])
                    jobs.append((ld, cp))
                stream(jobs)
                if has_next:
                    transpose_modulate(0, 16)
                wdn_v = wdn_s.rearrange("(k p) n -> p k n", p=128)
                jobs = []
                for nb in range(4):
                    for g in range(6):
                        nf = 8 if g < 5 else 4

                        def ld(nb=nb, g=g, nf=nf):
                            return wload_bf([(lambda t: kview(t, 8, 512)[:, 0:nf, :],
                                              wdn_v[:, g * 8:g * 8 + nf, nb * 512:(nb + 1) * 512])], deps=fold_deps)

                        def cp(i, nb=nb, g=g, nf=nf):
                            wv = kview(wr[i], 8, 512)
                            for fi in range(nf):
                                fc = g * 8 + fi
                                for s in range(4):
                                    mm(psA[s][:], aT[:, fc, s * 128:(s + 1) * 128], wv[:, fi, :], fc == 0, fc == 43,
                                       [aT_b[fc], wr_b[i]], [psA_b[s]])
                            if g == 5:
                                for s in range(4):
                                    xs = xbuf[:, s, nb * 512:(nb + 1) * 512]
                                    k.emit(dve, lambda: V.scalar_tensor_tensor(out=xs, in0=xs, scalar=float(ALPHA),
                                                                               in1=psA[s][:], op0=ALU.mult, op1=ALU.add),
                                           reads=[psA_b[s], xbuf_b[s]], writes=[xbuf_b[s]])
                        jobs.append((ld, cp))
                stream(jobs)
                k.ps_i = 4
                g9 = k.fence()
                for bb in r1_att:
                    bb.guard = g9
                for s in range(4):
                    ln_affine(s)
                    k.dma(sp, y[r0 + s * 128:r0 + (s + 1) * 128, :], xbuf[:, s, :], yslot[s], reads=[xbuf_b[s]])
                if has_next:
                    load_x_resid(qt + 1)

        for s_ in k.slots:
            if s_.count:
                sp.e.wait_ge(s_.sem, s_.count)
    return nc, dump_specs


def _perm(n):
    h = n // 2
    p = np.empty(n, np.int64)
    p[:h] = 2 * np.arange(h)
    p[h:] = 2 * np.arange(h) + 1
    return p


def _rope_tables(dim):
    quarter = dim // 4
    inv = 10000.0 ** (-np.arange(quarter, dtype=np.float64) / quarter)
    t = np.arange(S)
    ang = np.concatenate([(t // 64)[:, None] * inv[None, :], (t % 64)[:, None] * inv[None, :]], axis=1)
    c, s = np.cos(ang).T, np.sin(ang).T
    cosT = np.concatenate([c, c], axis=0)
    sinT = np.concatenate([-s, s], axis=0)
    return np.ascontiguousarray(cosT, np.float32), np.ascontiguousarray(sinT, np.float32)


def _swap(n):
    m = np.zeros((n, n), np.float32)
    idx = np.arange(n)
    m[(idx + n // 2) % n, idx] = 1.0
    return m


def make_in_maps(x, c, w_ada, b_ada, w_in, b_gates, gqa_q_gain, gqa_k_gain, mla_q_gain, mla_kv_gain,
                 w_mla_uq, w_mla_ukv, w_branch_gqa, w_branch_mla, w_out, ln1_g, ln1_b,
                 w_ffn_gate, w_ffn_up, w_ffn_down, ln2_g, ln2_b):
    f = lambda a: np.ascontiguousarray(np.asarray(a, dtype=np.float32))
    p128, p64 = _perm(128), _perm(64)
    cols = np.arange(6720)
    for h in range(8):
        cols[h * 128:(h + 1) * 128] = h * 128 + p128
    for h in range(2):
        cols[1024 + h * 128:1024 + (h + 1) * 128] = 1024 + h * 128 + p128
    cols[2560:2624] = 2560 + p64
    w_in_p = f(np.asarray(w_in)[0][:, cols])
    ucols = np.arange(1536)
    for h in range(8):
        ucols[h * 192 + 128:(h + 1) * 192] = h * 192 + 128 + p64
    w_uq_p = f(np.asarray(w_mla_uq)[0][:, ucols])
    ba = np.asarray(b_ada, np.float32)[0]
    badaT = f(np.concatenate([ba[0:D], ba[D:2 * D], ba[3 * D:4 * D], ba[4 * D:5 * D]]).reshape(64, 128).T)
    bgbc = f(np.broadcast_to(np.concatenate([ba[2 * D:3 * D], ba[5 * D:6 * D]])[None, :], (128, 2 * D)))
    lnc = f(np.stack([np.broadcast_to(np.asarray(v, np.float32)[0][None, :], (128, D))
                      for v in (ln1_g, ln1_b, ln2_g, ln2_b)]))
    cosg, sing = _rope_tables(128)
    cosm, sinm = _rope_tables(64)
    cst = f(np.stack([np.eye(128, dtype=np.float32), np.ones((128, 128), np.float32), _swap(128)], axis=1))
    common = dict(
        w_ada=f(np.asarray(w_ada)[0]), badaT=badaT, bgbc=bgbc, w_in=w_in_p,
        bgatesT=f(np.asarray(b_gates, np.float32)[0].reshape(32, 128).T),
        gq=f(np.asarray(gqa_q_gain, np.float32)[0][p128][:, None]),
        gk=f(np.asarray(gqa_k_gain, np.float32)[0][p128][:, None]),
        mqg=f(np.asarray(mla_q_gain, np.float32)[0].reshape(4, 128).T),
        mkg=f(np.asarray(mla_kv_gain, np.float32)[0].reshape(4, 128).T),
        w_uq=w_uq_p, w_ukv=f(np.asarray(w_mla_ukv)[0]), w_bg=f(np.asarray(w_branch_gqa)[0]),
        w_bm=f(np.asarray(w_branch_mla)[0]), w_out=f(np.asarray(w_out)[0]), lnc=lnc,
        w_fg=f(np.asarray(w_ffn_gate)[0]), w_fu=f(np.asarray(w_ffn_up)[0]), w_fd=f(np.asarray(w_ffn_down)[0]),
        cst=cst, pmat=_swap(64),
    )
    x = np.asarray(x, np.float32)
    c = np.asarray(c, np.float32)
    maps = []
    for core in range(8):
        b, half = divmod(core, 2)
        order = np.concatenate([np.arange(half * SQ, (half + 1) * SQ), np.arange((1 - half) * SQ, (2 - half) * SQ)])
        m = dict(common)
        m["xa"] = f(x[b][order])
        m["cT"] = f(c[b].reshape(16, 128).T)
        m["cosg"] = f(cosg[:, order])
        m["sing"] = f(sing[:, order])
        m["cosm"] = f(cosm[:, order])
        m["sinm"] = f(sinm[:, order])
        maps.append(m)
    return maps


_NC_CACHE = {}


def kernel(**inputs):
    if "nc" not in _NC_CACHE:
        _NC_CACHE["nc"] = build_program()[0]
    nc = _NC_CACHE["nc"]
    in_maps = make_in_maps(**inputs)
    res = run_bass_kernel_spmd(nc, in_maps, core_ids=list(range(8)))
    out = np.empty((4, S, D), np.float32)
    for core in range(8):
        b, half = divmod(core, 2)
        out[b, half * SQ:(half + 1) * SQ] = res.results[core]["y"]
    return out
```
